# Optimizing a Trainium2 kernel written in Bass

```python
import jax
import jax.numpy as jnp
from jax import lax
import numpy as np

D_MODEL = 2048
BATCH = 4
SEQ = 8192
DEPTH = 2

N_MIXERS = 2
N_RET_LAYERS = (DEPTH + 1) // 2
N_MLA_LAYERS = DEPTH // 2
EPS = 1e-6
ROPE_BASE = 10000.0
BLOCK = 128

RET_HEADS = 8
RET_DK = D_MODEL // RET_HEADS
RET_DV = 2 * RET_DK
RET_WIDTH = RET_HEADS * RET_DV
RET_QK = RET_HEADS * RET_DK
RET_IN = 2 * RET_QK + 2 * RET_WIDTH

MLA_HEADS = 16
MLA_NOPE = 128
MLA_ROPE = 64
MLA_V = 128
MLA_Q_RANK = 512
MLA_KV_RANK = 512
MLA_WIDTH = MLA_HEADS * MLA_V
MLA_IN = MLA_Q_RANK + MLA_KV_RANK + MLA_ROPE + MLA_WIDTH

kernel_name = "hybrid_retention_mla_adaln"


def rmsnorm(x, g):
    xf = x.astype(jnp.float32)
    y = xf * lax.rsqrt(jnp.mean(xf * xf, axis=-1, keepdims=True) + EPS)
    return (y * g.astype(jnp.float32)).astype(x.dtype)


def rope(x, pos):
    d = x.shape[-1]
    inv_freq = ROPE_BASE ** (-jnp.arange(0, d, 2, dtype=jnp.float32) / d)
    ang = pos.astype(jnp.float32)[:, :, None, None] * inv_freq
    cos = jnp.cos(ang).astype(x.dtype)
    sin = jnp.sin(ang).astype(x.dtype)
    x1, x2 = x[..., : d // 2], x[..., d // 2:]
    return jnp.concatenate([x1 * cos - x2 * sin, x2 * cos + x1 * sin], axis=-1)


def chunkwise_retention(q, k, v):
    B, S, H, _ = q.shape
    nc = S // BLOCK
    log_gamma = jnp.log1p(-jnp.exp2(-5.0 - jnp.arange(H, dtype=jnp.float32)))
    idx = jnp.arange(BLOCK, dtype=jnp.float32)
    diff = idx[:, None] - idx[None, :]
    d_intra = jnp.where(diff >= 0, jnp.exp(log_gamma[:, None, None] * jnp.maximum(diff, 0.0)), 0.0)
    xi = jnp.exp(log_gamma[:, None] * (idx + 1.0))[None, :, :, None]
    zeta = jnp.exp(log_gamma[:, None] * (BLOCK - 1.0 - idx))[None, :, :, None]
    chunk_decay = jnp.exp(log_gamma * BLOCK)[None, :, None, None]

    def to_chunks(t):
        return t.astype(jnp.float32).reshape(B, nc, BLOCK, H, t.shape[-1]).transpose(1, 0, 3, 2, 4)

    def step(state, inp):
        qi, ki, vi = inp
        scores = jnp.einsum('bhnd,bhmd->bhnm', qi, ki) * d_intra
        inner = jnp.einsum('bhnm,bhmv->bhnv', scores, vi)
        cross = jnp.einsum('bhnd,bhdv->bhnv', qi, state) * xi
        state = state * chunk_decay + jnp.einsum('bhmd,bhmv->bhdv', ki, vi * zeta)
        return state, inner + cross

    state0 = jnp.zeros((B, H, q.shape[-1], v.shape[-1]), jnp.float32)
    _, out = lax.scan(step, state0, (to_chunks(q), to_chunks(k), to_chunks(v)))
    return out.transpose(1, 0, 3, 2, 4).reshape(B, S, H, v.shape[-1])


def retention_branch(h, pos, w_in, gn_g, w_out):
    B, S, _ = h.shape
    proj = h @ w_in
    q, k, v, g = jnp.split(proj, [RET_QK, 2 * RET_QK, 2 * RET_QK + RET_WIDTH], axis=-1)
    q = rope(q.reshape(B, S, RET_HEADS, RET_DK), pos)
    k = rope(k.reshape(B, S, RET_HEADS, RET_DK), pos) * (RET_DK ** -0.5)
    v = v.reshape(B, S, RET_HEADS, RET_DV)
    y = chunkwise_retention(q, k, v)
    mu = jnp.mean(y, axis=-1, keepdims=True)
    var = jnp.mean(jnp.square(y - mu), axis=-1, keepdims=True)
    y = (y - mu) * lax.rsqrt(var + EPS)
    y = (y.reshape(B, S, RET_WIDTH) * gn_g.astype(jnp.float32)).astype(h.dtype)
    return (jax.nn.silu(g) * y) @ w_out


def causal_block_attention(q, k, v):
    B, S, H, dq = q.shape
    nb = S // BLOCK
    scale = dq ** -0.5
    qb = q.reshape(B, nb, BLOCK, H, dq).transpose(1, 0, 3, 2, 4)
    kt = k.transpose(0, 2, 1, 3)
    vt = v.transpose(0, 2, 1, 3)
    key_pos = jnp.arange(S)

    def one_block(args):
        i, qi = args
        s = jnp.einsum('bhqd,bhkd->bhqk', qi, kt).astype(jnp.float32) * scale
        qpos = i * BLOCK + jnp.arange(BLOCK)
        s = jnp.where(key_pos[None, :] <= qpos[:, None], s, -1e30)
        p = jax.nn.softmax(s, axis=-1).astype(vt.dtype)
        return jnp.einsum('bhqk,bhkd->bhqd', p, vt)

    o = lax.map(one_block, (jnp.arange(nb), qb))
    return o.transpose(1, 0, 3, 2, 4).reshape(B, S, H, v.shape[-1])


def mla_branch(h, pos, w_in, q_norm_g, w_uq, kv_norm_g, w_ukv, w_out):
    B, S, _ = h.shape
    proj = h @ w_in
    cq, ckv, k_rope, g = jnp.split(
        proj, [MLA_Q_RANK, MLA_Q_RANK + MLA_KV_RANK, MLA_Q_RANK + MLA_KV_RANK + MLA_ROPE], axis=-1)
    q = (rmsnorm(cq, q_norm_g) @ w_uq).reshape(B, S, MLA_HEADS, MLA_NOPE + MLA_ROPE)
    q = jnp.concatenate([q[..., :MLA_NOPE], rope(q[..., MLA_NOPE:], pos)], axis=-1)
    kv = (rmsnorm(ckv, kv_norm_g) @ w_ukv).reshape(B, S, MLA_HEADS, MLA_NOPE + MLA_V)
    k_nope, v = kv[..., :MLA_NOPE], kv[..., MLA_NOPE:]
    k_rope = rope(k_rope[:, :, None, :], pos)
    k = jnp.concatenate([k_nope, jnp.broadcast_to(k_rope, (B, S, MLA_HEADS, MLA_ROPE))], axis=-1)
    o = causal_block_attention(q, k, v).reshape(B, S, MLA_WIDTH)
    return (jax.nn.silu(g) * o) @ w_out


def setup_inputs(seed: int = 0) -> dict:
    key = jax.random.key(seed)
    ks = jax.random.split(key, 18)

    def w(k, shape, fan_in, gain=1.0):
        return jax.random.normal(k, shape, jnp.float32) * (gain * fan_in ** -0.5)

    def gain(k, shape):
        return 1.0 + 0.05 * jax.random.normal(k, shape, jnp.float32)

    x = jax.random.normal(ks[0], (BATCH, SEQ, D_MODEL), jnp.float32)
    c = jax.random.normal(ks[1], (BATCH, D_MODEL), jnp.float32)
    start = jax.random.randint(ks[2], (BATCH, 1), 0, 1024, dtype=jnp.int32)
    positions = start + jnp.arange(SEQ, dtype=jnp.int32)[None, :]
    return {
        "x": x,
        "c": c,
        "positions": positions,
        "ada_w": w(ks[3], (DEPTH, D_MODEL, 3 * D_MODEL), D_MODEL, 0.5),
        "ada_b": 0.01 * jax.random.normal(ks[4], (DEPTH, 3 * D_MODEL), jnp.float32),
        "norm_g": gain(ks[5], (DEPTH, D_MODEL)),
        "ret_w_in": w(ks[6], (N_RET_LAYERS, D_MODEL, RET_IN), D_MODEL),
        "ret_gn_g": gain(ks[7], (N_RET_LAYERS, RET_WIDTH)),
        "ret_w_out": w(ks[8], (N_RET_LAYERS, RET_WIDTH, D_MODEL), RET_WIDTH),
        "mla_w_in": w(ks[9], (N_MLA_LAYERS, D_MODEL, MLA_IN), D_MODEL),
        "mla_q_norm_g": gain(ks[10], (N_MLA_LAYERS, MLA_Q_RANK)),
        "mla_w_uq": w(ks[11], (N_MLA_LAYERS, MLA_Q_RANK, MLA_HEADS * (MLA_NOPE + MLA_ROPE)), MLA_Q_RANK),
        "mla_kv_norm_g": gain(ks[12], (N_MLA_LAYERS, MLA_KV_RANK)),
        "mla_w_ukv": w(ks[13], (N_MLA_LAYERS, MLA_KV_RANK, MLA_HEADS * (MLA_NOPE + MLA_V)), MLA_KV_RANK),
        "mla_w_out": w(ks[14], (N_MLA_LAYERS, MLA_WIDTH, D_MODEL), MLA_WIDTH),
        "final_norm_g": gain(ks[15], (D_MODEL,)),
    }


def reference(x, c, positions, ada_w, ada_b, norm_g, ret_w_in, ret_gn_g, ret_w_out,
              mla_w_in, mla_q_norm_g, mla_w_uq, mla_kv_norm_g, mla_w_ukv, mla_w_out,
              final_norm_g):
    c_act = jax.nn.silu(c)
    for i in range(DEPTH):
        mod = (c_act @ ada_w[i] + ada_b[i])[:, None, :]
        shift, scale, gate = jnp.split(mod, 3, axis=-1)
        h = rmsnorm(x, norm_g[i]) * (1.0 + scale) + shift
        j = i // N_MIXERS
        if i % N_MIXERS == 0:
            y = retention_branch(h, positions, ret_w_in[j], ret_gn_g[j], ret_w_out[j])
        else:
            y = mla_branch(h, positions, mla_w_in[j], mla_q_norm_g[j], mla_w_uq[j],
                           mla_kv_norm_g[j], mla_w_ukv[j], mla_w_out[j])
        x = x + gate * y
    return rmsnorm(x, final_norm_g)
```

```python
import numpy as np
import ml_dtypes
from contextlib import ExitStack
import concourse.bass as bass
import concourse.mybir as mybir
from concourse.bass_utils import run_bass_kernel_spmd

F32 = mybir.dt.float32
BF16 = mybir.dt.bfloat16
I32 = mybir.dt.int32
ALU = mybir.AluOpType
AF = mybir.ActivationFunctionType
AX = mybir.AxisListType

D = 2048
EPS = 1e-6
NCORES = 8
PAIRS = [[0, 1], [2, 3], [4, 5], [6, 7]]
NCORES_RUN = [8]
TWO_PI = 2.0 * np.pi
CW1 = 6.28125
CW2 = float(TWO_PI - 6.28125)


class Ctx:
    EPOCH = 30000

    def __init__(self, nc, es):
        self.nc = nc
        self.es = es
        self.eng = {"pe": nc.tensor, "act": nc.scalar, "dve": nc.vector, "pool": nc.gpsimd, "sp": nc.sync}
        self.psem = {}
        self.pcount = {}
        self.pname = {}
        self.pgen = {}
        self.last_tok = {}
        for e in ["pe", "act", "dve", "pool"]:
            self.pgen[e] = 0
            self._new_psem(e)
        self.waited = {}
        self.dsems = {}
        self.lw = {}
        self.rd = {}
        self.stores = []

    def _new_psem(self, e):
        name = f"p_{e}{self.pgen[e]}"
        self.pgen[e] += 1
        self.psem[e] = self.es.enter_context(self.nc.semaphore(name))
        self.pcount[e] = 0
        self.pname[e] = name

    def sig(self, e, inst):
        if self.pcount[e] >= self.EPOCH:
            self._new_psem(e)
        self.pcount[e] += 1
        inst.then_inc(self.psem[e], 1)
        tok = (self.psem[e], self.pcount[e], self.pname[e])
        self.last_tok[e] = tok
        return tok

    def wait(self, e, *toks):
        flat = []

        def rec(ts):
            for t in ts:
                if t is None:
                    continue
                if isinstance(t, list):
                    rec(t)
                else:
                    flat.append(t)
        rec(toks)
        best = {}
        for t in flat:
            if t[2] not in best or best[t[2]][1] < t[1]:
                best[t[2]] = t
        for sem, val, name in best.values():
            key = (e, name)
            if e == "pe" and name.startswith("p_pe"):
                continue
            if self.waited.get(key, 0) >= val:
                continue
            self.waited[key] = val
            self.eng[e].wait_ge(sem, val)

    def _deps(self, e, R, W):
        toks = [self.lw.get(r) for r in R]
        for w in W:
            toks.append(self.lw.get(w))
            toks.append(self.rd.get(w, []))
        self.wait(e, toks)

    def _commit(self, tok, R, W):
        for r in R:
            self.rd.setdefault(r, []).append(tok)
            if len(self.rd[r]) > 6:
                best = {}
                for t in self.rd[r]:
                    if t[2] not in best or best[t[2]][1] < t[1]:
                        best[t[2]] = t
                self.rd[r] = list(best.values())
        for w in W:
            self.lw[w] = tok
            self.rd[w] = []

    def barrier(self, exclude=()):
        toks = [self.last_tok[e] for e in ["pe", "act", "dve", "pool"] if e in self.last_tok]
        toks += [(d[0], d[1], d[2]) for d in self.dsems.values() if d[1] > 0 and d[2] not in exclude]
        for e in ["pe", "act", "dve", "pool", "sp"]:
            self.wait(e, *toks)

    def do(self, e, R, W, fn):
        self._deps(e, R, W)
        tok = self.sig(e, fn())
        self._commit(tok, R, W)
        return tok

    def group(self, e, R, W, fns):
        self._deps(e, R, W)
        inst = None
        for fn in fns:
            inst = fn()
        tok = self.sig(e, inst)
        self._commit(tok, R, W)
        return tok

    def dma_batch(self, q, items, semname):
        toks = [self.dma(q, R, W, out, in_, semname, fence=(i == 0)) for i, (R, W, out, in_) in enumerate(items)]
        for (R, W, out, in_) in items:
            for w in W:
                self.lw[w] = toks[-1]
        return toks[-1]

    def dma(self, q, R, W, out, in_, semname, fence=True):
        self._deps(q, R, W)
        if semname not in self.dsems:
            self.dsems[semname] = [self.es.enter_context(self.nc.semaphore("d_" + semname)), 0, "d_" + semname]
        ds = self.dsems[semname]
        if fence and ds[1] > 0:
            self.wait(q, (ds[0], ds[1], ds[2]))
        inst = self.eng[q].dma_start(out=out, in_=in_)
        ds[1] += 16
        inst.then_inc(ds[0], 16)
        tok = (ds[0], ds[1], ds[2])
        self._commit(tok, R, W)
        return tok


def build_program(S, dbg=False, upto=99):
    NT = S // 128
    import os
    GT = min(NT, int(os.environ.get("KGT", "4")))
    NG = NT // GT
    GTOK = GT * 128
    nc = bass.Bass("TRN2", target_bir_lowering=False)
    es = ExitStack()
    k = Ctx(nc, es)
    pe, act, dve, pool, sp = nc.tensor, nc.scalar, nc.vector, nc.gpsimd, nc.sync

    def din(name, shape, dt=F32):
        return nc.dram_tensor(name, list(shape), dt, kind="ExternalInput").ap()

    def dscr(name, shape, dt, out=False, internal=False):
        kind = "ExternalOutput" if ((out or dbg) and not internal) else "Internal"
        return nc.dram_tensor(name, list(shape), dt, kind=kind).ap()

    x_d = din("x", [S, D])
    c_d = din("c_l", [128, 16])
    pos_d = din("pos_l", [128, NT], I32)
    ada_w_d = din("ada_w", [2, D, 3072])
    ada_b_d = din("ada_b", [2, 3 * D])
    norm_g_d = din("norm_g", [2, D])
    rwin_d = din("ret_w_in", [D, 6144])
    rgn_d = din("ret_gn_g", [2048])
    rwout_d = din("ret_w_out", [2048, D])
    mwin_d = din("mla_w_in", [D, 2112])
    mqn_d = din("mla_q_norm_g", [512])
    muqn_d = din("mla_w_uq_nope", [512, 1024])
    muqr_d = din("mla_w_uq_rope", [512, 512])
    mkvn_d = din("mla_kv_norm_g", [512])
    mukvk_d = din("mla_w_ukv_k", [512, 1024])
    mukvv_d = din("mla_w_ukv_v", [512, 1024])
    mwout_d = din("mla_w_out", [1024, D])
    fng_d = din("final_norm_g", [D])
    ident_d = din("ident", [128, 128], BF16)
    invf0_d = din("invf0", [128])
    invf1_d = din("invf1", [32])
    dq_d = din("dq", [128, 4])
    dk_d = din("dk", [128, 4])
    g128_d = din("g128", [128, 4])
    cmask_d = din("cmask", [128, 128], BF16)
    amask_d = din("amask", [128, 4, 512], BF16)

    out_d = nc.dram_tensor("out", [S // 2, D], F32, kind="ExternalOutput").ap()

    rwin_b = dscr("rwin_b", [D, 6144], BF16)
    rwout_b = dscr("rwout_b", [2048, D], BF16)
    mwin_b = dscr("mwin_b", [D, 2112], BF16)
    muqn_b = dscr("muqn_b", [512, 1024], BF16)
    muqr_b = dscr("muqr_b", [512, 512], BF16)
    mukvk_b = dscr("mukvk_b", [512, 1024], BF16)
    mukvv_b = dscr("mukvv_b", [512, 1024], BF16)
    mwout_b = dscr("mwout_b", [1024, D], BF16)
    modloc_d = dscr("modloc", [1, 6144], F32, internal=True)
    modall_d = dscr("modall", [2, 6144], F32, internal=True)
    qs_d = dscr("qs", [S, 1024], BF16)
    ks_d = dscr("ks", [S, 1024], BF16)
    vs_d = dscr("vs", [S, 2048], BF16)
    gs_d = dscr("gs", [S, 2048], BF16)
    zs_d = dscr("zs", [S, 2048], BF16)
    p0_d = dscr("p0_i", [S, D], F32, internal=True)
    s0_d = dscr("s0_i", [S, D], F32, internal=True)
    x1_d = dscr("x1", [S, D], F32)
    qtn_d = dscr("qtn", [8, 128, S], BF16)
    qtr_d = dscr("qtr", [8, 64, S], BF16)
    ktn_d = dscr("ktn", [8, 128, S], BF16)
    ktr_d = dscr("ktr", [64, S], BF16)
    v1_d = dscr("v1", [8, 128, S // 128, 128], BF16)
    gt_d = dscr("gt", [1024, S], BF16)
    zt_d = dscr("zt", [1024, S], BF16)
    p1_d = dscr("p1_i", [S, D], F32, internal=True)
    s1_d = dscr("s1_i", [S // 2, D], F32, internal=True)

    uid = [0]
    def sb(name, shape, dt, stack=es):
        uid[0] += 1
        return stack.enter_context(nc.sbuf_tensor(f"s{uid[0]}_" + name, list(shape), dt))

    def ps(name, shape, dt, stack):
        uid[0] += 1
        return stack.enter_context(nc.psum_tensor(f"ps{uid[0]}_" + name, list(shape), dt))

    ident = sb("ident", [128, 128], BF16)
    shift = [sb(f"shift{l}", [128, D], F32) for l in range(2)]
    g1 = [sb(f"g1_{l}", [128, D], F32) for l in range(2)]
    gate = [sb(f"gate{l}", [128, D], F32) for l in range(2)]
    posf = sb("posf", [128, NT], F32)
    invf0 = sb("invf0", [128, 128], F32)
    invf1 = sb("invf1", [128, 32], F32)
    dq = sb("dq", [128, 4], F32)
    dk = sb("dk", [128, 4], F32)

    ccsem = {}

    def collective(src_ap, dst_ap, name, kind, toks, op=ALU.add):
        if name not in ccsem:
            ccsem[name] = [es.enter_context(nc.semaphore("cc_" + name)), 0]
            k.dsems["cc_" + name] = [ccsem[name][0], 0, "cc_" + name]
        k.wait("pool", toks)
        inst = pool.collective_compute(kind, op, replica_groups=PAIRS[:max(1, NCORES_RUN[0] // 2)], ins=[src_ap], outs=[dst_ap])
        inst.then_inc(ccsem[name][0])
        ccsem[name][1] += 1
        k.dsems["cc_" + name][1] = ccsem[name][1]
        return (ccsem[name][0], ccsem[name][1], "cc_" + name)

    for nm, src, dst in [("rwin_b", rwin_d, rwin_b), ("rwout_b", rwout_d, rwout_b), ("mwin_b", mwin_d, mwin_b),
                         ("muqn_b", muqn_d, muqn_b), ("muqr_b", muqr_d, muqr_b), ("mukvk_b", mukvk_d, mukvk_b),
                         ("mukvv_b", mukvv_d, mukvv_b), ("mwout_b", mwout_d, mwout_b)]:
        rows = src.shape[0]
        for i in range(rows // 128):
            tok = k.dma("pool", [], [(nm, i)], dst[i * 128:(i + 1) * 128, :], src[i * 128:(i + 1) * 128, :], "cast_" + nm, fence=False)
        k.lw[nm] = tok

    with ExitStack() as p0:
        cl = sb("cl", [128, 16], F32, p0)
        cact = sb("cact", [128, 16], F32, p0)
        cbc = sb("cbc", [128, 16, 128], F32, p0)
        posi = sb("posi", [128, NT], I32, p0)
        gbc = sb("gbc", [128, D], F32, p0)
        bbc = sb("bbc", [128, 3 * D], F32, p0)
        modrow = sb("modrow", [1, 6144], F32, p0)
        wbufs = [sb(f"adaw{i}", [128, 3072], F32, p0) for i in range(3)]
        mps = [ps(f"mps{j}", [128, 512], F32, p0) for j in range(6)]

        k.dma_batch("sp", [([], ["cl"], cl[:], c_d), ([], ["posi"], posi[:], pos_d), ([], ["ident"], ident[:], ident_d),
                           ([], ["invf0"], invf0[:], invf0_d.partition_broadcast(128)), ([], ["invf1"], invf1[:], invf1_d.partition_broadcast(128)),
                           ([], ["dq"], dq[:], dq_d), ([], ["dk"], dk[:], dk_d)], "cb0")
        k.do("act", ["cl"], ["cact"], lambda: act.activation(out=cact[:], in_=cl[:], func=AF.Silu))
        k.do("dve", ["posi"], ["posf"], lambda: dve.tensor_copy(out=posf[:], in_=posi[:]))
        k.do("dve", ["cact"], ["cbc"], lambda: dve.tensor_copy(out=cbc[:], in_=cact[:].unsqueeze(2).broadcast_to([128, 16, 128])))

        wi = 0
        for l in range(2):
            for kc in range(16):
                j = wi % 3
                wi += 1
                wb = wbufs[j]
                k.dma("sp", [], [("adaw", j)], wb[:], ada_w_d[l, kc * 128:(kc + 1) * 128, :], f"adaw{j}")
                k.group("pe", ["cbc", ("adaw", j)], ["mps"],
                        [(lambda jj=jj, kc=kc, wb=wb: pe.matmul(mps[jj][:], lhsT=cbc[:, kc, :], rhs=wb[:, jj * 512:(jj + 1) * 512],
                                                         start=(kc == 0), stop=(kc == 15))) for jj in range(6)])
            for jj in range(6):
                k.do("dve", ["mps"], [("modrow", l, jj)], lambda: dve.tensor_copy(out=modrow[0:1, l * 3072 + jj * 512:l * 3072 + (jj + 1) * 512], in_=mps[jj][0:1, :]))
        tok = k.dma("sp", [("modrow", l, jj) for l in range(2) for jj in range(6)], [], modloc_d, modrow[0:1, :], "cb1")
        tok = collective(modloc_d, modall_d, "ag", "AllGather", [tok], op=ALU.bypass)
        k.wait("sp", tok)
        for l in range(2):
            o = l * 3072
            k.dma_batch("sp", [([], [f"shift{l}"], shift[l][:], modall_d[0, o:o + 2048].partition_broadcast(128)),
                               ([], [(f"g1_{l}", 0)], g1[l][:, 0:1024], modall_d[0, o + 2048:o + 3072].partition_broadcast(128)),
                               ([], [(f"g1_{l}", 1)], g1[l][:, 1024:2048], modall_d[1, o:o + 1024].partition_broadcast(128)),
                               ([], [f"gate{l}"], gate[l][:], modall_d[1, o + 1024:o + 3072].partition_broadcast(128)),
                               ([], ["gbc"], gbc[:], norm_g_d[l].partition_broadcast(128)),
                               ([], ["bbc"], bbc[:], ada_b_d[l].partition_broadcast(128))], "cb1")
            k.do("dve", [f"shift{l}", "bbc"], [f"shift{l}"], lambda: dve.tensor_tensor(out=shift[l][:], in0=shift[l][:], in1=bbc[:, 0:D], op=ALU.add))
            k.do("dve", [(f"g1_{l}", 0), (f"g1_{l}", 1), "bbc"], [f"g1_{l}"], lambda: dve.tensor_tensor(out=g1[l][:], in0=g1[l][:], in1=bbc[:, D:2 * D], op=ALU.add))
            k.do("dve", [f"g1_{l}", "gbc"], [f"g1_{l}"], lambda: dve.scalar_tensor_tensor(out=g1[l][:], in0=g1[l][:], scalar=1.0, in1=gbc[:], op0=ALU.add, op1=ALU.mult))
            k.do("dve", [f"gate{l}", "bbc"], [f"gate{l}"], lambda: dve.tensor_tensor(out=gate[l][:], in0=gate[l][:], in1=bbc[:, 2 * D:3 * D], op=ALU.add))

    k.barrier()

    def rope_tables(tmps, t, invf, ikey, nf, cos_out, sin_out, okey):
        ang, kf, r, m, ki = [x[:, :nf] for x in tmps]
        k.do("dve", [ikey, "posf"], ["rt_ang"], lambda: dve.tensor_scalar(out=ang, in0=invf[:, :nf], scalar1=posf[:, t:t + 1], scalar2=None, op0=ALU.mult))
        k.do("dve", ["rt_ang"], ["rt_ki"], lambda: dve.tensor_scalar(out=ki, in0=ang, scalar1=float(1.0 / TWO_PI), scalar2=None, op0=ALU.mult))
        k.do("dve", ["rt_ki"], ["rt_kf"], lambda: dve.tensor_copy(out=kf, in_=ki))
        k.do("dve", ["rt_kf", "rt_ang"], ["rt_r"], lambda: dve.scalar_tensor_tensor(out=r, in0=kf, scalar=-CW1, in1=ang, op0=ALU.mult, op1=ALU.add))
        k.do("dve", ["rt_kf", "rt_r"], ["rt_r"], lambda: dve.scalar_tensor_tensor(out=r, in0=kf, scalar=-CW2, in1=r, op0=ALU.mult, op1=ALU.add))

        def fold(dst, dkey, src, skey, add):
            if add != 0.0:
                k.do("dve", [skey], [dkey], lambda: dve.tensor_scalar(out=dst, in0=src, scalar1=float(add), scalar2=None, op0=ALU.add))
                src, skey = dst, dkey
            k.do("dve", [skey], ["rt_m"], lambda: dve.tensor_scalar(out=m, in0=src, scalar1=float(np.pi), scalar2=float(-TWO_PI), op0=ALU.is_gt, op1=ALU.mult))
            k.do("dve", [skey, "rt_m"], [dkey], lambda: dve.tensor_tensor(out=dst, in0=src, in1=m, op=ALU.add))
            k.do("dve", [dkey], ["rt_m"], lambda: dve.tensor_scalar(out=m, in0=dst, scalar1=float(-np.pi), scalar2=float(TWO_PI), op0=ALU.is_lt, op1=ALU.mult))
            k.do("dve", [dkey, "rt_m"], [dkey], lambda: dve.tensor_tensor(out=dst, in0=dst, in1=m, op=ALU.add))
            k.do("dve", [dkey], [dkey], lambda: dve.tensor_scalar(out=dst, in0=dst, scalar1=float(-np.pi), scalar2=float(np.pi), op0=ALU.max, op1=ALU.min))

        fold(r, "rt_r", r, "rt_r", 0.0)
        fold(ang, "rt_ang", r, "rt_r", float(np.pi / 2))
        k.do("act", ["rt_r"], [(okey, "sin")], lambda: act.activation(out=sin_out, in_=r, func=AF.Sin))
        k.do("act", ["rt_ang"], [(okey, "cos")], lambda: act.activation(out=cos_out, in_=ang, func=AF.Sin))

    def prologue_a(l, xt, xkey, junk, ssq, rstd, hb):
        k.do("act", [xkey], ["junk", "ssq"], lambda: act.activation(out=junk[:], in_=xt[:], func=AF.Square, accum_out=ssq[:]))
        k.do("act", ["ssq"], ["rstd"], lambda: act.activation(out=rstd[:], in_=ssq[:], func=AF.Sqrt, scale=float(1.0 / D), bias=float(EPS)))
        k.do("dve", ["rstd"], ["rstd"], lambda: dve.reciprocal(out=rstd[:], in_=rstd[:]))
        k.do("dve", [xkey, "rstd", f"g1_{l}"], [xkey], lambda: dve.scalar_tensor_tensor(out=xt[:], in0=xt[:], scalar=rstd[:, 0:1], in1=g1[l][:], op0=ALU.mult, op1=ALU.mult))
        k.do("dve", [xkey, f"shift{l}"], ["hb"], lambda: dve.tensor_tensor(out=hb[:], in0=xt[:], in1=shift[l][:], op=ALU.add))

    def prologue_b(hb, pT, hT, tl, hk="hT"):
        k.group("pe", ["hb", "ident"], ["pT"], [(lambda kc=kc: pe.transpose(pT[:, kc * 128:(kc + 1) * 128], hb[:, kc * 128:(kc + 1) * 128], ident[:])) for kc in range(16)])
        for hf in range(2):
            k.do("act", ["pT"], [(hk, tl)], lambda hf=hf: act.copy(out=hT[:, hf * 8:(hf + 1) * 8, tl * 128:(tl + 1) * 128],
                                                                in_=pT[:, hf * 1024:(hf + 1) * 1024].rearrange("p (a b) -> p a b", a=8)))

    def prologue_tile(l, xt, xkey, junk, ssq, rstd, hb, pT, hT, tl, hk="hT"):
        prologue_a(l, xt, xkey, junk, ssq, rstd, hb)
        prologue_b(hb, pT, hT, tl, hk)

    if upto >= 1:
        with ExitStack() as pa:
            xts = [sb(f"xt{i}", [128, D], F32, pa) for i in range(2)]
            junk = sb("junk", [128, D], BF16, pa)
            hb = sb("hb", [128, D], BF16, pa)
            ssq = sb("ssq", [128, 1], F32, pa)
            rstd = sb("rstd", [128, 1], F32, pa)
            hTs = [sb(f"hT{i}", [128, 16, GTOK], BF16, pa) for i in range(2)]
            wblk = [sb(f"wblk{i}", [128, 16, 512], BF16, pa) for i in range(2)]
            stg = [sb(f"stg{i}", [128, GT, 512], BF16, pa) for i in range(2)]
            cosbs = [sb(f"cosb{i}", [128, GT, 128], F32, pa) for i in range(2)]
            sinbs = [sb(f"sinb{i}", [128, GT, 128], F32, pa) for i in range(2)]
            rtmp = [sb(f"rt{i}", [128, 128], F32, pa) for i in range(4)] + [sb("rti", [128, 128], I32, pa)]
            tt = [sb(f"tt{i}", [128, 128], F32, pa) for i in range(4)]
            gnb = sb("gnb", [128, 2048], F32, pa)
            pT = ps("pT", [128, D], BF16, pa)
            mmps = [ps(f"mm{i}", [128, 512], F32, pa) for i in range(4)]

            k.dma("sp", [], ["gnb"], gnb[:], rgn_d.partition_broadcast(128), "cb0")
            xi = wi = si = mi = 0

            def pro_a(g, tl):
                nonlocal xi
                t = g * GT + tl
                j = xi % 2
                xi += 1
                k.dma("sp", ["x"], [("xt", j)], xts[j][:], x_d[t * 128:(t + 1) * 128, :], f"xt{j}")
                prologue_a(0, xts[j], ("xt", j), junk, ssq, rstd, hb)
                rope_tables(rtmp, t, invf0, "invf0", 128, cosbs[g % 2][:, tl, :], sinbs[g % 2][:, tl, :], ("tab%d" % (g % 2), tl))

            def pro_b(g, tl):
                prologue_b(hb, pT, hTs[g % 2], tl, "hT%d" % (g % 2))

            def prologue_group(g):
                for tl in range(GT):
                    pro_a(g, tl)
                    pro_b(g, tl)

            def load_w(idx):
                cb_ = idx % 12
                j_ = idx % 2
                k.dma("sp", ["rwin_b"], [("wblk", j_)], wblk[j_][:], rwin_b[:, cb_ * 512:(cb_ + 1) * 512].rearrange("(kc p) c -> p kc c", p=128), f"wblk{j_}")

            prologue_group(0)
            load_w(0)
            for g in range(NG):
                hT = hTs[g % 2]
                hk = "hT%d" % (g % 2)
                cosb, sinb = cosbs[g % 2], sinbs[g % 2]
                tabk = "tab%d" % (g % 2)
                for cb in range(12):
                    j = wi % 2
                    wi += 1
                    wb = wblk[j]
                    if wi < NG * 12:
                        load_w(wi)
                    if g + 1 < NG and GT == 4:
                        if 5 <= cb <= 8:
                            pro_b(g + 1, cb - 5)
                        if 4 <= cb <= 7:
                            pro_a(g + 1, cb - 4)
                    elif g + 1 < NG and cb == 6:
                        prologue_group(g + 1)
                    sj = si % 2
                    si += 1
                    st = stg[sj]
                    for tl in range(GT):
                        mj = mi % 4
                        mi += 1
                        mp = mmps[mj]
                        k.group("pe", [("wblk", j), (hk, tl)], [("mm", mj)],
                                [(lambda kc=kc, mp=mp, tl=tl, wb=wb: pe.matmul(mp[:], lhsT=hT[:, kc, tl * 128:(tl + 1) * 128], rhs=wb[:, kc, :],
                                                                          start=(kc == 0), stop=(kc == 15))) for kc in range(16)])
                        skey = ("stg", sj, tl)
                        if cb < 4:
                            dtab, dkey = (dq, "dq") if cb < 2 else (dk, "dk")
                            for hh in range(2):
                                h = (cb % 2) * 2 + hh
                                x1 = mp[:, hh * 256:hh * 256 + 128]
                                x2 = mp[:, hh * 256 + 128:hh * 256 + 256]
                                dsc = dtab[:, h:h + 1]
                                cs, sn = cosb[:, tl, :], sinb[:, tl, :]
                                RK = [("mm", mj), dkey, ((tabk, tl), "cos"), ((tabk, tl), "sin")]
                                k.do("dve", RK, ["tt0"], lambda: dve.scalar_tensor_tensor(out=tt[0][:], in0=x1, scalar=dsc, in1=cs, op0=ALU.mult, op1=ALU.mult))
                                k.do("dve", RK, ["tt1"], lambda: dve.scalar_tensor_tensor(out=tt[1][:], in0=x2, scalar=dsc, in1=sn, op0=ALU.mult, op1=ALU.mult))
                                k.do("dve", RK, ["tt2"], lambda: dve.scalar_tensor_tensor(out=tt[2][:], in0=x2, scalar=dsc, in1=cs, op0=ALU.mult, op1=ALU.mult))
                                k.do("dve", RK, ["tt3"], lambda: dve.scalar_tensor_tensor(out=tt[3][:], in0=x1, scalar=dsc, in1=sn, op0=ALU.mult, op1=ALU.mult))
                                k.do("pool", ["tt0", "tt1"], [skey + (hh, 0)], lambda: pool.tensor_tensor(out=st[:, tl, hh * 256:hh * 256 + 128], in0=tt[0][:], in1=tt[1][:], op=ALU.subtract))
                                k.do("pool", ["tt2", "tt3"], [skey + (hh, 1)], lambda: pool.tensor_tensor(out=st[:, tl, hh * 256 + 128:hh * 256 + 256], in0=tt[2][:], in1=tt[3][:], op=ALU.add))
                        elif cb < 8:
                            sk4 = [skey + (hh, q) for hh in range(2) for q in range(2)]
                            k.do("act", [("mm", mj)], sk4, lambda: act.copy(out=st[:, tl, :], in_=mp[:]))
                        else:
                            sk4 = [skey + (hh, q) for hh in range(2) for q in range(2)]
                            k.do("act", [("mm", mj)], sk4, lambda: act.activation(out=st[:, tl, :], in_=mp[:], func=AF.Silu))
                            k.do("pool", sk4 + ["gnb"], sk4, lambda: pool.tensor_tensor(out=st[:, tl, :], in0=st[:, tl, :], in1=gnb[:, (cb - 8) * 512:(cb - 7) * 512], op=ALU.mult))
                    if cb < 2:
                        dst, dname = qs_d[g * GTOK:(g + 1) * GTOK, cb * 512:(cb + 1) * 512], "qs"
                    elif cb < 4:
                        dst, dname = ks_d[g * GTOK:(g + 1) * GTOK, (cb - 2) * 512:(cb - 1) * 512], "ks"
                    elif cb < 8:
                        dst, dname = vs_d[g * GTOK:(g + 1) * GTOK, (cb - 4) * 512:(cb - 3) * 512], "vs"
                    else:
                        dst, dname = gs_d[g * GTOK:(g + 1) * GTOK, (cb - 8) * 512:(cb - 7) * 512], "gs"
                    rkeys = [("stg", sj, tl, hh, q) for tl in range(GT) for hh in range(2) for q in range(2)]
                    tok = k.dma("sp", rkeys, [], dst.rearrange("(t p) c -> p t c", p=128), st[:], f"stg{sj}")
                    k.stores.append(tok)

            k.barrier()

    if upto >= 2:
        with ExitStack() as pb:
            qin = [sb(f"qin{i}", [128, 1024], BF16, pb) for i in range(2)]
            kin = [sb(f"kin{i}", [128, 1024], BF16, pb) for i in range(2)]
            vin = [sb(f"vin{i}", [128, 2048], BF16, pb) for i in range(2)]
            gin = [sb(f"gin{i}", [128, 2048], BF16, pb) for i in range(2)]
            qTs = [sb(f"qTs{i}", [128, 8, 128], BF16, pb) for i in range(2)]
            kTs = [sb(f"kTs{i}", [128, 8, 128], BF16, pb) for i in range(2)]
            sTm = [sb(f"sTm{i}", [128, 4, 128], BF16, pb) for i in range(2)]
            cmask = sb("cmask", [128, 128], BF16, pb)
            g128 = sb("g128", [128, 4], F32, pb)
            state = sb("state", [128, 8, 512], F32, pb)
            sbf = [sb(f"sbf{i}", [128, 8, 512], BF16, pb) for i in range(2)]
            sgt = [sb(f"sgt{i}", [128, 512], F32, pb) for i in range(2)]
            st6 = sb("st6", [128, 6], F32, pb)
            mv = sb("mv", [128, 2], F32, pb)
            rs = sb("rs", [128, 1], F32, pb)
            yn = [sb(f"yn{i}", [128, 512], F32, pb) for i in range(2)]
            zst = [sb(f"zst{i}", [128, 2048], BF16, pb) for i in range(2)]
            pq = ps("pq", [128, 1024], BF16, pb)
            pk = ps("pk", [128, 1024], BF16, pb)
            psT = ps("psT", [128, 512], F32, pb)
            po = [ps(f"po{i}", [128, 512], F32, pb) for i in range(2)]
            pU = [ps(f"pU{i}", [128, 512], F32, pb) for i in range(3)]

            k.dma_batch("sp", [([], ["cmask"], cmask[:], cmask_d), ([], ["g128"], g128[:], g128_d)], "cb0")
            k.do("dve", [], [("state", i) for i in range(8)], lambda: dve.memset(state[:], 0.0))
            k.do("pool", [], [("sbf", 0, i) for i in range(8)], lambda: pool.memset(sbf[0][:], 0.0))

            def loads(c):
                j = c % 2
                r0 = c * 128
                k.dma("sp", [], [("qin", j)], qin[j][:], qs_d[r0:r0 + 128, :], f"qin{j}")
                k.dma("sp", [], [("kin", j)], kin[j][:], ks_d[r0:r0 + 128, :], f"kin{j}")
                k.dma("sp", [], [("vin", j)], vin[j][:], vs_d[r0:r0 + 128, :], f"vin{j}")
                k.dma("sp", [], [("gin", j)], gin[j][:], gs_d[r0:r0 + 128, :], f"gin{j}")

            def stage1(c):
                j = c % 2
                k.group("pe", [("qin", j), "ident"], ["pq"], [(lambda i=i: pe.transpose(pq[:, i * 128:(i + 1) * 128], qin[j][:, i * 128:(i + 1) * 128], ident[:])) for i in range(8)])
                k.group("pe", [("kin", j), "ident"], ["pk"], [(lambda i=i: pe.transpose(pk[:, i * 128:(i + 1) * 128], kin[j][:, i * 128:(i + 1) * 128], ident[:])) for i in range(8)])
                k.do("act", ["pq"], [("qTs", j)], lambda: act.copy(out=qTs[j][:], in_=pq[:].rearrange("p (a b) -> p a b", a=8)))
                k.do("act", ["pk"], [("kTs", j)], lambda: act.copy(out=kTs[j][:], in_=pk[:].rearrange("p (a b) -> p a b", a=8)))
                fns = []
                for h in range(4):
                    for dc in range(2):
                        fns.append(lambda h=h, dc=dc: pe.matmul(psT[:, h * 128:(h + 1) * 128], lhsT=kTs[j][:, h * 2 + dc, :], rhs=qTs[j][:, h * 2 + dc, :],
                                                               start=(dc == 0), stop=(dc == 1)))
                k.group("pe", [("qTs", j), ("kTs", j)], ["psT"], fns)
                k.do("dve", ["psT", "cmask"], [("sTm", j)], lambda: dve.tensor_tensor(out=sTm[j][:], in0=psT[:].rearrange("p (a b) -> p a b", a=4),
                                                                                  in1=cmask[:].unsqueeze(1).broadcast_to([128, 4, 128]), op=ALU.mult))

            cnt = {"u": 0, "o": 0}

            def stage2(c):
                j = c % 2
                ver = c % 2
                for h in range(4):
                    vsl = vin[j][:, h * 512:(h + 1) * 512]
                    ujs = []
                    for dc in range(2):
                        uj = cnt["u"] % 3
                        cnt["u"] += 1
                        ujs.append(uj)
                        k.do("pe", [("kin", j), ("vin", j)], [("pU", uj)],
                             lambda uj=uj, dc=dc: pe.matmul(pU[uj][:], lhsT=kin[j][:, h * 256 + dc * 128:h * 256 + dc * 128 + 128], rhs=vsl, start=True, stop=True))
                    oj = cnt["o"] % 2
                    cnt["o"] += 1
                    k.group("pe", [("sTm", j), ("vin", j), ("qTs", j), ("sbf", ver, h * 2), ("sbf", ver, h * 2 + 1)], [("po", oj)], [
                        lambda: pe.matmul(po[oj][:], lhsT=sTm[j][:, h, :], rhs=vsl, start=True, stop=False),
                        lambda: pe.matmul(po[oj][:], lhsT=qTs[j][:, h * 2, :], rhs=sbf[ver][:, h * 2, :], start=False, stop=False),
                        lambda: pe.matmul(po[oj][:], lhsT=qTs[j][:, h * 2 + 1, :], rhs=sbf[ver][:, h * 2 + 1, :], start=False, stop=True)])
                    k.do("dve", [("po", oj)], ["st6"], lambda: dve.bn_stats(out=st6[:], in_=po[oj][:]))
                    k.do("dve", ["st6"], ["mv"], lambda: dve.bn_aggr(out=mv[:], in_=st6[:]))
                    k.do("act", ["mv"], ["rs"], lambda: act.activation(out=rs[:], in_=mv[:, 1:2], func=AF.Sqrt, bias=float(EPS)))
                    k.do("dve", ["rs"], ["rs"], lambda: dve.reciprocal(out=rs[:], in_=rs[:]))
                    yj = oj
                    k.do("dve", [("po", oj), "mv", "rs"], [("yn", yj)], lambda: dve.tensor_scalar(out=yn[yj][:], in0=po[oj][:], scalar1=mv[:, 0:1], scalar2=rs[:, 0:1],
                                                                                              op0=ALU.subtract, op1=ALU.mult))
                    k.do("pool", [("yn", yj), ("gin", j)], [("zst", j, h)], lambda: pool.tensor_tensor(out=zst[j][:, h * 512:(h + 1) * 512], in0=yn[yj][:],
                                                                                                 in1=gin[j][:, h * 512:(h + 1) * 512], op=ALU.mult))
                    for dc in range(2):
                        i = h * 2 + dc
                        uj = ujs[dc]
                        k.do("act", [("state", i), "g128"], [("sgt", dc)], lambda: act.activation(out=sgt[dc][:], in_=state[:, i, :], func=AF.Copy, scale=g128[:, h:h + 1]))
                        k.do("dve", [("pU", uj), ("sgt", dc), "g128"], [("state", i)],
                             lambda: dve.scalar_tensor_tensor(out=state[:, i, :], in0=pU[uj][:], scalar=g128[:, h:h + 1], in1=sgt[dc][:], op0=ALU.mult, op1=ALU.add))
                        k.do("act", [("state", i)], [("sbf", 1 - ver, i)], lambda: act.copy(out=sbf[1 - ver][:, i, :], in_=state[:, i, :]))
                tok = k.dma("sp", [("zst", j, h) for h in range(4)], [], zs_d[c * 128:(c + 1) * 128, :], zst[j][:], f"zst{j}")
                k.stores.append(tok)

            loads(0)
            if NT > 1:
                loads(1)
            stage1(0)
            for c in range(NT):
                if c + 1 < NT:
                    stage1(c + 1)
                stage2(c)
                if c + 2 < NT:
                    loads(c + 2)
            k.barrier()

    ar0_toks = []
    rs1_toks = []
    if upto >= 3:
        with ExitStack() as pc:
            wout = sb("wout", [128, 16, D], BF16, pc)
            zin = [sb(f"zin{i}", [128, 2048], BF16, pc) for i in range(2)]
            zT = [sb(f"zT{i}", [128, 16, 128], BF16, pc) for i in range(2)]
            ost = [sb(f"ost{i}", [128, D], F32, pc) for i in range(2)]
            xc = [sb(f"xc{i}", [128, D], F32, pc) for i in range(2)]
            pT = ps("pT3", [128, D], BF16, pc)
            pout = [ps(f"pout{i}", [128, 512], F32, pc) for i in range(4)]
            for q4 in range(4):
                k.dma("sp", ["rwout_b"], [("wout", q4)], wout[:, q4 * 4:(q4 + 1) * 4, :],
                      rwout_b[q4 * 512:(q4 + 1) * 512, :].rearrange("(kc p) c -> p kc c", p=128), "cb1", fence=(q4 == 0))
            k.lw[("wout", 0)] = k.lw[("wout", 1)] = k.lw[("wout", 2)] = k.lw[("wout", 3)]
            ctoks = []

            def c0_loads(t):
                j_ = t % 2
                k.dma("sp", [], [("zin", j_)], zin[j_][:], zs_d[t * 128:(t + 1) * 128, :], f"zin{j_}")
                k.dma("sp", ["x"], [("xc", j_)], xc[j_][:], x_d[t * 128:(t + 1) * 128, :], f"xc{j_}")

            c0_loads(0)
            for t in range(NT):
                j = t % 2
                if t + 1 < NT:
                    c0_loads(t + 1)
                k.group("pe", [("zin", j), "ident"], ["pT3"], [(lambda kc=kc: pe.transpose(pT[:, kc * 128:(kc + 1) * 128], zin[j][:, kc * 128:(kc + 1) * 128], ident[:])) for kc in range(16)])
                for hf in range(2):
                    k.do("act", ["pT3"], [("zT", j, hf)], lambda: act.copy(out=zT[j][:, hf * 8:(hf + 1) * 8, :], in_=pT[:, hf * 1024:(hf + 1) * 1024].rearrange("p (a b) -> p a b", a=8)))
                for cb in range(4):
                    k.group("pe", [("zT", j, 0), ("zT", j, 1)] + [("wout", q4) for q4 in range(4)], [("pout", cb)],
                            [(lambda kc=kc: pe.matmul(pout[cb][:], lhsT=zT[j][:, kc, :], rhs=wout[:, kc, cb * 512:(cb + 1) * 512], start=(kc == 0), stop=(kc == 15))) for kc in range(16)])
                    csl = slice(cb * 512, (cb + 1) * 512)
                    k.do("dve", [("pout", cb), "gate0"], [("ost", j, cb)], lambda: dve.tensor_tensor(out=ost[j][:, csl], in0=pout[cb][:], in1=gate[0][:, csl], op=ALU.mult))
                    k.do("dve", [("ost", j, cb), ("xc", j)], [("ost", j, cb)], lambda: dve.scalar_tensor_tensor(out=ost[j][:, csl], in0=xc[j][:, csl], scalar=0.5, in1=ost[j][:, csl], op0=ALU.mult, op1=ALU.add))
                tok = k.dma("sp", [("ost", j, cb) for cb in range(4)], [], p0_d[t * 128:(t + 1) * 128, :], ost[j][:], f"ost{j}")
                k.stores.append(tok)
                ctoks.append(tok)
                if t % 4 == 3 and upto >= 4:
                    r0 = (t - 3) * 128
                    ar0_toks.append(collective(p0_d[r0:r0 + 512, :], s0_d[r0:r0 + 512, :], "ar0", "AllReduce", ctoks))
                    ctoks = []
            k.barrier(exclude=("cc_ar0",))

    QSC = float(192.0 ** -0.5)
    NQG = S // 512
    if upto >= 5:
        with ExitStack() as pa:
            xts = [sb(f"xt{i}", [128, D], F32, pa) for i in range(2)]
            junk = sb("junk", [128, D], BF16, pa)
            hb = sb("hb", [128, D], BF16, pa)
            ssq = sb("ssq", [128, 1], F32, pa)
            rstd = sb("rstd", [128, 1], F32, pa)
            hT = sb("hT", [128, 16, 512], BF16, pa)
            wblk = [sb(f"wblk{i}", [128, 16, 512], BF16, pa) for i in range(2)]
            wkr = sb("wkr", [128, 16, 64], BF16, pa)
            uqn = sb("uqn", [128, 4, 1024], BF16, pa)
            uqr = sb("uqr", [128, 4, 512], BF16, pa)
            ukvk = sb("ukvk", [128, 4, 1024], BF16, pa)
            ukvv = sb("ukvv", [128, 4, 1024], BF16, pa)
            qng = sb("qng", [128, 512], F32, pa)
            kvng = sb("kvng", [128, 512], F32, pa)
            cn = [sb(f"cn{i}", [128, 512], BF16, pa) for i in range(2)]
            cnT = [sb("cqnT", [128, 4, 512], BF16, pa), sb("ckvnT", [128, 4, 512], BF16, pa)]
            cos1 = sb("cos1", [128, 4, 32], F32, pa)
            sin1 = sb("sin1", [128, 4, 32], F32, pa)
            cosq = sb("cosq", [128, 4, 32], F32, pa)
            sinq = sb("sinq", [128, 4, 32], F32, pa)
            rtmp = [sb(f"rt{i}", [128, 32], F32, pa) for i in range(4)] + [sb("rti", [128, 32], I32, pa)]
            tk = [sb(f"tk{i}", [128, 32], F32, pa) for i in range(4)]
            tq = [sb(f"tq{i}", [128, 8, 32], F32, pa) for i in range(4)]
            kr = sb("kr", [128, 64], BF16, pa)
            qr = sb("qr", [128, 8, 64], BF16, pa)
            ktrst = [sb(f"ktrst{i}", [64, 512], BF16, pa) for i in range(2)]
            qtrst = [sb(f"qtrst{i}", [64, 8, 512], BF16, pa) for i in range(1)]
            fst = [sb(f"fst{i}", [128, 512], BF16, pa) for i in range(3)]
            vst = [sb(f"vst{i}", [128, 4, 1024], BF16, pa) for i in range(1)]
            pT = ps("pT5", [128, D], BF16, pa)
            mmps = [ps(f"mm5_{i}", [128, 512], F32, pa) for i in range(4)]
            pt2 = ps("pt2", [128, 1024], BF16, pa)

            k.dma_batch("sp", [(["mwin_b"], ["wkr"], wkr[:], mwin_b[:, 1024:1088].rearrange("(kc p) c -> p kc c", p=128)),
                               (["muqn_b"], ["uqn"], uqn[:], muqn_b.rearrange("(kc p) c -> p kc c", p=128)),
                               (["muqr_b"], ["uqr"], uqr[:], muqr_b.rearrange("(kc p) c -> p kc c", p=128)),
                               (["mukvk_b"], ["ukvk"], ukvk[:], mukvk_b.rearrange("(kc p) c -> p kc c", p=128)),
                               (["mukvv_b"], ["ukvv"], ukvv[:], mukvv_b.rearrange("(kc p) c -> p kc c", p=128)),
                               ([], ["qng"], qng[:], mqn_d.partition_broadcast(128)),
                               ([], ["kvng"], kvng[:], mkvn_d.partition_broadcast(128))], "cb0")
            xi = wi = mi = fi = 0
            wcols = [0, 512, 1088, 1600]

            def load_w1(idx):
                c0 = wcols[idx % 4]
                j_ = idx % 2
                k.dma("sp", ["mwin_b"], [("wblk", j_)], wblk[j_][:], mwin_b[:, c0:c0 + 512].rearrange("(kc p) c -> p kc c", p=128), f"wblk{j_}")

            load_w1(0)
            for g in range(NQG):
                tsl = slice(g * 512, (g + 1) * 512)
                for tl in range(4):
                    t = g * 4 + tl
                    j = xi % 2
                    xi += 1
                    k.wait("sp", ar0_toks[g])
                    k.dma("sp", [], [("xt", j)], xts[j][:], s0_d[t * 128:(t + 1) * 128, :], f"xt{j}")
                    prologue_tile(1, xts[j], ("xt", j), junk, ssq, rstd, hb, pT, hT, tl)
                    rope_tables(rtmp, t, invf1, "invf1", 32, cos1[:, tl, :], sin1[:, tl, :], ("tab1", tl))
                    k.do("dve", [(("tab1", tl), "cos")], [("cosq", tl)], lambda: dve.tensor_scalar(out=cosq[:, tl, :], in0=cos1[:, tl, :], scalar1=QSC, scalar2=None, op0=ALU.mult))
                    k.do("dve", [(("tab1", tl), "sin")], [("sinq", tl)], lambda: dve.tensor_scalar(out=sinq[:, tl, :], in0=sin1[:, tl, :], scalar1=QSC, scalar2=None, op0=ALU.mult))

                for blk in range(2):
                    j = wi % 2
                    wi += 1
                    wb = wblk[j]
                    if wi < NQG * 4:
                        load_w1(wi)
                    gn_t, gn_k = (qng, "qng") if blk == 0 else (kvng, "kvng")
                    def c_mm(tl):
                        nonlocal mi
                        mj = mi % 4
                        mi += 1
                        mp = mmps[mj]
                        k.group("pe", [("wblk", j), ("hT", tl)], [("mm", mj)],
                                [(lambda kc=kc: pe.matmul(mp[:], lhsT=hT[:, kc, tl * 128:(tl + 1) * 128], rhs=wb[:, kc, :], start=(kc == 0), stop=(kc == 15))) for kc in range(16)])
                        return mj

                    def c_post(tl, mj):
                        mp = mmps[mj]
                        k.do("act", [("mm", mj)], ["junk", "ssq"], lambda: act.activation(out=junk[:, 0:512], in_=mp[:], func=AF.Square, accum_out=ssq[:]))
                        k.do("act", ["ssq"], ["rstd"], lambda: act.activation(out=rstd[:], in_=ssq[:], func=AF.Sqrt, scale=float(1.0 / 512), bias=float(EPS)))
                        k.do("dve", ["rstd"], ["rstd"], lambda: dve.reciprocal(out=rstd[:], in_=rstd[:]))
                        cj = tl % 2
                        k.do("dve", [("mm", mj), "rstd", gn_k], [("cn", cj)], lambda: dve.scalar_tensor_tensor(out=cn[cj][:], in0=mp[:], scalar=rstd[:, 0:1], in1=gn_t[:], op0=ALU.mult, op1=ALU.mult))

                    def c_tr(tl):
                        cj = tl % 2
                        k.group("pe", [("cn", cj), "ident"], ["pt2"], [(lambda rc=rc: pe.transpose(pt2[:, rc * 128:(rc + 1) * 128], cn[cj][:, rc * 128:(rc + 1) * 128], ident[:])) for rc in range(4)])
                        k.do("act", ["pt2"], [("cnT", blk, tl)], lambda: act.copy(out=cnT[blk][:, :, tl * 128:(tl + 1) * 128], in_=pt2[:, 0:512].rearrange("p (a b) -> p a b", a=4)))

                    mjs = {}
                    mjs[0] = c_mm(0)
                    for tl in range(4):
                        if tl + 1 < 4:
                            mjs[tl + 1] = c_mm(tl + 1)
                        c_post(tl, mjs[tl])
                        if tl >= 1:
                            c_tr(tl - 1)
                    c_tr(3)

                kj = g % 2
                for tl in range(4):
                    mj = mi % 4
                    mi += 1
                    mp = mmps[mj]
                    k.group("pe", ["wkr", ("hT", tl)], [("mm", mj)],
                            [(lambda kc=kc: pe.matmul(mp[:, 0:64], lhsT=hT[:, kc, tl * 128:(tl + 1) * 128], rhs=wkr[:, kc, :], start=(kc == 0), stop=(kc == 15))) for kc in range(16)])
                    x1, x2 = mp[:, 0:32], mp[:, 32:64]
                    RK = [("mm", mj), (("tab1", tl), "cos"), (("tab1", tl), "sin")]
                    k.do("dve", RK, ["tk0"], lambda: dve.tensor_tensor(out=tk[0][:], in0=x1, in1=cos1[:, tl, :], op=ALU.mult))
                    k.do("dve", RK, ["tk1"], lambda: dve.tensor_tensor(out=tk[1][:], in0=x2, in1=sin1[:, tl, :], op=ALU.mult))
                    k.do("dve", RK, ["tk2"], lambda: dve.tensor_tensor(out=tk[2][:], in0=x2, in1=cos1[:, tl, :], op=ALU.mult))
                    k.do("dve", RK, ["tk3"], lambda: dve.tensor_tensor(out=tk[3][:], in0=x1, in1=sin1[:, tl, :], op=ALU.mult))
                    k.do("pool", ["tk0", "tk1"], [("kr", 0)], lambda: pool.tensor_tensor(out=kr[:, 0:32], in0=tk[0][:], in1=tk[1][:], op=ALU.subtract))
                    k.do("pool", ["tk2", "tk3"], [("kr", 1)], lambda: pool.tensor_tensor(out=kr[:, 32:64], in0=tk[2][:], in1=tk[3][:], op=ALU.add))
                    k.do("pe", [("kr", 0), ("kr", 1), "ident"], ["pt2"], lambda: pe.transpose(pt2[0:64, 0:128], kr[:], ident[:]))
                    k.do("act", ["pt2"], [("ktrst", kj, tl)], lambda: act.copy(out=ktrst[kj][:, tl * 128:(tl + 1) * 128], in_=pt2[0:64, 0:128]))
                tok = k.dma("sp", [("ktrst", kj, tl) for tl in range(4)], [], ktr_d[:, tsl], ktrst[kj][:], f"ktrst{kj}")
                k.stores.append(tok)

                for blk in range(2):
                    j = wi % 2
                    wi += 1
                    wb = wblk[j]
                    if wi < NQG * 4:
                        load_w1(wi)
                    for cc in range(4):
                        mj = mi % 4
                        mi += 1
                        mp = mmps[mj]
                        k.group("pe", [("wblk", j)] + [("hT", tl) for tl in range(4)], [("mm", mj)],
                                [(lambda kc=kc: pe.matmul(mp[:], lhsT=wb[:, kc, cc * 128:(cc + 1) * 128], rhs=hT[:, kc, :], start=(kc == 0), stop=(kc == 15))) for kc in range(16)])
                        fj = fi % 3
                        fi += 1
                        k.do("act", [("mm", mj)], [("fst", fj)], lambda: act.activation(out=fst[fj][:], in_=mp[:], func=AF.Silu))
                        row = (blk * 4 + cc) * 128
                        tok = k.dma("sp", [("fst", fj)], [], gt_d[row:row + 128, tsl], fst[fj][:], f"fst{fj}")
                        k.stores.append(tok)

                for which in range(2):
                    wt, wk_, dst_d = [(uqn, "uqn", qtn_d), (ukvk, "ukvk", ktn_d)][which]
                    for hd in range(8):
                        mj = mi % 4
                        mi += 1
                        mp = mmps[mj]
                        k.group("pe", [wk_] + [("cnT", which, tl) for tl in range(4)], [("mm", mj)],
                                [(lambda rc=rc: pe.matmul(mp[:], lhsT=wt[:, rc, hd * 128:(hd + 1) * 128], rhs=cnT[which][:, rc, :], start=(rc == 0), stop=(rc == 3))) for rc in range(4)])
                        fj = fi % 3
                        fi += 1
                        if which == 0:
                            k.do("act", [("mm", mj)], [("fst", fj)], lambda: act.activation(out=fst[fj][:], in_=mp[:], func=AF.Copy, scale=QSC))
                        else:
                            k.do("dve", [("mm", mj)], [("fst", fj)], lambda: dve.tensor_copy(out=fst[fj][:], in_=mp[:]))
                        tok = k.dma("sp", [("fst", fj)], [], dst_d[hd, :, tsl], fst[fj][:], f"fst{fj}")
                        k.stores.append(tok)

                qj = 0
                for tl in range(4):
                    mj = mi % 4
                    mi += 1
                    mp = mmps[mj]
                    k.group("pe", ["uqr", ("cnT", 0, tl)], [("mm", mj)],
                            [(lambda rc=rc: pe.matmul(mp[:], lhsT=cnT[0][:, rc, tl * 128:(tl + 1) * 128], rhs=uqr[:, rc, :], start=(rc == 0), stop=(rc == 3))) for rc in range(4)])
                    mv4 = mp[:].rearrange("p (h t f) -> p h t f", h=8, t=2)
                    x1, x2 = mv4[:, :, 0, :], mv4[:, :, 1, :]
                    cb_ = cosq[:, tl, :].unsqueeze(1).broadcast_to([128, 8, 32])
                    sb_ = sinq[:, tl, :].unsqueeze(1).broadcast_to([128, 8, 32])
                    RK = [("mm", mj), ("cosq", tl), ("sinq", tl)]
                    k.do("dve", RK, ["tq0"], lambda: dve.tensor_tensor(out=tq[0][:], in0=x1, in1=cb_, op=ALU.mult))
                    k.do("dve", RK, ["tq1"], lambda: dve.tensor_tensor(out=tq[1][:], in0=x2, in1=sb_, op=ALU.mult))
                    k.do("dve", RK, ["tq2"], lambda: dve.tensor_tensor(out=tq[2][:], in0=x2, in1=cb_, op=ALU.mult))
                    k.do("dve", RK, ["tq3"], lambda: dve.tensor_tensor(out=tq[3][:], in0=x1, in1=sb_, op=ALU.mult))
                    k.do("pool", ["tq0", "tq1"], [("qr", 0)], lambda: pool.tensor_tensor(out=qr[:, :, 0:32], in0=tq[0][:], in1=tq[1][:], op=ALU.subtract))
                    k.do("pool", ["tq2", "tq3"], [("qr", 1)], lambda: pool.tensor_tensor(out=qr[:, :, 32:64], in0=tq[2][:], in1=tq[3][:], op=ALU.add))
                    k.group("pe", [("qr", 0), ("qr", 1), "ident"], ["pt2"], [(lambda hd=hd: pe.transpose(pt2[0:64, hd * 128:(hd + 1) * 128], qr[:, hd, :], ident[:])) for hd in range(8)])
                    k.do("act", ["pt2"], [("qtrst", qj, tl)], lambda: act.copy(out=qtrst[qj][:, :, tl * 128:(tl + 1) * 128], in_=pt2[0:64, :].rearrange("p (a b) -> p a b", a=8)))
                tok = k.dma("sp", [("qtrst", qj, tl) for tl in range(4)], [], qtr_d[:, :, tsl].rearrange("h d t -> d h t"), qtrst[qj][:], f"qtrst{qj}")
                k.stores.append(tok)

                vj = 0
                for tl in range(4):
                    for vb in range(2):
                        mj = mi % 4
                        mi += 1
                        mp = mmps[mj]
                        k.group("pe", ["ukvv", ("cnT", 1, tl)], [("mm", mj)],
                                [(lambda rc=rc: pe.matmul(mp[:], lhsT=cnT[1][:, rc, tl * 128:(tl + 1) * 128], rhs=ukvv[:, rc, vb * 512:(vb + 1) * 512], start=(rc == 0), stop=(rc == 3))) for rc in range(4)])
                        if vb == 0:
                            k.do("act", [("mm", mj)], [("vst", vj, tl, vb)], lambda: act.copy(out=vst[vj][:, tl, 0:512], in_=mp[:]))
                        else:
                            k.do("dve", [("mm", mj)], [("vst", vj, tl, vb)], lambda: dve.tensor_copy(out=vst[vj][:, tl, 512:1024], in_=mp[:]))
                for hd in range(8):
                    tok = k.dma("sp", [("vst", vj, tl, vb) for tl in range(4) for vb in range(2)], [],
                                v1_d[hd, :, g * 4:(g + 1) * 4, :], vst[vj][:, :, hd * 128:(hd + 1) * 128], f"vst{vj}", fence=(hd == 0))
                k.stores.append(tok)
            k.barrier()

    if upto >= 6:
        with ExitStack() as pb:
            ktn = [sb(f"ktn{i}", [128, S], BF16, pb) for i in range(2)]
            vh = [sb(f"vh{i}", [128, NT, 128], BF16, pb) for i in range(2)]
            ktr = sb("ktr", [64, S], BF16, pb)
            qn = [sb(f"qn{i}", [128, 512], BF16, pb) for i in range(2)]
            qrr = [sb(f"qrr{i}", [64, 512], BF16, pb) for i in range(2)]
            gtt = [sb(f"gtt{i}", [128, 512], BF16, pb) for i in range(2)]
            pTt = [sb(f"pTt{i}", [128, 512], BF16, pb) for i in range(5)]
            amask = sb("amask", [128, 4, 512], BF16, pb)
            ones = sb("ones", [128, 128], F32, pb)
            accD = [sb(f"accD{i}", [128, 512], F32, pb) for i in range(2)]
            accP = [sb(f"accP{i}", [128, 512], F32, pb) for i in range(2)]
            rden = [sb(f"rden{i}", [128, 512], F32, pb) for i in range(2)]
            on = [sb(f"on{i}", [128, 512], F32, pb) for i in range(2)]
            zst = [sb(f"zst1_{i}", [128, 512], BF16, pb) for i in range(2)]
            pss = [ps(f"pss{i}", [128, 512], F32, pb) for i in range(3)]
            pso = [ps(f"pso{i}", [128, 512], F32, pb) for i in range(2)]
            psd = [ps(f"psd{i}", [128, 512], F32, pb) for i in range(2)]

            k.dma_batch("sp", [([], ["amask"], amask[:], amask_d), ([], ["ktr"], ktr[:], ktr_d)], "cb0")
            k.do("dve", [], ["ones"], lambda: dve.memset(ones[:], 1.0))

            def load_head(hd):
                hj = hd % 2
                k.dma("sp", [], [("ktn", hj)], ktn[hj][:], ktn_d[hd], f"ktn{hj}")
                k.dma("sp", [], [("vh", hj)], vh[hj][:], v1_d[hd], f"vh{hj}")

            load_head(0)
            si = pi = qi = 0
            for hd in range(8):
                hj = hd % 2
                if hd + 1 < 8:
                    load_head(hd + 1)
                for G in range(NQG):
                    qj = qi % 2
                    qi += 1
                    tsl = slice(G * 512, (G + 1) * 512)
                    k.dma("sp", [], [("qn", qj)], qn[qj][:], qtn_d[hd, :, tsl], f"qn{qj}")
                    k.dma("sp", [], [("qrr", qj)], qrr[qj][:], qtr_d[hd, :, tsl], f"qrr{qj}")
                    k.dma("sp", [], [("gtt", qj)], gtt[qj][:], gt_d[hd * 128:(hd + 1) * 128, tsl], f"gtt{qj}")
                    nkb = 4 * G + 4

                    def s_part(kb):
                        sj = kb % 3
                        ksl = slice(kb * 128, (kb + 1) * 128)
                        k.group("pe", [("ktn", hj), "ktr", ("qn", qj), ("qrr", qj)], [("pss", sj)], [
                            lambda: pe.matmul(pss[sj][:], lhsT=ktn[hj][:, ksl], rhs=qn[qj][:], start=True, stop=False),
                            lambda: pe.matmul(pss[sj][:], lhsT=ktr[:, ksl], rhs=qrr[qj][:], start=False, stop=True)])

                    def p_part(kb):
                        sj = kb % 3
                        pj = kb % 5
                        k.do("act", [("pss", sj)], [("pTt", pj)], lambda: act.activation(out=pTt[pj][:], in_=pss[sj][:], func=AF.Exp))
                        if kb >= 4 * G:
                            r = kb - 4 * G
                            k.do("pool", [("pTt", pj), "amask"], [("pTt", pj)], lambda: pool.tensor_tensor(out=pTt[pj][:], in0=pTt[pj][:], in1=amask[:, r, :], op=ALU.mult))
                        eng_, acc_, akey = ("dve", accD[qj], ("accD", qj)) if kb % 2 == 0 else ("pool", accP[qj], ("accP", qj))
                        e_ = dve if eng_ == "dve" else pool
                        if kb < 2:
                            k.do(eng_, [("pTt", pj)], [akey], lambda: e_.tensor_copy(out=acc_[:], in_=pTt[pj][:]))
                        else:
                            k.do(eng_, [("pTt", pj), akey], [akey], lambda: e_.tensor_tensor(out=acc_[:], in0=acc_[:], in1=pTt[pj][:], op=ALU.add))
                        k.do("pe", [("vh", hj), ("pTt", pj)], [("pso", qj)],
                             lambda: pe.matmul(pso[qj][:], lhsT=vh[hj][:, kb, :], rhs=pTt[pj][:], start=(kb == 0), stop=(kb == nkb - 1)))

                    s_part(0)
                    if nkb > 1:
                        s_part(1)
                    for kb in range(nkb):
                        if kb + 2 < nkb:
                            s_part(kb + 2)
                        p_part(kb)
                    k.do("dve", [("accD", qj), ("accP", qj)], [("accD", qj)], lambda: dve.tensor_tensor(out=accD[qj][:], in0=accD[qj][:], in1=accP[qj][:], op=ALU.add))
                    k.do("pe", [("accD", qj), "ones"], [("psd", qj)], lambda: pe.matmul(psd[qj][:], lhsT=ones[:], rhs=accD[qj][:], start=True, stop=True))
                    k.do("dve", [("psd", qj)], [("rden", qj)], lambda: dve.reciprocal(out=rden[qj][:], in_=psd[qj][:]))
                    k.do("dve", [("pso", qj), ("rden", qj)], [("on", qj)], lambda: dve.tensor_tensor(out=on[qj][:], in0=pso[qj][:], in1=rden[qj][:], op=ALU.mult))
                    k.do("pool", [("on", qj), ("gtt", qj)], [("zst1", qj)], lambda: pool.tensor_tensor(out=zst[qj][:], in0=on[qj][:], in1=gtt[qj][:], op=ALU.mult))
                    tok = k.dma("sp", [("zst1", qj)], [], zt_d[hd * 128:(hd + 1) * 128, tsl], zst[qj][:], f"zst1_{qj}")
                    k.stores.append(tok)
            k.barrier()

    if upto >= 7:
        with ExitStack() as pc:
            wout = sb("wout1", [128, 8, D], BF16, pc)
            zTg = [sb(f"zTg{i}", [128, 8, 512], BF16, pc) for i in range(2)]
            ost = [sb(f"ost{i}", [128, D], F32, pc) for i in range(2)]
            xc = [sb(f"xc{i}", [128, D], F32, pc) for i in range(2)]
            pout = [ps(f"pout1_{i}", [128, 512], F32, pc) for i in range(4)]
            for q4 in range(2):
                k.dma("sp", ["mwout_b"], [("wout1", q4)], wout[:, q4 * 4:(q4 + 1) * 4, :],
                      mwout_b[q4 * 512:(q4 + 1) * 512, :].rearrange("(kc p) c -> p kc c", p=128), "cb1", fence=(q4 == 0))
            k.lw[("wout1", 0)] = k.lw[("wout1", 1)]
            oi = 0
            for g in range(NQG):
                ctoks = []
                gj = g % 2
                k.dma("sp", [], [("zTg", gj)], zTg[gj][:], zt_d[:, g * 512:(g + 1) * 512].rearrange("(kc p) t -> p kc t", p=128), f"zTg{gj}")
                for tl in range(4):
                    t = g * 4 + tl
                    j = oi % 2
                    oi += 1
                    k.dma("sp", [], [("xc", j)], xc[j][:], s0_d[t * 128:(t + 1) * 128, :], f"xc{j}")
                    for cb in range(4):
                        k.group("pe", [("zTg", gj), ("wout1", 0), ("wout1", 1)], [("pout", cb)],
                                [(lambda kc=kc: pe.matmul(pout[cb][:], lhsT=zTg[gj][:, kc, tl * 128:(tl + 1) * 128], rhs=wout[:, kc, cb * 512:(cb + 1) * 512], start=(kc == 0), stop=(kc == 7))) for kc in range(8)])
                        csl = slice(cb * 512, (cb + 1) * 512)
                        k.do("dve", [("pout", cb), "gate1"], [("ost", j, cb)], lambda: dve.tensor_tensor(out=ost[j][:, csl], in0=pout[cb][:], in1=gate[1][:, csl], op=ALU.mult))
                        k.do("dve", [("ost", j, cb), ("xc", j)], [("ost", j, cb)], lambda: dve.scalar_tensor_tensor(out=ost[j][:, csl], in0=xc[j][:, csl], scalar=0.5, in1=ost[j][:, csl], op0=ALU.mult, op1=ALU.add))
                    tok = k.dma("sp", [("ost", j, cb) for cb in range(4)], [], p1_d[t * 128:(t + 1) * 128, :], ost[j][:], f"ost{j}")
                    k.stores.append(tok)
                    ctoks.append(tok)
                rs1_toks.append(collective(p1_d[g * 512:(g + 1) * 512, :], s1_d[g * 256:(g + 1) * 256, :], "rs1", "ReduceScatter", ctoks))
            k.barrier(exclude=("cc_rs1",))

    if upto >= 8:
        with ExitStack() as pf:
            xf = [sb(f"xf{i}", [128, D], F32, pf) for i in range(2)]
            junk = sb("junkf", [128, D], BF16, pf)
            ssq = sb("ssqf", [128, 1], F32, pf)
            rstd = sb("rstdf", [128, 1], F32, pf)
            fg = sb("fg", [128, D], F32, pf)
            k.dma("sp", [], ["fg"], fg[:], fng_d.partition_broadcast(128), "cb0")
            for t in range(NT // 2):
                j = t % 2
                k.wait("sp", rs1_toks[t // 2])
                k.dma("sp", [], [("xf", j)], xf[j][:], s1_d[t * 128:(t + 1) * 128, :], f"xf{j}")
                k.do("act", [("xf", j)], ["junkf", "ssqf"], lambda: act.activation(out=junk[:], in_=xf[j][:], func=AF.Square, accum_out=ssq[:]))
                k.do("act", ["ssqf"], ["rstdf"], lambda: act.activation(out=rstd[:], in_=ssq[:], func=AF.Sqrt, scale=float(1.0 / D), bias=float(EPS)))
                k.do("dve", ["rstdf"], ["rstdf"], lambda: dve.reciprocal(out=rstd[:], in_=rstd[:]))
                k.do("dve", [("xf", j), "rstdf", "fg"], [("xf", j)], lambda: dve.scalar_tensor_tensor(out=xf[j][:], in0=xf[j][:], scalar=rstd[:, 0:1], in1=fg[:], op0=ALU.mult, op1=ALU.mult))
                tok = k.dma("sp", [("xf", j)], [], out_d[t * 128:(t + 1) * 128, :], xf[j][:], f"xfo{j}")
                k.stores.append(tok)
            k.barrier()

    if dbg:
        for nm, src in ([("p0", p0_d), ("s0", s0_d)] if upto >= 4 else []) + ([("p1", p1_d), ("s1", s1_d)] if upto >= 7 else []):
            dd = nc.dram_tensor(nm, list(src.shape), F32, kind="ExternalOutput").ap()
            k.stores.append(k.dma("sp", [], [nm + "_dbg"], dd, src, "dbg_" + nm))

    k.wait("sp", *k.stores)
    for nm in ["rwin_b", "rwout_b", "mwin_b", "muqn_b", "muqr_b", "mukvk_b", "mukvv_b", "mwout_b"]:
        k.wait("pool", k.lw[nm])
    for e in ["pe", "act", "dve", "pool"]:
        if e in k.last_tok:
            k.wait("sp", k.last_tok[e])
    es.close()
    return nc


def host_inputs(S, x, c, positions, ada_w, ada_b, norm_g, ret_w_in, ret_gn_g, ret_w_out,
                mla_w_in, mla_q_norm_g, mla_w_uq, mla_kv_norm_g, mla_w_ukv, mla_w_out, final_norm_g):
    f32 = np.float32
    bf = ml_dtypes.bfloat16
    NT = S // 128
    ident = np.eye(128, dtype=f32).astype(bf)
    invf0 = (10000.0 ** (-np.arange(0, 256, 2, dtype=f32) / f32(256))).astype(f32)
    invf1 = (10000.0 ** (-np.arange(0, 64, 2, dtype=f32) / f32(64))).astype(f32)
    n = np.arange(128, dtype=np.float64)
    cm = (n[None, :] >= n[:, None]).astype(f32).astype(bf)
    am = np.zeros((128, 4, 512), f32)
    qq = np.arange(512)
    for r in range(4):
        am[:, r, :] = (128 * r + np.arange(128)[:, None] <= qq[None, :])
    am = am.astype(bf)
    maps = []
    for core in range(NCORES):
        b, hg = core // 2, core % 2
        hs = np.arange(hg * 4, hg * 4 + 4)
        lg = np.log1p(-np.exp2(-5.0 - hs.astype(np.float64)))
        dqv = np.exp(lg[None, :] * (n[:, None] + 1.0)).astype(f32)
        dkv = (np.exp(-lg[None, :] * (n[:, None] + 1.0)) / 16.0).astype(f32)
        w = ret_w_in[0]
        qc = np.concatenate([np.arange(h * 256, (h + 1) * 256) for h in hs])
        vc = np.concatenate([np.arange(h * 512, (h + 1) * 512) for h in hs])
        rw = np.concatenate([w[:, qc], w[:, 2048 + qc], w[:, 4096 + vc], w[:, 8192 + vc]], axis=1)
        mh = np.arange(hg * 8, hg * 8 + 8)
        mw = mla_w_in[0]
        gcol = 1088 + np.concatenate([np.arange(h * 128, (h + 1) * 128) for h in mh])
        mwin = np.concatenate([mw[:, :1088], mw[:, gcol]], axis=1)
        uq = mla_w_uq[0].reshape(512, 16, 192)[:, mh, :]
        ukv = mla_w_ukv[0].reshape(512, 16, 256)[:, mh, :]
        m = {
            "x": np.ascontiguousarray(x[b]),
            "c_l": np.ascontiguousarray(c[b].reshape(16, 128).T),
            "pos_l": np.ascontiguousarray(positions[b].reshape(NT, 128).T.astype(np.int32)),
            "ada_w": np.ascontiguousarray(ada_w[:, :, hg * 3072:(hg + 1) * 3072]), "ada_b": ada_b, "norm_g": norm_g,
            "ret_w_in": np.ascontiguousarray(rw),
            "ret_gn_g": np.ascontiguousarray(ret_gn_g[0][vc]),
            "ret_w_out": np.ascontiguousarray(ret_w_out[0][vc, :]),
            "mla_w_in": np.ascontiguousarray(mwin),
            "mla_q_norm_g": mla_q_norm_g[0],
            "mla_w_uq_nope": np.ascontiguousarray(uq[:, :, :128].reshape(512, 1024)),
            "mla_w_uq_rope": np.ascontiguousarray(uq[:, :, 128:].reshape(512, 512)),
            "mla_kv_norm_g": mla_kv_norm_g[0],
            "mla_w_ukv_k": np.ascontiguousarray(ukv[:, :, :128].reshape(512, 1024)),
            "mla_w_ukv_v": np.ascontiguousarray(ukv[:, :, 128:].reshape(512, 1024)),
            "mla_w_out": np.ascontiguousarray(mla_w_out[0][np.concatenate([np.arange(h * 128, (h + 1) * 128) for h in mh]), :]),
            "final_norm_g": final_norm_g,
            "ident": ident, "invf0": invf0, "invf1": invf1, "dq": dqv, "dk": dkv, "g128": np.ascontiguousarray(np.broadcast_to(np.exp(lg * 128.0).astype(f32)[None, :], (128, 4))),
            "cmask": cm, "amask": am,
        }
        maps.append(m)
    return maps


_CACHE = {}


def kernel(**inputs):
    inputs = {kk: np.asarray(v) for kk, v in inputs.items()}
    B, S, _ = inputs["x"].shape
    if S not in _CACHE:
        _CACHE[S] = build_program(S)
    nc = _CACHE[S]
    maps = host_inputs(S, **inputs)
    res = run_bass_kernel_spmd(nc, maps, core_ids=list(range(NCORES)))
    out = np.empty((B, S, D), np.float32)
    for core in range(NCORES):
        b, hg = core // 2, core % 2
        out[b].reshape(S // 512, 2, 256, D)[:, hg] = res.results[core]["out"].reshape(S // 512, 256, D)
    return out
```

```python
import numpy as np
import ml_dtypes
from contextlib import ExitStack
import concourse.bass as bass
import concourse.mybir as mybir
from concourse.bass_utils import run_bass_kernel_spmd

F32 = mybir.dt.float32
BF16 = mybir.dt.bfloat16
I32 = mybir.dt.int32
ALU = mybir.AluOpType
AF = mybir.ActivationFunctionType
AX = mybir.AxisListType

D = 2048
EPS = 1e-6
NCORES = 8
PAIRS = [[0, 1], [2, 3], [4, 5], [6, 7]]
NCORES_RUN = [8]
TWO_PI = 2.0 * np.pi
CW1 = 6.28125
CW2 = float(TWO_PI - 6.28125)


class Ctx:
    EPOCH = 30000

    def __init__(self, nc, es):
        self.nc = nc
        self.es = es
        self.eng = {"pe": nc.tensor, "act": nc.scalar, "dve": nc.vector, "pool": nc.gpsimd, "sp": nc.sync}
        self.psem = {}
        self.pcount = {}
        self.pname = {}
        self.pgen = {}
        self.last_tok = {}
        for e in ["pe", "act", "dve", "pool"]:
            self.pgen[e] = 0
            self._new_psem(e)
        self.waited = {}
        self.dsems = {}
        self.lw = {}
        self.rd = {}
        self.stores = []

    def _new_psem(self, e):
        name = f"p_{e}{self.pgen[e]}"
        self.pgen[e] += 1
        self.psem[e] = self.es.enter_context(self.nc.semaphore(name))
        self.pcount[e] = 0
        self.pname[e] = name

    def sig(self, e, inst):
        if self.pcount[e] >= self.EPOCH:
            self._new_psem(e)
        self.pcount[e] += 1
        inst.then_inc(self.psem[e], 1)
        tok = (self.psem[e], self.pcount[e], self.pname[e])
        self.last_tok[e] = tok
        return tok

    def wait(self, e, *toks):
        flat = []

        def rec(ts):
            for t in ts:
                if t is None:
                    continue
                if isinstance(t, list):
                    rec(t)
                else:
                    flat.append(t)
        rec(toks)
        best = {}
        for t in flat:
            if t[2] not in best or best[t[2]][1] < t[1]:
                best[t[2]] = t
        for sem, val, name in best.values():
            key = (e, name)
            if e == "pe" and name.startswith("p_pe"):
                continue
            if self.waited.get(key, 0) >= val:
                continue
            self.waited[key] = val
            self.eng[e].wait_ge(sem, val)

    def _deps(self, e, R, W):
        toks = [self.lw.get(r) for r in R]
        for w in W:
            toks.append(self.lw.get(w))
            toks.append(self.rd.get(w, []))
        self.wait(e, toks)

    def _commit(self, tok, R, W):
        for r in R:
            self.rd.setdefault(r, []).append(tok)
            if len(self.rd[r]) > 6:
                best = {}
                for t in self.rd[r]:
                    if t[2] not in best or best[t[2]][1] < t[1]:
                        best[t[2]] = t
                self.rd[r] = list(best.values())
        for w in W:
            self.lw[w] = tok
            self.rd[w] = []

    def barrier(self, exclude=()):
        toks = [self.last_tok[e] for e in ["pe", "act", "dve", "pool"] if e in self.last_tok]
        toks += [(d[0], d[1], d[2]) for d in self.dsems.values() if d[1] > 0 and d[2] not in exclude]
        for e in ["pe", "act", "dve", "pool", "sp"]:
            self.wait(e, *toks)

    def do(self, e, R, W, fn):
        self._deps(e, R, W)
        tok = self.sig(e, fn())
        self._commit(tok, R, W)
        return tok

    def group(self, e, R, W, fns):
        self._deps(e, R, W)
        inst = None
        for fn in fns:
            inst = fn()
        tok = self.sig(e, inst)
        self._commit(tok, R, W)
        return tok

    def dma_batch(self, q, items, semname):
        toks = [self.dma(q, R, W, out, in_, semname, fence=(i == 0)) for i, (R, W, out, in_) in enumerate(items)]
        for (R, W, out, in_) in items:
            for w in W:
                self.lw[w] = toks[-1]
        return toks[-1]

    def dma(self, q, R, W, out, in_, semname, fence=True):
        self._deps(q, R, W)
        if semname not in self.dsems:
            self.dsems[semname] = [self.es.enter_context(self.nc.semaphore("d_" + semname)), 0, "d_" + semname]
        ds = self.dsems[semname]
        if fence and ds[1] > 0:
            self.wait(q, (ds[0], ds[1], ds[2]))
        inst = self.eng[q].dma_start(out=out, in_=in_)
        ds[1] += 16
        inst.then_inc(ds[0], 16)
        tok = (ds[0], ds[1], ds[2])
        self._commit(tok, R, W)
        return tok


def build_program(S, dbg=False, upto=99):
    NT = S // 128
    import os
    GT = min(NT, int(os.environ.get("KGT", "4")))
    NG = NT // GT
    GTOK = GT * 128
    nc = bass.Bass("TRN2", target_bir_lowering=False)
    es = ExitStack()
    k = Ctx(nc, es)
    pe, act, dve, pool, sp = nc.tensor, nc.scalar, nc.vector, nc.gpsimd, nc.sync

    def din(name, shape, dt=F32):
        return nc.dram_tensor(name, list(shape), dt, kind="ExternalInput").ap()

    def dscr(name, shape, dt, out=False, internal=False):
        kind = "ExternalOutput" if ((out or dbg) and not internal) else "Internal"
        return nc.dram_tensor(name, list(shape), dt, kind=kind).ap()

    x_d = din("x", [S, D])
    c_d = din("c_l", [128, 16])
    pos_d = din("pos_l", [128, NT], I32)
    ada_w_d = din("ada_w", [2, D, 3072])
    ada_b_d = din("ada_b", [2, 3 * D])
    norm_g_d = din("norm_g", [2, D])
    rwin_d = din("ret_w_in", [D, 6144])
    rgn_d = din("ret_gn_g", [2048])
    rwout_d = din("ret_w_out", [2048, D])
    mwin_d = din("mla_w_in", [D, 2112])
    mqn_d = din("mla_q_norm_g", [512])
    muqn_d = din("mla_w_uq_nope", [512, 1024])
    muqr_d = din("mla_w_uq_rope", [512, 512])
    mkvn_d = din("mla_kv_norm_g", [512])
    mukvk_d = din("mla_w_ukv_k", [512, 1024])
    mukvv_d = din("mla_w_ukv_v", [512, 1024])
    mwout_d = din("mla_w_out", [1024, D])
    fng_d = din("final_norm_g", [D])
    ident_d = din("ident", [128, 128], BF16)
    invf0_d = din("invf0", [128])
    invf1_d = din("invf1", [32])
    dq_d = din("dq", [128, 4])
    dk_d = din("dk", [128, 4])
    g128_d = din("g128", [128, 4])
    cmask_d = din("cmask", [128, 128], BF16)
    amask_d = din("amask", [128, 4, 512], BF16)

    out_d = nc.dram_tensor("out", [S // 2, D], F32, kind="ExternalOutput").ap()

    rwin_b = dscr("rwin_b", [D, 6144], BF16)
    rwout_b = dscr("rwout_b", [2048, D], BF16)
    mwin_b = dscr("mwin_b", [D, 2112], BF16)
    muqn_b = dscr("muqn_b", [512, 1024], BF16)
    muqr_b = dscr("muqr_b", [512, 512], BF16)
    mukvk_b = dscr("mukvk_b", [512, 1024], BF16)
    mukvv_b = dscr("mukvv_b", [512, 1024], BF16)
    mwout_b = dscr("mwout_b", [1024, D], BF16)
    modloc_d = dscr("modloc", [1, 6144], F32, internal=True)
    modall_d = dscr("modall", [2, 6144], F32, internal=True)
    qs_d = dscr("qs", [S, 1024], BF16)
    ks_d = dscr("ks", [S, 1024], BF16)
    vs_d = dscr("vs", [S, 2048], BF16)
    gs_d = dscr("gs", [S, 2048], BF16)
    zs_d = dscr("zs", [S, 2048], BF16)
    p0_d = dscr("p0_i", [S, D], F32, internal=True)
    s0_d = dscr("s0_i", [S, D], F32, internal=True)
    x1_d = dscr("x1", [S, D], F32)
    qtn_d = dscr("qtn", [8, 128, S], BF16)
    qtr_d = dscr("qtr", [8, 64, S], BF16)
    ktn_d = dscr("ktn", [8, 128, S], BF16)
    ktr_d = dscr("ktr", [64, S], BF16)
    v1_d = dscr("v1", [8, 128, S // 128, 128], BF16)
    gt_d = dscr("gt", [1024, S], BF16)
    zt_d = dscr("zt", [1024, S], BF16)
    p1_d = dscr("p1_i", [S, D], F32, internal=True)
    s1_d = dscr("s1_i", [S // 2, D], F32, internal=True)

    uid = [0]
    def sb(name, shape, dt, stack=es):
        uid[0] += 1
        return stack.enter_context(nc.sbuf_tensor(f"s{uid[0]}_" + name, list(shape), dt))

    def ps(name, shape, dt, stack):
        uid[0] += 1
        return stack.enter_context(nc.psum_tensor(f"ps{uid[0]}_" + name, list(shape), dt))

    ident = sb("ident", [128, 128], BF16)
    shift = [sb(f"shift{l}", [128, D], F32) for l in range(2)]
    g1 = [sb(f"g1_{l}", [128, D], F32) for l in range(2)]
    gate = [sb(f"gate{l}", [128, D], F32) for l in range(2)]
    posf = sb("posf", [128, NT], F32)
    invf0 = sb("invf0", [128, 128], F32)
    invf1 = sb("invf1", [128, 32], F32)
    dq = sb("dq", [128, 4], F32)
    dk = sb("dk", [128, 4], F32)

    ccsem = {}

    def collective(src_ap, dst_ap, name, kind, toks, op=ALU.add):
        if name not in ccsem:
            ccsem[name] = [es.enter_context(nc.semaphore("cc_" + name)), 0]
            k.dsems["cc_" + name] = [ccsem[name][0], 0, "cc_" + name]
        k.wait("pool", toks)
        inst = pool.collective_compute(kind, op, replica_groups=PAIRS[:max(1, NCORES_RUN[0] // 2)], ins=[src_ap], outs=[dst_ap])
        inst.then_inc(ccsem[name][0])
        ccsem[name][1] += 1
        k.dsems["cc_" + name][1] = ccsem[name][1]
        return (ccsem[name][0], ccsem[name][1], "cc_" + name)

    for nm, src, dst in [("rwin_b", rwin_d, rwin_b), ("rwout_b", rwout_d, rwout_b), ("mwin_b", mwin_d, mwin_b),
                         ("muqn_b", muqn_d, muqn_b), ("muqr_b", muqr_d, muqr_b), ("mukvk_b", mukvk_d, mukvk_b),
                         ("mukvv_b", mukvv_d, mukvv_b), ("mwout_b", mwout_d, mwout_b)]:
        rows = src.shape[0]
        for i in range(rows // 128):
            tok = k.dma("pool", [], [(nm, i)], dst[i * 128:(i + 1) * 128, :], src[i * 128:(i + 1) * 128, :], "cast_" + nm, fence=False)
        k.lw[nm] = tok

    with ExitStack() as p0:
        cl = sb("cl", [128, 16], F32, p0)
        cact = sb("cact", [128, 16], F32, p0)
        cbc = sb("cbc", [128, 16, 128], F32, p0)
        posi = sb("posi", [128, NT], I32, p0)
        gbc = sb("gbc", [128, D], F32, p0)
        bbc = sb("bbc", [128, 3 * D], F32, p0)
        modrow = sb("modrow", [1, 6144], F32, p0)
        wbufs = [sb(f"adaw{i}", [128, 3072], F32, p0) for i in range(3)]
        mps = [ps(f"mps{j}", [128, 512], F32, p0) for j in range(6)]

        k.dma_batch("sp", [([], ["cl"], cl[:], c_d), ([], ["posi"], posi[:], pos_d), ([], ["ident"], ident[:], ident_d),
                           ([], ["invf0"], invf0[:], invf0_d.partition_broadcast(128)), ([], ["invf1"], invf1[:], invf1_d.partition_broadcast(128)),
                           ([], ["dq"], dq[:], dq_d), ([], ["dk"], dk[:], dk_d)], "cb0")
        k.do("act", ["cl"], ["cact"], lambda: act.activation(out=cact[:], in_=cl[:], func=AF.Silu))
        k.do("dve", ["posi"], ["posf"], lambda: dve.tensor_copy(out=posf[:], in_=posi[:]))
        k.do("dve", ["cact"], ["cbc"], lambda: dve.tensor_copy(out=cbc[:], in_=cact[:].unsqueeze(2).broadcast_to([128, 16, 128])))

        wi = 0
        for l in range(2):
            for kc in range(16):
                j = wi % 3
                wi += 1
                wb = wbufs[j]
                k.dma("sp", [], [("adaw", j)], wb[:], ada_w_d[l, kc * 128:(kc + 1) * 128, :], f"adaw{j}")
                k.group("pe", ["cbc", ("adaw", j)], ["mps"],
                        [(lambda jj=jj, kc=kc, wb=wb: pe.matmul(mps[jj][:], lhsT=cbc[:, kc, :], rhs=wb[:, jj * 512:(jj + 1) * 512],
                                                         start=(kc == 0), stop=(kc == 15))) for jj in range(6)])
            for jj in range(6):
                k.do("dve", ["mps"], [("modrow", l, jj)], lambda: dve.tensor_copy(out=modrow[0:1, l * 3072 + jj * 512:l * 3072 + (jj + 1) * 512], in_=mps[jj][0:1, :]))
        tok = k.dma("sp", [("modrow", l, jj) for l in range(2) for jj in range(6)], [], modloc_d, modrow[0:1, :], "cb1")
        tok = collective(modloc_d, modall_d, "ag", "AllGather", [tok], op=ALU.bypass)
        k.wait("sp", tok)
        for l in range(2):
            o = l * 3072
            k.dma_batch("sp", [([], [f"shift{l}"], shift[l][:], modall_d[0, o:o + 2048].partition_broadcast(128)),
                               ([], [(f"g1_{l}", 0)], g1[l][:, 0:1024], modall_d[0, o + 2048:o + 3072].partition_broadcast(128)),
                               ([], [(f"g1_{l}", 1)], g1[l][:, 1024:2048], modall_d[1, o:o + 1024].partition_broadcast(128)),
                               ([], [f"gate{l}"], gate[l][:], modall_d[1, o + 1024:o + 3072].partition_broadcast(128)),
                               ([], ["gbc"], gbc[:], norm_g_d[l].partition_broadcast(128)),
                               ([], ["bbc"], bbc[:], ada_b_d[l].partition_broadcast(128))], "cb1")
            k.do("dve", [f"shift{l}", "bbc"], [f"shift{l}"], lambda: dve.tensor_tensor(out=shift[l][:], in0=shift[l][:], in1=bbc[:, 0:D], op=ALU.add))
            k.do("dve", [(f"g1_{l}", 0), (f"g1_{l}", 1), "bbc"], [f"g1_{l}"], lambda: dve.tensor_tensor(out=g1[l][:], in0=g1[l][:], in1=bbc[:, D:2 * D], op=ALU.add))
            k.do("dve", [f"g1_{l}", "gbc"], [f"g1_{l}"], lambda: dve.scalar_tensor_tensor(out=g1[l][:], in0=g1[l][:], scalar=1.0, in1=gbc[:], op0=ALU.add, op1=ALU.mult))
            k.do("dve", [f"gate{l}", "bbc"], [f"gate{l}"], lambda: dve.tensor_tensor(out=gate[l][:], in0=gate[l][:], in1=bbc[:, 2 * D:3 * D], op=ALU.add))

    k.barrier()

    def rope_tables(tmps, t, invf, ikey, nf, cos_out, sin_out, okey):
        ang, kf, r, m, ki = [x[:, :nf] for x in tmps]
        k.do("dve", [ikey, "posf"], ["rt_ang"], lambda: dve.tensor_scalar(out=ang, in0=invf[:, :nf], scalar1=posf[:, t:t + 1], scalar2=None, op0=ALU.mult))
        k.do("dve", ["rt_ang"], ["rt_ki"], lambda: dve.tensor_scalar(out=ki, in0=ang, scalar1=float(1.0 / TWO_PI), scalar2=None, op0=ALU.mult))
        k.do("dve", ["rt_ki"], ["rt_kf"], lambda: dve.tensor_copy(out=kf, in_=ki))
        k.do("dve", ["rt_kf", "rt_ang"], ["rt_r"], lambda: dve.scalar_tensor_tensor(out=r, in0=kf, scalar=-CW1, in1=ang, op0=ALU.mult, op1=ALU.add))
        k.do("dve", ["rt_kf", "rt_r"], ["rt_r"], lambda: dve.scalar_tensor_tensor(out=r, in0=kf, scalar=-CW2, in1=r, op0=ALU.mult, op1=ALU.add))

        def fold(dst, dkey, src, skey, add):
            if add != 0.0:
                k.do("dve", [skey], [dkey], lambda: dve.tensor_scalar(out=dst, in0=src, scalar1=float(add), scalar2=None, op0=ALU.add))
                src, skey = dst, dkey
            k.do("dve", [skey], ["rt_m"], lambda: dve.tensor_scalar(out=m, in0=src, scalar1=float(np.pi), scalar2=float(-TWO_PI), op0=ALU.is_gt, op1=ALU.mult))
            k.do("dve", [skey, "rt_m"], [dkey], lambda: dve.tensor_tensor(out=dst, in0=src, in1=m, op=ALU.add))
            k.do("dve", [dkey], ["rt_m"], lambda: dve.tensor_scalar(out=m, in0=dst, scalar1=float(-np.pi), scalar2=float(TWO_PI), op0=ALU.is_lt, op1=ALU.mult))
            k.do("dve", [dkey, "rt_m"], [dkey], lambda: dve.tensor_tensor(out=dst, in0=dst, in1=m, op=ALU.add))
            k.do("dve", [dkey], [dkey], lambda: dve.tensor_scalar(out=dst, in0=dst, scalar1=float(-np.pi), scalar2=float(np.pi), op0=ALU.max, op1=ALU.min))

        fold(r, "rt_r", r, "rt_r", 0.0)
        fold(ang, "rt_ang", r, "rt_r", float(np.pi / 2))
        k.do("act", ["rt_r"], [(okey, "sin")], lambda: act.activation(out=sin_out, in_=r, func=AF.Sin))
        k.do("act", ["rt_ang"], [(okey, "cos")], lambda: act.activation(out=cos_out, in_=ang, func=AF.Sin))

    def prologue_a(l, xt, xkey, junk, ssq, rstd, hb):
        k.do("act", [xkey], ["junk", "ssq"], lambda: act.activation(out=junk[:], in_=xt[:], func=AF.Square, accum_out=ssq[:]))
        k.do("act", ["ssq"], ["rstd"], lambda: act.activation(out=rstd[:], in_=ssq[:], func=AF.Sqrt, scale=float(1.0 / D), bias=float(EPS)))
        k.do("dve", ["rstd"], ["rstd"], lambda: dve.reciprocal(out=rstd[:], in_=rstd[:]))
        k.do("dve", [xkey, "rstd", f"g1_{l}"], [xkey], lambda: dve.scalar_tensor_tensor(out=xt[:], in0=xt[:], scalar=rstd[:, 0:1], in1=g1[l][:], op0=ALU.mult, op1=ALU.mult))
        k.do("dve", [xkey, f"shift{l}"], ["hb"], lambda: dve.tensor_tensor(out=hb[:], in0=xt[:], in1=shift[l][:], op=ALU.add))

    def prologue_b(hb, pT, hT, tl, hk="hT"):
        k.group("pe", ["hb", "ident"], ["pT"], [(lambda kc=kc: pe.transpose(pT[:, kc * 128:(kc + 1) * 128], hb[:, kc * 128:(kc + 1) * 128], ident[:])) for kc in range(16)])
        for hf in range(2):
            k.do("act", ["pT"], [(hk, tl)], lambda hf=hf: act.copy(out=hT[:, hf * 8:(hf + 1) * 8, tl * 128:(tl + 1) * 128],
                                                                in_=pT[:, hf * 1024:(hf + 1) * 1024].rearrange("p (a b) -> p a b", a=8)))

    def prologue_tile(l, xt, xkey, junk, ssq, rstd, hb, pT, hT, tl, hk="hT"):
        prologue_a(l, xt, xkey, junk, ssq, rstd, hb)
        prologue_b(hb, pT, hT, tl, hk)

    if upto >= 1:
        with ExitStack() as pa:
            xts = [sb(f"xt{i}", [128, D], F32, pa) for i in range(2)]
            junk = sb("junk", [128, D], BF16, pa)
            hb = sb("hb", [128, D], BF16, pa)
            ssq = sb("ssq", [128, 1], F32, pa)
            rstd = sb("rstd", [128, 1], F32, pa)
            hTs = [sb(f"hT{i}", [128, 16, GTOK], BF16, pa) for i in range(2)]
            wblk = [sb(f"wblk{i}", [128, 16, 512], BF16, pa) for i in range(2)]
            stg = [sb(f"stg{i}", [128, GT, 512], BF16, pa) for i in range(2)]
            cosbs = [sb(f"cosb{i}", [128, GT, 128], F32, pa) for i in range(2)]
            sinbs = [sb(f"sinb{i}", [128, GT, 128], F32, pa) for i in range(2)]
            rtmp = [sb(f"rt{i}", [128, 128], F32, pa) for i in range(4)] + [sb("rti", [128, 128], I32, pa)]
            tt = [sb(f"tt{i}", [128, 128], F32, pa) for i in range(4)]
            gnb = sb("gnb", [128, 2048], F32, pa)
            pT = ps("pT", [128, D], BF16, pa)
            mmps = [ps(f"mm{i}", [128, 512], F32, pa) for i in range(4)]

            k.dma("sp", [], ["gnb"], gnb[:], rgn_d.partition_broadcast(128), "cb0")
            xi = wi = si = mi = 0

            def pro_a(g, tl):
                nonlocal xi
                t = g * GT + tl
                j = xi % 2
                xi += 1
                k.dma("sp", ["x"], [("xt", j)], xts[j][:], x_d[t * 128:(t + 1) * 128, :], f"xt{j}")
                prologue_a(0, xts[j], ("xt", j), junk, ssq, rstd, hb)
                rope_tables(rtmp, t, invf0, "invf0", 128, cosbs[g % 2][:, tl, :], sinbs[g % 2][:, tl, :], ("tab%d" % (g % 2), tl))

            def pro_b(g, tl):
                prologue_b(hb, pT, hTs[g % 2], tl, "hT%d" % (g % 2))

            def prologue_group(g):
                for tl in range(GT):
                    pro_a(g, tl)
                    pro_b(g, tl)

            def load_w(idx):
                cb_ = idx % 12
                j_ = idx % 2
                k.dma("sp", ["rwin_b"], [("wblk", j_)], wblk[j_][:], rwin_b[:, cb_ * 512:(cb_ + 1) * 512].rearrange("(kc p) c -> p kc c", p=128), f"wblk{j_}")

            prologue_group(0)
            load_w(0)
            for g in range(NG):
                hT = hTs[g % 2]
                hk = "hT%d" % (g % 2)
                cosb, sinb = cosbs[g % 2], sinbs[g % 2]
                tabk = "tab%d" % (g % 2)
                for cb in range(12):
                    j = wi % 2
                    wi += 1
                    wb = wblk[j]
                    if wi < NG * 12:
                        load_w(wi)
                    if g + 1 < NG and GT == 4:
                        if 5 <= cb <= 8:
                            pro_b(g + 1, cb - 5)
                        if 4 <= cb <= 7:
                            pro_a(g + 1, cb - 4)
                    elif g + 1 < NG and cb == 6:
                        prologue_group(g + 1)
                    sj = si % 2
                    si += 1
                    st = stg[sj]
                    for tl in range(GT):
                        mj = mi % 4
                        mi += 1
                        mp = mmps[mj]
                        k.group("pe", [("wblk", j), (hk, tl)], [("mm", mj)],
                                [(lambda kc=kc, mp=mp, tl=tl, wb=wb: pe.matmul(mp[:], lhsT=hT[:, kc, tl * 128:(tl + 1) * 128], rhs=wb[:, kc, :],
                                                                          start=(kc == 0), stop=(kc == 15))) for kc in range(16)])
                        skey = ("stg", sj, tl)
                        if cb < 4:
                            dtab, dkey = (dq, "dq") if cb < 2 else (dk, "dk")
                            for hh in range(2):
                                h = (cb % 2) * 2 + hh
                                x1 = mp[:, hh * 256:hh * 256 + 128]
                                x2 = mp[:, hh * 256 + 128:hh * 256 + 256]
                                dsc = dtab[:, h:h + 1]
                                cs, sn = cosb[:, tl, :], sinb[:, tl, :]
                                RK = [("mm", mj), dkey, ((tabk, tl), "cos"), ((tabk, tl), "sin")]
                                k.do("dve", RK, ["tt0"], lambda: dve.scalar_tensor_tensor(out=tt[0][:], in0=x1, scalar=dsc, in1=cs, op0=ALU.mult, op1=ALU.mult))
                                k.do("dve", RK, ["tt1"], lambda: dve.scalar_tensor_tensor(out=tt[1][:], in0=x2, scalar=dsc, in1=sn, op0=ALU.mult, op1=ALU.mult))
                                k.do("dve", RK, ["tt2"], lambda: dve.scalar_tensor_tensor(out=tt[2][:], in0=x2, scalar=dsc, in1=cs, op0=ALU.mult, op1=ALU.mult))
                                k.do("dve", RK, ["tt3"], lambda: dve.scalar_tensor_tensor(out=tt[3][:], in0=x1, scalar=dsc, in1=sn, op0=ALU.mult, op1=ALU.mult))
                                k.do("pool", ["tt0", "tt1"], [skey + (hh, 0)], lambda: pool.tensor_tensor(out=st[:, tl, hh * 256:hh * 256 + 128], in0=tt[0][:], in1=tt[1][:], op=ALU.subtract))
                                k.do("pool", ["tt2", "tt3"], [skey + (hh, 1)], lambda: pool.tensor_tensor(out=st[:, tl, hh * 256 + 128:hh * 256 + 256], in0=tt[2][:], in1=tt[3][:], op=ALU.add))
                        elif cb < 8:
                            sk4 = [skey + (hh, q) for hh in range(2) for q in range(2)]
                            k.do("act", [("mm", mj)], sk4, lambda: act.copy(out=st[:, tl, :], in_=mp[:]))
                        else:
                            sk4 = [skey + (hh, q) for hh in range(2) for q in range(2)]
                            k.do("act", [("mm", mj)], sk4, lambda: act.activation(out=st[:, tl, :], in_=mp[:], func=AF.Silu))
                            k.do("pool", sk4 + ["gnb"], sk4, lambda: pool.tensor_tensor(out=st[:, tl, :], in0=st[:, tl, :], in1=gnb[:, (cb - 8) * 512:(cb - 7) * 512], op=ALU.mult))
                    if cb < 2:
                        dst, dname = qs_d[g * GTOK:(g + 1) * GTOK, cb * 512:(cb + 1) * 512], "qs"
                    elif cb < 4:
                        dst, dname = ks_d[g * GTOK:(g + 1) * GTOK, (cb - 2) * 512:(cb - 1) * 512], "ks"
                    elif cb < 8:
                        dst, dname = vs_d[g * GTOK:(g + 1) * GTOK, (cb - 4) * 512:(cb - 3) * 512], "vs"
                    else:
                        dst, dname = gs_d[g * GTOK:(g + 1) * GTOK, (cb - 8) * 512:(cb - 7) * 512], "gs"
                    rkeys = [("stg", sj, tl, hh, q) for tl in range(GT) for hh in range(2) for q in range(2)]
                    tok = k.dma("sp", rkeys, [], dst.rearrange("(t p) c -> p t c", p=128), st[:], f"stg{sj}")
                    k.stores.append(tok)

            k.barrier()

    if upto >= 2:
        with ExitStack() as pb:
            qin = [sb(f"qin{i}", [128, 1024], BF16, pb) for i in range(2)]
            kin = [sb(f"kin{i}", [128, 1024], BF16, pb) for i in range(2)]
            vin = [sb(f"vin{i}", [128, 2048], BF16, pb) for i in range(2)]
            gin = [sb(f"gin{i}", [128, 2048], BF16, pb) for i in range(2)]
            qTs = [sb(f"qTs{i}", [128, 8, 128], BF16, pb) for i in range(2)]
            kTs = [sb(f"kTs{i}", [128, 8, 128], BF16, pb) for i in range(2)]
            sTm = [sb(f"sTm{i}", [128, 4, 128], BF16, pb) for i in range(2)]
            cmask = sb("cmask", [128, 128], BF16, pb)
            g128 = sb("g128", [128, 4], F32, pb)
            state = sb("state", [128, 8, 512], F32, pb)
            sbf = [sb(f"sbf{i}", [128, 8, 512], BF16, pb) for i in range(2)]
            sgt = [sb(f"sgt{i}", [128, 512], F32, pb) for i in range(2)]
            st6 = sb("st6", [128, 6], F32, pb)
            mv = sb("mv", [128, 2], F32, pb)
            rs = sb("rs", [128, 1], F32, pb)
            yn = [sb(f"yn{i}", [128, 512], F32, pb) for i in range(2)]
            zst = [sb(f"zst{i}", [128, 2048], BF16, pb) for i in range(2)]
            pq = ps("pq", [128, 1024], BF16, pb)
            pk = ps("pk", [128, 1024], BF16, pb)
            psT = ps("psT", [128, 512], F32, pb)
            po = [ps(f"po{i}", [128, 512], F32, pb) for i in range(2)]
            pU = [ps(f"pU{i}", [128, 512], F32, pb) for i in range(3)]

            k.dma_batch("sp", [([], ["cmask"], cmask[:], cmask_d), ([], ["g128"], g128[:], g128_d)], "cb0")
            k.do("dve", [], [("state", i) for i in range(8)], lambda: dve.memset(state[:], 0.0))
            k.do("pool", [], [("sbf", 0, i) for i in range(8)], lambda: pool.memset(sbf[0][:], 0.0))

            def loads(c):
                j = c % 2
                r0 = c * 128
                k.dma("sp", [], [("qin", j)], qin[j][:], qs_d[r0:r0 + 128, :], f"qin{j}")
                k.dma("sp", [], [("kin", j)], kin[j][:], ks_d[r0:r0 + 128, :], f"kin{j}")
                k.dma("sp", [], [("vin", j)], vin[j][:], vs_d[r0:r0 + 128, :], f"vin{j}")
                k.dma("sp", [], [("gin", j)], gin[j][:], gs_d[r0:r0 + 128, :], f"gin{j}")

            def stage1(c):
                j = c % 2
                k.group("pe", [("qin", j), "ident"], ["pq"], [(lambda i=i: pe.transpose(pq[:, i * 128:(i + 1) * 128], qin[j][:, i * 128:(i + 1) * 128], ident[:])) for i in range(8)])
                k.group("pe", [("kin", j), "ident"], ["pk"], [(lambda i=i: pe.transpose(pk[:, i * 128:(i + 1) * 128], kin[j][:, i * 128:(i + 1) * 128], ident[:])) for i in range(8)])
                k.do("act", ["pq"], [("qTs", j)], lambda: act.copy(out=qTs[j][:], in_=pq[:].rearrange("p (a b) -> p a b", a=8)))
                k.do("act", ["pk"], [("kTs", j)], lambda: act.copy(out=kTs[j][:], in_=pk[:].rearrange("p (a b) -> p a b", a=8)))
                fns = []
                for h in range(4):
                    for dc in range(2):
                        fns.append(lambda h=h, dc=dc: pe.matmul(psT[:, h * 128:(h + 1) * 128], lhsT=kTs[j][:, h * 2 + dc, :], rhs=qTs[j][:, h * 2 + dc, :],
                                                               start=(dc == 0), stop=(dc == 1)))
                k.group("pe", [("qTs", j), ("kTs", j)], ["psT"], fns)
                k.do("dve", ["psT", "cmask"], [("sTm", j)], lambda: dve.tensor_tensor(out=sTm[j][:], in0=psT[:].rearrange("p (a b) -> p a b", a=4),
                                                                                  in1=cmask[:].unsqueeze(1).broadcast_to([128, 4, 128]), op=ALU.mult))

            cnt = {"u": 0, "o": 0}

            def stage2(c):
                j = c % 2
                ver = c % 2
                for h in range(4):
                    vsl = vin[j][:, h * 512:(h + 1) * 512]
                    ujs = []
                    for dc in range(2):
                        uj = cnt["u"] % 3
                        cnt["u"] += 1
                        ujs.append(uj)
                        k.do("pe", [("kin", j), ("vin", j)], [("pU", uj)],
                             lambda uj=uj, dc=dc: pe.matmul(pU[uj][:], lhsT=kin[j][:, h * 256 + dc * 128:h * 256 + dc * 128 + 128], rhs=vsl, start=True, stop=True))
                    oj = cnt["o"] % 2
                    cnt["o"] += 1
                    k.group("pe", [("sTm", j), ("vin", j), ("qTs", j), ("sbf", ver, h * 2), ("sbf", ver, h * 2 + 1)], [("po", oj)], [
                        lambda: pe.matmul(po[oj][:], lhsT=sTm[j][:, h, :], rhs=vsl, start=True, stop=False),
                        lambda: pe.matmul(po[oj][:], lhsT=qTs[j][:, h * 2, :], rhs=sbf[ver][:, h * 2, :], start=False, stop=False),
                        lambda: pe.matmul(po[oj][:], lhsT=qTs[j][:, h * 2 + 1, :], rhs=sbf[ver][:, h * 2 + 1, :], start=False, stop=True)])
                    k.do("dve", [("po", oj)], ["st6"], lambda: dve.bn_stats(out=st6[:], in_=po[oj][:]))
                    k.do("dve", ["st6"], ["mv"], lambda: dve.bn_aggr(out=mv[:], in_=st6[:]))
                    k.do("act", ["mv"], ["rs"], lambda: act.activation(out=rs[:], in_=mv[:, 1:2], func=AF.Sqrt, bias=float(EPS)))
                    k.do("dve", ["rs"], ["rs"], lambda: dve.reciprocal(out=rs[:], in_=rs[:]))
                    yj = oj
                    k.do("dve", [("po", oj), "mv", "rs"], [("yn", yj)], lambda: dve.tensor_scalar(out=yn[yj][:], in0=po[oj][:], scalar1=mv[:, 0:1], scalar2=rs[:, 0:1],
                                                                                              op0=ALU.subtract, op1=ALU.mult))
                    k.do("pool", [("yn", yj), ("gin", j)], [("zst", j, h)], lambda: pool.tensor_tensor(out=zst[j][:, h * 512:(h + 1) * 512], in0=yn[yj][:],
                                                                                                 in1=gin[j][:, h * 512:(h + 1) * 512], op=ALU.mult))
                    for dc in range(2):
                        i = h * 2 + dc
                        uj = ujs[dc]
                        k.do("act", [("state", i), "g128"], [("sgt", dc)], lambda: act.activation(out=sgt[dc][:], in_=state[:, i, :], func=AF.Copy, scale=g128[:, h:h + 1]))
                        k.do("dve", [("pU", uj), ("sgt", dc), "g128"], [("state", i)],
                             lambda: dve.scalar_tensor_tensor(out=state[:, i, :], in0=pU[uj][:], scalar=g128[:, h:h + 1], in1=sgt[dc][:], op0=ALU.mult, op1=ALU.add))
                        k.do("act", [("state", i)], [("sbf", 1 - ver, i)], lambda: act.copy(out=sbf[1 - ver][:, i, :], in_=state[:, i, :]))
                tok = k.dma("sp", [("zst", j, h) for h in range(4)], [], zs_d[c * 128:(c + 1) * 128, :], zst[j][:], f"zst{j}")
                k.stores.append(tok)

            loads(0)
            if NT > 1:
                loads(1)
            stage1(0)
            for c in range(NT):
                if c + 1 < NT:
                    stage1(c + 1)
                stage2(c)
                if c + 2 < NT:
                    loads(c + 2)
            k.barrier()

    ar0_toks = []
    rs1_toks = []
    if upto >= 3:
        with ExitStack() as pc:
            wout = sb("wout", [128, 16, D], BF16, pc)
            zin = [sb(f"zin{i}", [128, 2048], BF16, pc) for i in range(2)]
            zT = [sb(f"zT{i}", [128, 16, 128], BF16, pc) for i in range(2)]
            ost = [sb(f"ost{i}", [128, D], F32, pc) for i in range(2)]
            xc = [sb(f"xc{i}", [128, D], F32, pc) for i in range(2)]
            pT = ps("pT3", [128, D], BF16, pc)
            pout = [ps(f"pout{i}", [128, 512], F32, pc) for i in range(4)]
            for q4 in range(4):
                k.dma("sp", ["rwout_b"], [("wout", q4)], wout[:, q4 * 4:(q4 + 1) * 4, :],
                      rwout_b[q4 * 512:(q4 + 1) * 512, :].rearrange("(kc p) c -> p kc c", p=128), "cb1", fence=(q4 == 0))
            k.lw[("wout", 0)] = k.lw[("wout", 1)] = k.lw[("wout", 2)] = k.lw[("wout", 3)]
            ctoks = []

            def c0_loads(t):
                j_ = t % 2
                k.dma("sp", [], [("zin", j_)], zin[j_][:], zs_d[t * 128:(t + 1) * 128, :], f"zin{j_}")
                k.dma("sp", ["x"], [("xc", j_)], xc[j_][:], x_d[t * 128:(t + 1) * 128, :], f"xc{j_}")

            c0_loads(0)
            for t in range(NT):
                j = t % 2
                if t + 1 < NT:
                    c0_loads(t + 1)
                k.group("pe", [("zin", j), "ident"], ["pT3"], [(lambda kc=kc: pe.transpose(pT[:, kc * 128:(kc + 1) * 128], zin[j][:, kc * 128:(kc + 1) * 128], ident[:])) for kc in range(16)])
                for hf in range(2):
                    k.do("act", ["pT3"], [("zT", j, hf)], lambda: act.copy(out=zT[j][:, hf * 8:(hf + 1) * 8, :], in_=pT[:, hf * 1024:(hf + 1) * 1024].rearrange("p (a b) -> p a b", a=8)))
                for cb in range(4):
                    k.group("pe", [("zT", j, 0), ("zT", j, 1)] + [("wout", q4) for q4 in range(4)], [("pout", cb)],
                            [(lambda kc=kc: pe.matmul(pout[cb][:], lhsT=zT[j][:, kc, :], rhs=wout[:, kc, cb * 512:(cb + 1) * 512], start=(kc == 0), stop=(kc == 15))) for kc in range(16)])
                    csl = slice(cb * 512, (cb + 1) * 512)
                    k.do("dve", [("pout", cb), "gate0"], [("ost", j, cb)], lambda: dve.tensor_tensor(out=ost[j][:, csl], in0=pout[cb][:], in1=gate[0][:, csl], op=ALU.mult))
                    k.do("dve", [("ost", j, cb), ("xc", j)], [("ost", j, cb)], lambda: dve.scalar_tensor_tensor(out=ost[j][:, csl], in0=xc[j][:, csl], scalar=0.5, in1=ost[j][:, csl], op0=ALU.mult, op1=ALU.add))
                tok = k.dma("sp", [("ost", j, cb) for cb in range(4)], [], p0_d[t * 128:(t + 1) * 128, :], ost[j][:], f"ost{j}")
                k.stores.append(tok)
                ctoks.append(tok)
                if t % 4 == 3 and upto >= 4:
                    r0 = (t - 3) * 128
                    ar0_toks.append(collective(p0_d[r0:r0 + 512, :], s0_d[r0:r0 + 512, :], "ar0", "AllReduce", ctoks))
                    ctoks = []
            k.barrier(exclude=("cc_ar0",))

    QSC = float(192.0 ** -0.5)
    NQG = S // 512
    if upto >= 5:
        with ExitStack() as pa:
            xts = [sb(f"xt{i}", [128, D], F32, pa) for i in range(2)]
            junk = sb("junk", [128, D], BF16, pa)
            hb = sb("hb", [128, D], BF16, pa)
            ssq = sb("ssq", [128, 1], F32, pa)
            rstd = sb("rstd", [128, 1], F32, pa)
            hT = sb("hT", [128, 16, 512], BF16, pa)
            wblk = [sb(f"wblk{i}", [128, 16, 512], BF16, pa) for i in range(2)]
            wkr = sb("wkr", [128, 16, 64], BF16, pa)
            uqn = sb("uqn", [128, 4, 1024], BF16, pa)
            uqr = sb("uqr", [128, 4, 512], BF16, pa)
            ukvk = sb("ukvk", [128, 4, 1024], BF16, pa)
            ukvv = sb("ukvv", [128, 4, 1024], BF16, pa)
            qng = sb("qng", [128, 512], F32, pa)
            kvng = sb("kvng", [128, 512], F32, pa)
            cn = [sb(f"cn{i}", [128, 512], BF16, pa) for i in range(2)]
            cnT = [sb("cqnT", [128, 4, 512], BF16, pa), sb("ckvnT", [128, 4, 512], BF16, pa)]
            cos1 = sb("cos1", [128, 4, 32], F32, pa)
            sin1 = sb("sin1", [128, 4, 32], F32, pa)
            cosq = sb("cosq", [128, 4, 32], F32, pa)
            sinq = sb("sinq", [128, 4, 32], F32, pa)
            rtmp = [sb(f"rt{i}", [128, 32], F32, pa) for i in range(4)] + [sb("rti", [128, 32], I32, pa)]
            tk = [sb(f"tk{i}", [128, 32], F32, pa) for i in range(4)]
            tq = [sb(f"tq{i}", [128, 8, 32], F32, pa) for i in range(4)]
            kr = sb("kr", [128, 64], BF16, pa)
            qr = sb("qr", [128, 8, 64], BF16, pa)
            ktrst = [sb(f"ktrst{i}", [64, 512], BF16, pa) for i in range(2)]
            qtrst = [sb(f"qtrst{i}", [64, 8, 512], BF16, pa) for i in range(1)]
            fst = [sb(f"fst{i}", [128, 512], BF16, pa) for i in range(3)]
            vst = [sb(f"vst{i}", [128, 4, 1024], BF16, pa) for i in range(1)]
            pT = ps("pT5", [128, D], BF16, pa)
            mmps = [ps(f"mm5_{i}", [128, 512], F32, pa) for i in range(4)]
            pt2 = ps("pt2", [128, 1024], BF16, pa)

            k.dma_batch("sp", [(["mwin_b"], ["wkr"], wkr[:], mwin_b[:, 1024:1088].rearrange("(kc p) c -> p kc c", p=128)),
                               (["muqn_b"], ["uqn"], uqn[:], muqn_b.rearrange("(kc p) c -> p kc c", p=128)),
                               (["muqr_b"], ["uqr"], uqr[:], muqr_b.rearrange("(kc p) c -> p kc c", p=128)),
                               (["mukvk_b"], ["ukvk"], ukvk[:], mukvk_b.rearrange("(kc p) c -> p kc c", p=128)),
                               (["mukvv_b"], ["ukvv"], ukvv[:], mukvv_b.rearrange("(kc p) c -> p kc c", p=128)),
                               ([], ["qng"], qng[:], mqn_d.partition_broadcast(128)),
                               ([], ["kvng"], kvng[:], mkvn_d.partition_broadcast(128))], "cb0")
            xi = wi = mi = fi = 0
            wcols = [0, 512, 1088, 1600]

            def load_w1(idx):
                c0 = wcols[idx % 4]
                j_ = idx % 2
                k.dma("sp", ["mwin_b"], [("wblk", j_)], wblk[j_][:], mwin_b[:, c0:c0 + 512].rearrange("(kc p) c -> p kc c", p=128), f"wblk{j_}")

            load_w1(0)
            for g in range(NQG):
                tsl = slice(g * 512, (g + 1) * 512)
                for tl in range(4):
                    t = g * 4 + tl
                    j = xi % 2
                    xi += 1
                    k.wait("sp", ar0_toks[g])
                    k.dma("sp", [], [("xt", j)], xts[j][:], s0_d[t * 128:(t + 1) * 128, :], f"xt{j}")
                    prologue_tile(1, xts[j], ("xt", j), junk, ssq, rstd, hb, pT, hT, tl)
                    rope_tables(rtmp, t, invf1, "invf1", 32, cos1[:, tl, :], sin1[:, tl, :], ("tab1", tl))
                    k.do("dve", [(("tab1", tl), "cos")], [("cosq", tl)], lambda: dve.tensor_scalar(out=cosq[:, tl, :], in0=cos1[:, tl, :], scalar1=QSC, scalar2=None, op0=ALU.mult))
                    k.do("dve", [(("tab1", tl), "sin")], [("sinq", tl)], lambda: dve.tensor_scalar(out=sinq[:, tl, :], in0=sin1[:, tl, :], scalar1=QSC, scalar2=None, op0=ALU.mult))

                for blk in range(2):
                    j = wi % 2
                    wi += 1
                    wb = wblk[j]
                    if wi < NQG * 4:
                        load_w1(wi)
                    gn_t, gn_k = (qng, "qng") if blk == 0 else (kvng, "kvng")
                    for tl in range(4):
                        mj = mi % 4
                        mi += 1
                        mp = mmps[mj]
                        k.group("pe", [("wblk", j), ("hT", tl)], [("mm", mj)],
                                [(lambda kc=kc: pe.matmul(mp[:], lhsT=hT[:, kc, tl * 128:(tl + 1) * 128], rhs=wb[:, kc, :], start=(kc == 0), stop=(kc == 15))) for kc in range(16)])
                        k.do("act", [("mm", mj)], ["junk", "ssq"], lambda: act.activation(out=junk[:, 0:512], in_=mp[:], func=AF.Square, accum_out=ssq[:]))
                        k.do("act", ["ssq"], ["rstd"], lambda: act.activation(out=rstd[:], in_=ssq[:], func=AF.Sqrt, scale=float(1.0 / 512), bias=float(EPS)))
                        k.do("dve", ["rstd"], ["rstd"], lambda: dve.reciprocal(out=rstd[:], in_=rstd[:]))
                        cj = tl % 2
                        k.do("dve", [("mm", mj), "rstd", gn_k], [("cn", cj)], lambda: dve.scalar_tensor_tensor(out=cn[cj][:], in0=mp[:], scalar=rstd[:, 0:1], in1=gn_t[:], op0=ALU.mult, op1=ALU.mult))
                        k.group("pe", [("cn", cj), "ident"], ["pt2"], [(lambda rc=rc: pe.transpose(pt2[:, rc * 128:(rc + 1) * 128], cn[cj][:, rc * 128:(rc + 1) * 128], ident[:])) for rc in range(4)])
                        k.do("act", ["pt2"], [("cnT", blk, tl)], lambda: act.copy(out=cnT[blk][:, :, tl * 128:(tl + 1) * 128], in_=pt2[:, 0:512].rearrange("p (a b) -> p a b", a=4)))

                kj = g % 2
                for tl in range(4):
                    mj = mi % 4
                    mi += 1
                    mp = mmps[mj]
                    k.group("pe", ["wkr", ("hT", tl)], [("mm", mj)],
                            [(lambda kc=kc: pe.matmul(mp[:, 0:64], lhsT=hT[:, kc, tl * 128:(tl + 1) * 128], rhs=wkr[:, kc, :], start=(kc == 0), stop=(kc == 15))) for kc in range(16)])
                    x1, x2 = mp[:, 0:32], mp[:, 32:64]
                    RK = [("mm", mj), (("tab1", tl), "cos"), (("tab1", tl), "sin")]
                    k.do("dve", RK, ["tk0"], lambda: dve.tensor_tensor(out=tk[0][:], in0=x1, in1=cos1[:, tl, :], op=ALU.mult))
                    k.do("dve", RK, ["tk1"], lambda: dve.tensor_tensor(out=tk[1][:], in0=x2, in1=sin1[:, tl, :], op=ALU.mult))
                    k.do("dve", RK, ["tk2"], lambda: dve.tensor_tensor(out=tk[2][:], in0=x2, in1=cos1[:, tl, :], op=ALU.mult))
                    k.do("dve", RK, ["tk3"], lambda: dve.tensor_tensor(out=tk[3][:], in0=x1, in1=sin1[:, tl, :], op=ALU.mult))
                    k.do("pool", ["tk0", "tk1"], [("kr", 0)], lambda: pool.tensor_tensor(out=kr[:, 0:32], in0=tk[0][:], in1=tk[1][:], op=ALU.subtract))
                    k.do("pool", ["tk2", "tk3"], [("kr", 1)], lambda: pool.tensor_tensor(out=kr[:, 32:64], in0=tk[2][:], in1=tk[3][:], op=ALU.add))
                    k.do("pe", [("kr", 0), ("kr", 1), "ident"], ["pt2"], lambda: pe.transpose(pt2[0:64, 0:128], kr[:], ident[:]))
                    k.do("act", ["pt2"], [("ktrst", kj, tl)], lambda: act.copy(out=ktrst[kj][:, tl * 128:(tl + 1) * 128], in_=pt2[0:64, 0:128]))
                tok = k.dma("sp", [("ktrst", kj, tl) for tl in range(4)], [], ktr_d[:, tsl], ktrst[kj][:], f"ktrst{kj}")
                k.stores.append(tok)

                for blk in range(2):
                    j = wi % 2
                    wi += 1
                    wb = wblk[j]
                    if wi < NQG * 4:
                        load_w1(wi)
                    for cc in range(4):
                        mj = mi % 4
                        mi += 1
                        mp = mmps[mj]
                        k.group("pe", [("wblk", j)] + [("hT", tl) for tl in range(4)], [("mm", mj)],
                                [(lambda kc=kc: pe.matmul(mp[:], lhsT=wb[:, kc, cc * 128:(cc + 1) * 128], rhs=hT[:, kc, :], start=(kc == 0), stop=(kc == 15))) for kc in range(16)])
                        fj = fi % 3
                        fi += 1
                        k.do("act", [("mm", mj)], [("fst", fj)], lambda: act.activation(out=fst[fj][:], in_=mp[:], func=AF.Silu))
                        row = (blk * 4 + cc) * 128
                        tok = k.dma("sp", [("fst", fj)], [], gt_d[row:row + 128, tsl], fst[fj][:], f"fst{fj}")
                        k.stores.append(tok)

                for which in range(2):
                    wt, wk_, dst_d = [(uqn, "uqn", qtn_d), (ukvk, "ukvk", ktn_d)][which]
                    for hd in range(8):
                        mj = mi % 4
                        mi += 1
                        mp = mmps[mj]
                        k.group("pe", [wk_] + [("cnT", which, tl) for tl in range(4)], [("mm", mj)],
                                [(lambda rc=rc: pe.matmul(mp[:], lhsT=wt[:, rc, hd * 128:(hd + 1) * 128], rhs=cnT[which][:, rc, :], start=(rc == 0), stop=(rc == 3))) for rc in range(4)])
                        fj = fi % 3
                        fi += 1
                        if which == 0:
                            k.do("act", [("mm", mj)], [("fst", fj)], lambda: act.activation(out=fst[fj][:], in_=mp[:], func=AF.Copy, scale=QSC))
                        else:
                            k.do("dve", [("mm", mj)], [("fst", fj)], lambda: dve.tensor_copy(out=fst[fj][:], in_=mp[:]))
                        tok = k.dma("sp", [("fst", fj)], [], dst_d[hd, :, tsl], fst[fj][:], f"fst{fj}")
                        k.stores.append(tok)

                qj = 0
                for tl in range(4):
                    mj = mi % 4
                    mi += 1
                    mp = mmps[mj]
                    k.group("pe", ["uqr", ("cnT", 0, tl)], [("mm", mj)],
                            [(lambda rc=rc: pe.matmul(mp[:], lhsT=cnT[0][:, rc, tl * 128:(tl + 1) * 128], rhs=uqr[:, rc, :], start=(rc == 0), stop=(rc == 3))) for rc in range(4)])
                    mv4 = mp[:].rearrange("p (h t f) -> p h t f", h=8, t=2)
                    x1, x2 = mv4[:, :, 0, :], mv4[:, :, 1, :]
                    cb_ = cosq[:, tl, :].unsqueeze(1).broadcast_to([128, 8, 32])
                    sb_ = sinq[:, tl, :].unsqueeze(1).broadcast_to([128, 8, 32])
                    RK = [("mm", mj), ("cosq", tl), ("sinq", tl)]
                    k.do("dve", RK, ["tq0"], lambda: dve.tensor_tensor(out=tq[0][:], in0=x1, in1=cb_, op=ALU.mult))
                    k.do("dve", RK, ["tq1"], lambda: dve.tensor_tensor(out=tq[1][:], in0=x2, in1=sb_, op=ALU.mult))
                    k.do("dve", RK, ["tq2"], lambda: dve.tensor_tensor(out=tq[2][:], in0=x2, in1=cb_, op=ALU.mult))
                    k.do("dve", RK, ["tq3"], lambda: dve.tensor_tensor(out=tq[3][:], in0=x1, in1=sb_, op=ALU.mult))
                    k.do("pool", ["tq0", "tq1"], [("qr", 0)], lambda: pool.tensor_tensor(out=qr[:, :, 0:32], in0=tq[0][:], in1=tq[1][:], op=ALU.subtract))
                    k.do("pool", ["tq2", "tq3"], [("qr", 1)], lambda: pool.tensor_tensor(out=qr[:, :, 32:64], in0=tq[2][:], in1=tq[3][:], op=ALU.add))
                    k.group("pe", [("qr", 0), ("qr", 1), "ident"], ["pt2"], [(lambda hd=hd: pe.transpose(pt2[0:64, hd * 128:(hd + 1) * 128], qr[:, hd, :], ident[:])) for hd in range(8)])
                    k.do("act", ["pt2"], [("qtrst", qj, tl)], lambda: act.copy(out=qtrst[qj][:, :, tl * 128:(tl + 1) * 128], in_=pt2[0:64, :].rearrange("p (a b) -> p a b", a=8)))
                tok = k.dma("sp", [("qtrst", qj, tl) for tl in range(4)], [], qtr_d[:, :, tsl].rearrange("h d t -> d h t"), qtrst[qj][:], f"qtrst{qj}")
                k.stores.append(tok)

                vj = 0
                for tl in range(4):
                    for vb in range(2):
                        mj = mi % 4
                        mi += 1
                        mp = mmps[mj]
                        k.group("pe", ["ukvv", ("cnT", 1, tl)], [("mm", mj)],
                                [(lambda rc=rc: pe.matmul(mp[:], lhsT=cnT[1][:, rc, tl * 128:(tl + 1) * 128], rhs=ukvv[:, rc, vb * 512:(vb + 1) * 512], start=(rc == 0), stop=(rc == 3))) for rc in range(4)])
                        if vb == 0:
                            k.do("act", [("mm", mj)], [("vst", vj, tl, vb)], lambda: act.copy(out=vst[vj][:, tl, 0:512], in_=mp[:]))
                        else:
                            k.do("dve", [("mm", mj)], [("vst", vj, tl, vb)], lambda: dve.tensor_copy(out=vst[vj][:, tl, 512:1024], in_=mp[:]))
                for hd in range(8):
                    tok = k.dma("sp", [("vst", vj, tl, vb) for tl in range(4) for vb in range(2)], [],
                                v1_d[hd, :, g * 4:(g + 1) * 4, :], vst[vj][:, :, hd * 128:(hd + 1) * 128], f"vst{vj}", fence=(hd == 0))
                k.stores.append(tok)
            k.barrier()

    if upto >= 6:
        with ExitStack() as pb:
            ktn = [sb(f"ktn{i}", [128, S], BF16, pb) for i in range(2)]
            vh = [sb(f"vh{i}", [128, NT, 128], BF16, pb) for i in range(2)]
            ktr = sb("ktr", [128, S], BF16, pb)
            qn = [sb(f"qn{i}", [128, 512], BF16, pb) for i in range(2)]
            qrr = [sb(f"qrr{i}", [128, 512], BF16, pb) for i in range(2)]
            gtt = [sb(f"gtt{i}", [128, 512], BF16, pb) for i in range(2)]
            pTt = [sb(f"pTt{i}", [128, 512], BF16, pb) for i in range(3)]
            amask = sb("amask", [128, 4, 512], BF16, pb)
            ones = sb("ones", [128, 128], BF16, pb)
            rden = [sb(f"rden{i}", [128, 512], F32, pb) for i in range(2)]
            on = [sb(f"on{i}", [128, 512], F32, pb) for i in range(2)]
            zst = [sb(f"zst1_{i}", [128, 512], BF16, pb) for i in range(2)]
            pss = [ps(f"pss{i}", [128, 512], F32, pb) for i in range(3)]
            pso = [ps(f"pso{i}", [128, 512], F32, pb) for i in range(2)]
            psd = [ps(f"psd{i}", [128, 512], F32, pb) for i in range(2)]

            k.do("dve", [], ["ktr_pad"], lambda: dve.memset(ktr[64:128, :], 0.0))
            k.do("pool", [], [("qrr_pad", 0)], lambda: pool.memset(qrr[0][64:128, :], 0.0))
            k.do("pool", [], [("qrr_pad", 1)], lambda: pool.memset(qrr[1][64:128, :], 0.0))
            k.dma_batch("sp", [([], ["amask"], amask[:], amask_d), ([], ["ktr"], ktr[0:64, :], ktr_d)], "cb0")
            k.do("dve", [], ["ones"], lambda: dve.memset(ones[:], 1.0))

            def load_head(hd):
                hj = hd % 2
                k.dma("sp", [], [("ktn", hj)], ktn[hj][:], ktn_d[hd], f"ktn{hj}")
                k.dma("sp", [], [("vh", hj)], vh[hj][:], v1_d[hd], f"vh{hj}")

            load_head(0)
            si = pi = qi = 0
            for hd in range(8):
                hj = hd % 2
                if hd + 1 < 8:
                    load_head(hd + 1)
                for G in range(NQG):
                    qj = qi % 2
                    qi += 1
                    tsl = slice(G * 512, (G + 1) * 512)
                    k.dma("sp", [], [("qn", qj)], qn[qj][:], qtn_d[hd, :, tsl], f"qn{qj}")
                    k.dma("sp", [], [("qrr", qj)], qrr[qj][0:64, :], qtr_d[hd, :, tsl], f"qrr{qj}")
                    k.dma("sp", [], [("gtt", qj)], gtt[qj][:], gt_d[hd * 128:(hd + 1) * 128, tsl], f"gtt{qj}")
                    nkb = 4 * G + 4

                    def s_part(kb):
                        sj = kb % 3
                        ksl = slice(kb * 128, (kb + 1) * 128)
                        k.group("pe", [("ktn", hj), "ktr", "ktr_pad", ("qn", qj), ("qrr", qj), ("qrr_pad", qj)], [("pss", sj)], [
                            lambda: pe.matmul(pss[sj][:], lhsT=ktn[hj][:, ksl], rhs=qn[qj][:], start=True, stop=False),
                            lambda: pe.matmul(pss[sj][:], lhsT=ktr[:, ksl], rhs=qrr[qj][:], start=False, stop=True)])

                    def p_part(kb):
                        sj = kb % 3
                        pj = kb % 3
                        k.do("act", [("pss", sj)], [("pTt", pj)], lambda: act.activation(out=pTt[pj][:], in_=pss[sj][:], func=AF.Exp))
                        if kb >= 4 * G:
                            r = kb - 4 * G
                            k.do("pool", [("pTt", pj), "amask"], [("pTt", pj)], lambda: pool.tensor_tensor(out=pTt[pj][:], in0=pTt[pj][:], in1=amask[:, r, :], op=ALU.mult))
                        k.group("pe", [("vh", hj), ("pTt", pj), "ones"], [("pso", qj), ("psd", qj)], [
                            lambda: pe.matmul(pso[qj][:], lhsT=vh[hj][:, kb, :], rhs=pTt[pj][:], start=(kb == 0), stop=(kb == nkb - 1)),
                            lambda: pe.matmul(psd[qj][:], lhsT=ones[:], rhs=pTt[pj][:], start=(kb == 0), stop=(kb == nkb - 1))])

                    s_part(0)
                    if nkb > 1:
                        s_part(1)
                    for kb in range(nkb):
                        if kb + 2 < nkb:
                            k.wait("pe", k.lw.get(("pTt", kb % 3)))
                            s_part(kb + 2)
                        p_part(kb)
                    k.do("dve", [("psd", qj)], [("rden", qj)], lambda: dve.reciprocal(out=rden[qj][:], in_=psd[qj][:]))
                    k.do("dve", [("pso", qj), ("rden", qj)], [("on", qj)], lambda: dve.tensor_tensor(out=on[qj][:], in0=pso[qj][:], in1=rden[qj][:], op=ALU.mult))
                    k.do("pool", [("on", qj), ("gtt", qj)], [("zst1", qj)], lambda: pool.tensor_tensor(out=zst[qj][:], in0=on[qj][:], in1=gtt[qj][:], op=ALU.mult))
                    tok = k.dma("sp", [("zst1", qj)], [], zt_d[hd * 128:(hd + 1) * 128, tsl], zst[qj][:], f"zst1_{qj}")
                    k.stores.append(tok)
            k.barrier()

    if upto >= 7:
        with ExitStack() as pc:
            wout = sb("wout1", [128, 8, D], BF16, pc)
            zTg = [sb(f"zTg{i}", [128, 8, 512], BF16, pc) for i in range(2)]
            ost = [sb(f"ost{i}", [128, D], F32, pc) for i in range(2)]
            xc = [sb(f"xc{i}", [128, D], F32, pc) for i in range(2)]
            pout = [ps(f"pout1_{i}", [128, 512], F32, pc) for i in range(4)]
            for q4 in range(2):
                k.dma("sp", ["mwout_b"], [("wout1", q4)], wout[:, q4 * 4:(q4 + 1) * 4, :],
                      mwout_b[q4 * 512:(q4 + 1) * 512, :].rearrange("(kc p) c -> p kc c", p=128), "cb1", fence=(q4 == 0))
            k.lw[("wout1", 0)] = k.lw[("wout1", 1)]
            oi = 0
            for g in range(NQG):
                ctoks = []
                gj = g % 2
                k.dma("sp", [], [("zTg", gj)], zTg[gj][:], zt_d[:, g * 512:(g + 1) * 512].rearrange("(kc p) t -> p kc t", p=128), f"zTg{gj}")
                for tl in range(4):
                    t = g * 4 + tl
                    j = oi % 2
                    oi += 1
                    k.dma("sp", [], [("xc", j)], xc[j][:], s0_d[t * 128:(t + 1) * 128, :], f"xc{j}")
                    for cb in range(4):
                        k.group("pe", [("zTg", gj), ("wout1", 0), ("wout1", 1)], [("pout", cb)],
                                [(lambda kc=kc: pe.matmul(pout[cb][:], lhsT=zTg[gj][:, kc, tl * 128:(tl + 1) * 128], rhs=wout[:, kc, cb * 512:(cb + 1) * 512], start=(kc == 0), stop=(kc == 7))) for kc in range(8)])
                        csl = slice(cb * 512, (cb + 1) * 512)
                        k.do("dve", [("pout", cb), "gate1"], [("ost", j, cb)], lambda: dve.tensor_tensor(out=ost[j][:, csl], in0=pout[cb][:], in1=gate[1][:, csl], op=ALU.mult))
                        k.do("dve", [("ost", j, cb), ("xc", j)], [("ost", j, cb)], lambda: dve.scalar_tensor_tensor(out=ost[j][:, csl], in0=xc[j][:, csl], scalar=0.5, in1=ost[j][:, csl], op0=ALU.mult, op1=ALU.add))
                    tok = k.dma("sp", [("ost", j, cb) for cb in range(4)], [], p1_d[t * 128:(t + 1) * 128, :], ost[j][:], f"ost{j}")
                    k.stores.append(tok)
                    ctoks.append(tok)
                rs1_toks.append(collective(p1_d[g * 512:(g + 1) * 512, :], s1_d[g * 256:(g + 1) * 256, :], "rs1", "ReduceScatter", ctoks))
            k.barrier(exclude=("cc_rs1",))

    if upto >= 8:
        with ExitStack() as pf:
            xf = [sb(f"xf{i}", [128, D], F32, pf) for i in range(2)]
            junk = sb("junkf", [128, D], BF16, pf)
            ssq = sb("ssqf", [128, 1], F32, pf)
            rstd = sb("rstdf", [128, 1], F32, pf)
            fg = sb("fg", [128, D], F32, pf)
            k.dma("sp", [], ["fg"], fg[:], fng_d.partition_broadcast(128), "cb0")
            for t in range(NT // 2):
                j = t % 2
                k.wait("sp", rs1_toks[t // 2])
                k.dma("sp", [], [("xf", j)], xf[j][:], s1_d[t * 128:(t + 1) * 128, :], f"xf{j}")
                k.do("act", [("xf", j)], ["junkf", "ssqf"], lambda: act.activation(out=junk[:], in_=xf[j][:], func=AF.Square, accum_out=ssq[:]))
                k.do("act", ["ssqf"], ["rstdf"], lambda: act.activation(out=rstd[:], in_=ssq[:], func=AF.Sqrt, scale=float(1.0 / D), bias=float(EPS)))
                k.do("dve", ["rstdf"], ["rstdf"], lambda: dve.reciprocal(out=rstd[:], in_=rstd[:]))
                k.do("dve", [("xf", j), "rstdf", "fg"], [("xf", j)], lambda: dve.scalar_tensor_tensor(out=xf[j][:], in0=xf[j][:], scalar=rstd[:, 0:1], in1=fg[:], op0=ALU.mult, op1=ALU.mult))
                tok = k.dma("sp", [("xf", j)], [], out_d[t * 128:(t + 1) * 128, :], xf[j][:], f"xfo{j}")
                k.stores.append(tok)
            k.barrier()

    if dbg:
        for nm, src in ([("p0", p0_d), ("s0", s0_d)] if upto >= 4 else []) + ([("p1", p1_d), ("s1", s1_d)] if upto >= 7 else []):
            dd = nc.dram_tensor(nm, list(src.shape), F32, kind="ExternalOutput").ap()
            k.stores.append(k.dma("sp", [], [nm + "_dbg"], dd, src, "dbg_" + nm))

    k.wait("sp", *k.stores)
    for nm in ["rwin_b", "rwout_b", "mwin_b", "muqn_b", "muqr_b", "mukvk_b", "mukvv_b", "mwout_b"]:
        k.wait("pool", k.lw[nm])
    for e in ["pe", "act", "dve", "pool"]:
        if e in k.last_tok:
            k.wait("sp", k.last_tok[e])
    es.close()
    return nc


def host_inputs(S, x, c, positions, ada_w, ada_b, norm_g, ret_w_in, ret_gn_g, ret_w_out,
                mla_w_in, mla_q_norm_g, mla_w_uq, mla_kv_norm_g, mla_w_ukv, mla_w_out, final_norm_g):
    f32 = np.float32
    bf = ml_dtypes.bfloat16
    NT = S // 128
    ident = np.eye(128, dtype=f32).astype(bf)
    invf0 = (10000.0 ** (-np.arange(0, 256, 2, dtype=f32) / f32(256))).astype(f32)
    invf1 = (10000.0 ** (-np.arange(0, 64, 2, dtype=f32) / f32(64))).astype(f32)
    n = np.arange(128, dtype=np.float64)
    cm = (n[None, :] >= n[:, None]).astype(f32).astype(bf)
    am = np.zeros((128, 4, 512), f32)
    qq = np.arange(512)
    for r in range(4):
        am[:, r, :] = (128 * r + np.arange(128)[:, None] <= qq[None, :])
    am = am.astype(bf)
    maps = []
    for core in range(NCORES):
        b, hg = core // 2, core % 2
        hs = np.arange(hg * 4, hg * 4 + 4)
        lg = np.log1p(-np.exp2(-5.0 - hs.astype(np.float64)))
        dqv = np.exp(lg[None, :] * (n[:, None] + 1.0)).astype(f32)
        dkv = (np.exp(-lg[None, :] * (n[:, None] + 1.0)) / 16.0).astype(f32)
        w = ret_w_in[0]
        qc = np.concatenate([np.arange(h * 256, (h + 1) * 256) for h in hs])
        vc = np.concatenate([np.arange(h * 512, (h + 1) * 512) for h in hs])
        rw = np.concatenate([w[:, qc], w[:, 2048 + qc], w[:, 4096 + vc], w[:, 8192 + vc]], axis=1)
        mh = np.arange(hg * 8, hg * 8 + 8)
        mw = mla_w_in[0]
        gcol = 1088 + np.concatenate([np.arange(h * 128, (h + 1) * 128) for h in mh])
        mwin = np.concatenate([mw[:, :1088], mw[:, gcol]], axis=1)
        uq = mla_w_uq[0].reshape(512, 16, 192)[:, mh, :]
        ukv = mla_w_ukv[0].reshape(512, 16, 256)[:, mh, :]
        m = {
            "x": np.ascontiguousarray(x[b]),
            "c_l": np.ascontiguousarray(c[b].reshape(16, 128).T),
            "pos_l": np.ascontiguousarray(positions[b].reshape(NT, 128).T.astype(np.int32)),
            "ada_w": np.ascontiguousarray(ada_w[:, :, hg * 3072:(hg + 1) * 3072]), "ada_b": ada_b, "norm_g": norm_g,
            "ret_w_in": np.ascontiguousarray(rw),
            "ret_gn_g": np.ascontiguousarray(ret_gn_g[0][vc]),
            "ret_w_out": np.ascontiguousarray(ret_w_out[0][vc, :]),
            "mla_w_in": np.ascontiguousarray(mwin),
            "mla_q_norm_g": mla_q_norm_g[0],
            "mla_w_uq_nope": np.ascontiguousarray(uq[:, :, :128].reshape(512, 1024)),
            "mla_w_uq_rope": np.ascontiguousarray(uq[:, :, 128:].reshape(512, 512)),
            "mla_kv_norm_g": mla_kv_norm_g[0],
            "mla_w_ukv_k": np.ascontiguousarray(ukv[:, :, :128].reshape(512, 1024)),
            "mla_w_ukv_v": np.ascontiguousarray(ukv[:, :, 128:].reshape(512, 1024)),
            "mla_w_out": np.ascontiguousarray(mla_w_out[0][np.concatenate([np.arange(h * 128, (h + 1) * 128) for h in mh]), :]),
            "final_norm_g": final_norm_g,
            "ident": ident, "invf0": invf0, "invf1": invf1, "dq": dqv, "dk": dkv, "g128": np.ascontiguousarray(np.broadcast_to(np.exp(lg * 128.0).astype(f32)[None, :], (128, 4))),
            "cmask": cm, "amask": am,
        }
        maps.append(m)
    return maps


_CACHE = {}


def kernel(**inputs):
    inputs = {kk: np.asarray(v) for kk, v in inputs.items()}
    B, S, _ = inputs["x"].shape
    if S not in _CACHE:
        _CACHE[S] = build_program(S)
    nc = _CACHE[S]
    maps = host_inputs(S, **inputs)
    res = run_bass_kernel_spmd(nc, maps, core_ids=list(range(NCORES)))
    out = np.empty((B, S, D), np.float32)
    for core in range(NCORES):
        b, hg = core // 2, core % 2
        out[b].reshape(S // 512, 2, 256, D)[:, hg] = res.results[core]["out"].reshape(S // 512, 256, D)
    return out
```

```python
import numpy as np
import ml_dtypes
from contextlib import ExitStack
import concourse.bass as bass
import concourse.mybir as mybir
from concourse.bass_utils import run_bass_kernel_spmd

F32 = mybir.dt.float32
BF16 = mybir.dt.bfloat16
I32 = mybir.dt.int32
ALU = mybir.AluOpType
AF = mybir.ActivationFunctionType
AX = mybir.AxisListType

D = 2048
EPS = 1e-6
NCORES = 8
PAIRS = [[0, 1], [2, 3], [4, 5], [6, 7]]
NCORES_RUN = [8]
TWO_PI = 2.0 * np.pi
CW1 = 6.28125
CW2 = float(TWO_PI - 6.28125)


class Ctx:
    EPOCH = 30000

    def __init__(self, nc, es):
        self.nc = nc
        self.es = es
        self.eng = {"pe": nc.tensor, "act": nc.scalar, "dve": nc.vector, "pool": nc.gpsimd, "sp": nc.sync}
        self.psem = {}
        self.pcount = {}
        self.pname = {}
        self.pgen = {}
        self.last_tok = {}
        for e in ["pe", "act", "dve", "pool"]:
            self.pgen[e] = 0
            self._new_psem(e)
        self.waited = {}
        self.dsems = {}
        self.lw = {}
        self.rd = {}
        self.stores = []

    def _new_psem(self, e):
        name = f"p_{e}{self.pgen[e]}"
        self.pgen[e] += 1
        self.psem[e] = self.es.enter_context(self.nc.semaphore(name))
        self.pcount[e] = 0
        self.pname[e] = name

    def sig(self, e, inst):
        if self.pcount[e] >= self.EPOCH:
            self._new_psem(e)
        self.pcount[e] += 1
        inst.then_inc(self.psem[e], 1)
        tok = (self.psem[e], self.pcount[e], self.pname[e])
        self.last_tok[e] = tok
        return tok

    def wait(self, e, *toks):
        flat = []

        def rec(ts):
            for t in ts:
                if t is None:
                    continue
                if isinstance(t, list):
                    rec(t)
                else:
                    flat.append(t)
        rec(toks)
        best = {}
        for t in flat:
            if t[2] not in best or best[t[2]][1] < t[1]:
                best[t[2]] = t
        for sem, val, name in best.values():
            key = (e, name)
            if e == "pe" and name.startswith("p_pe"):
                continue
            if self.waited.get(key, 0) >= val:
                continue
            self.waited[key] = val
            self.eng[e].wait_ge(sem, val)

    def _deps(self, e, R, W):
        toks = [self.lw.get(r) for r in R]
        for w in W:
            toks.append(self.lw.get(w))
            toks.append(self.rd.get(w, []))
        self.wait(e, toks)

    def _commit(self, tok, R, W):
        for r in R:
            self.rd.setdefault(r, []).append(tok)
            if len(self.rd[r]) > 6:
                best = {}
                for t in self.rd[r]:
                    if t[2] not in best or best[t[2]][1] < t[1]:
                        best[t[2]] = t
                self.rd[r] = list(best.values())
        for w in W:
            self.lw[w] = tok
            self.rd[w] = []

    def barrier(self, exclude=()):
        toks = [self.last_tok[e] for e in ["pe", "act", "dve", "pool"] if e in self.last_tok]
        toks += [(d[0], d[1], d[2]) for d in self.dsems.values() if d[1] > 0 and d[2] not in exclude]
        for e in ["pe", "act", "dve", "pool", "sp"]:
            self.wait(e, *toks)

    def do(self, e, R, W, fn):
        self._deps(e, R, W)
        tok = self.sig(e, fn())
        self._commit(tok, R, W)
        return tok

    def group(self, e, R, W, fns):
        self._deps(e, R, W)
        inst = None
        for fn in fns:
            inst = fn()
        tok = self.sig(e, inst)
        self._commit(tok, R, W)
        return tok

    def dma_batch(self, q, items, semname):
        toks = [self.dma(q, R, W, out, in_, semname, fence=(i == 0)) for i, (R, W, out, in_) in enumerate(items)]
        for (R, W, out, in_) in items:
            for w in W:
                self.lw[w] = toks[-1]
        return toks[-1]

    def dma(self, q, R, W, out, in_, semname, fence=True):
        self._deps(q, R, W)
        if semname not in self.dsems:
            self.dsems[semname] = [self.es.enter_context(self.nc.semaphore("d_" + semname)), 0, "d_" + semname]
        ds = self.dsems[semname]
        if fence and ds[1] > 0:
            self.wait(q, (ds[0], ds[1], ds[2]))
        inst = self.eng[q].dma_start(out=out, in_=in_)
        ds[1] += 16
        inst.then_inc(ds[0], 16)
        tok = (ds[0], ds[1], ds[2])
        self._commit(tok, R, W)
        return tok


def build_program(S, dbg=False, upto=99):
    NT = S // 128
    import os
    GT = min(NT, int(os.environ.get("KGT", "4")))
    NG = NT // GT
    GTOK = GT * 128
    nc = bass.Bass("TRN2", target_bir_lowering=False)
    es = ExitStack()
    k = Ctx(nc, es)
    pe, act, dve, pool, sp = nc.tensor, nc.scalar, nc.vector, nc.gpsimd, nc.sync

    def din(name, shape, dt=F32):
        return nc.dram_tensor(name, list(shape), dt, kind="ExternalInput").ap()

    def dscr(name, shape, dt, out=False, internal=False):
        kind = "ExternalOutput" if ((out or dbg) and not internal) else "Internal"
        return nc.dram_tensor(name, list(shape), dt, kind=kind).ap()

    x_d = din("x", [S, D])
    c_d = din("c_l", [128, 16])
    pos_d = din("pos_l", [128, NT], I32)
    ada_w_d = din("ada_w", [2, D, 3072])
    ada_b_d = din("ada_b", [2, 3 * D])
    norm_g_d = din("norm_g", [2, D])
    rwin_d = din("ret_w_in", [D, 6144])
    rgn_d = din("ret_gn_g", [2048])
    rwout_d = din("ret_w_out", [2048, D])
    mwin_d = din("mla_w_in", [D, 2112])
    mqn_d = din("mla_q_norm_g", [512])
    muqn_d = din("mla_w_uq_nope", [512, 1024])
    muqr_d = din("mla_w_uq_rope", [512, 512])
    mkvn_d = din("mla_kv_norm_g", [512])
    mukvk_d = din("mla_w_ukv_k", [512, 1024])
    mukvv_d = din("mla_w_ukv_v", [512, 1024])
    mwout_d = din("mla_w_out", [1024, D])
    fng_d = din("final_norm_g", [D])
    ident_d = din("ident", [128, 128], BF16)
    invf0_d = din("invf0", [128])
    invf1_d = din("invf1", [32])
    dq_d = din("dq", [128, 4])
    dk_d = din("dk", [128, 4])
    g128_d = din("g128", [128, 4])
    cmask_d = din("cmask", [128, 128], BF16)
    amask_d = din("amask", [128, 4, 512], BF16)

    out_d = nc.dram_tensor("out", [S // 2, D], F32, kind="ExternalOutput").ap()

    rwin_b = dscr("rwin_b", [D, 6144], BF16)
    rwout_b = dscr("rwout_b", [2048, D], BF16)
    mwin_b = dscr("mwin_b", [D, 2112], BF16)
    muqn_b = dscr("muqn_b", [512, 1024], BF16)
    muqr_b = dscr("muqr_b", [512, 512], BF16)
    mukvk_b = dscr("mukvk_b", [512, 1024], BF16)
    mukvv_b = dscr("mukvv_b", [512, 1024], BF16)
    mwout_b = dscr("mwout_b", [1024, D], BF16)
    modloc_d = dscr("modloc", [1, 6144], F32, internal=True)
    modall_d = dscr("modall", [2, 6144], F32, internal=True)
    qs_d = dscr("qs", [S, 1024], BF16)
    ks_d = dscr("ks", [S, 1024], BF16)
    vs_d = dscr("vs", [S, 2048], BF16)
    gs_d = dscr("gs", [S, 2048], BF16)
    zs_d = dscr("zs", [S, 2048], BF16)
    p0_d = dscr("p0_i", [S, D], F32, internal=True)
    s0_d = dscr("s0_i", [S, D], F32, internal=True)
    x1_d = dscr("x1", [S, D], F32)
    qtn_d = dscr("qtn", [8, 128, S], BF16)
    qtr_d = dscr("qtr", [8, 64, S], BF16)
    ktn_d = dscr("ktn", [8, 128, S], BF16)
    ktr_d = dscr("ktr", [64, S], BF16)
    v1_d = dscr("v1", [8, 128, S // 128, 128], BF16)
    gt_d = dscr("gt", [1024, S], BF16)
    zt_d = dscr("zt", [1024, S], BF16)
    p1_d = dscr("p1_i", [S, D], F32, internal=True)
    s1_d = dscr("s1_i", [S // 2, D], F32, internal=True)

    uid = [0]
    def sb(name, shape, dt, stack=es):
        uid[0] += 1
        return stack.enter_context(nc.sbuf_tensor(f"s{uid[0]}_" + name, list(shape), dt))

    def ps(name, shape, dt, stack):
        uid[0] += 1
        return stack.enter_context(nc.psum_tensor(f"ps{uid[0]}_" + name, list(shape), dt))

    ident = sb("ident", [128, 128], BF16)
    shift = [sb(f"shift{l}", [128, D], F32) for l in range(2)]
    g1 = [sb(f"g1_{l}", [128, D], F32) for l in range(2)]
    gate = [sb(f"gate{l}", [128, D], F32) for l in range(2)]
    posf = sb("posf", [128, NT], F32)
    invf0 = sb("invf0", [128, 128], F32)
    invf1 = sb("invf1", [128, 32], F32)
    dq = sb("dq", [128, 4], F32)
    dk = sb("dk", [128, 4], F32)

    ccsem = {}

    def collective(src_ap, dst_ap, name, kind, toks, op=ALU.add):
        if name not in ccsem:
            ccsem[name] = [es.enter_context(nc.semaphore("cc_" + name)), 0]
            k.dsems["cc_" + name] = [ccsem[name][0], 0, "cc_" + name]
        k.wait("pool", toks)
        inst = pool.collective_compute(kind, op, replica_groups=PAIRS[:max(1, NCORES_RUN[0] // 2)], ins=[src_ap], outs=[dst_ap])
        inst.then_inc(ccsem[name][0])
        ccsem[name][1] += 1
        k.dsems["cc_" + name][1] = ccsem[name][1]
        return (ccsem[name][0], ccsem[name][1], "cc_" + name)

    for nm, src, dst in [("rwin_b", rwin_d, rwin_b), ("rwout_b", rwout_d, rwout_b), ("mwin_b", mwin_d, mwin_b),
                         ("muqn_b", muqn_d, muqn_b), ("muqr_b", muqr_d, muqr_b), ("mukvk_b", mukvk_d, mukvk_b),
                         ("mukvv_b", mukvv_d, mukvv_b), ("mwout_b", mwout_d, mwout_b)]:
        rows = src.shape[0]
        for i in range(rows // 128):
            tok = k.dma("pool", [], [(nm, i)], dst[i * 128:(i + 1) * 128, :], src[i * 128:(i + 1) * 128, :], "cast_" + nm, fence=False)
        k.lw[nm] = tok

    with ExitStack() as p0:
        cl = sb("cl", [128, 16], F32, p0)
        cact = sb("cact", [128, 16], F32, p0)
        cbc = sb("cbc", [128, 16, 128], F32, p0)
        posi = sb("posi", [128, NT], I32, p0)
        gbc = sb("gbc", [128, D], F32, p0)
        bbc = sb("bbc", [128, 3 * D], F32, p0)
        modrow = sb("modrow", [1, 6144], F32, p0)
        wbufs = [sb(f"adaw{i}", [128, 3072], F32, p0) for i in range(3)]
        mps = [ps(f"mps{j}", [128, 512], F32, p0) for j in range(6)]

        k.dma_batch("sp", [([], ["cl"], cl[:], c_d), ([], ["posi"], posi[:], pos_d), ([], ["ident"], ident[:], ident_d),
                           ([], ["invf0"], invf0[:], invf0_d.partition_broadcast(128)), ([], ["invf1"], invf1[:], invf1_d.partition_broadcast(128)),
                           ([], ["dq"], dq[:], dq_d), ([], ["dk"], dk[:], dk_d)], "cb0")
        k.do("act", ["cl"], ["cact"], lambda: act.activation(out=cact[:], in_=cl[:], func=AF.Silu))
        k.do("dve", ["posi"], ["posf"], lambda: dve.tensor_copy(out=posf[:], in_=posi[:]))
        k.do("dve", ["cact"], ["cbc"], lambda: dve.tensor_copy(out=cbc[:], in_=cact[:].unsqueeze(2).broadcast_to([128, 16, 128])))

        wi = 0
        for l in range(2):
            for kc in range(16):
                j = wi % 3
                wi += 1
                wb = wbufs[j]
                k.dma("sp", [], [("adaw", j)], wb[:], ada_w_d[l, kc * 128:(kc + 1) * 128, :], f"adaw{j}")
                k.group("pe", ["cbc", ("adaw", j)], ["mps"],
                        [(lambda jj=jj, kc=kc, wb=wb: pe.matmul(mps[jj][:], lhsT=cbc[:, kc, :], rhs=wb[:, jj * 512:(jj + 1) * 512],
                                                         start=(kc == 0), stop=(kc == 15))) for jj in range(6)])
            for jj in range(6):
                k.do("dve", ["mps"], [("modrow", l, jj)], lambda: dve.tensor_copy(out=modrow[0:1, l * 3072 + jj * 512:l * 3072 + (jj + 1) * 512], in_=mps[jj][0:1, :]))
        tok = k.dma("sp", [("modrow", l, jj) for l in range(2) for jj in range(6)], [], modloc_d, modrow[0:1, :], "cb1")
        tok = collective(modloc_d, modall_d, "ag", "AllGather", [tok], op=ALU.bypass)
        k.wait("sp", tok)
        for l in range(2):
            o = l * 3072
            k.dma_batch("sp", [([], [f"shift{l}"], shift[l][:], modall_d[0, o:o + 2048].partition_broadcast(128)),
                               ([], [(f"g1_{l}", 0)], g1[l][:, 0:1024], modall_d[0, o + 2048:o + 3072].partition_broadcast(128)),
                               ([], [(f"g1_{l}", 1)], g1[l][:, 1024:2048], modall_d[1, o:o + 1024].partition_broadcast(128)),
                               ([], [f"gate{l}"], gate[l][:], modall_d[1, o + 1024:o + 3072].partition_broadcast(128)),
                               ([], ["gbc"], gbc[:], norm_g_d[l].partition_broadcast(128)),
                               ([], ["bbc"], bbc[:], ada_b_d[l].partition_broadcast(128))], "cb1")
            k.do("dve", [f"shift{l}", "bbc"], [f"shift{l}"], lambda: dve.tensor_tensor(out=shift[l][:], in0=shift[l][:], in1=bbc[:, 0:D], op=ALU.add))
            k.do("dve", [(f"g1_{l}", 0), (f"g1_{l}", 1), "bbc"], [f"g1_{l}"], lambda: dve.tensor_tensor(out=g1[l][:], in0=g1[l][:], in1=bbc[:, D:2 * D], op=ALU.add))
            k.do("dve", [f"g1_{l}", "gbc"], [f"g1_{l}"], lambda: dve.scalar_tensor_tensor(out=g1[l][:], in0=g1[l][:], scalar=1.0, in1=gbc[:], op0=ALU.add, op1=ALU.mult))
            k.do("dve", [f"gate{l}", "bbc"], [f"gate{l}"], lambda: dve.tensor_tensor(out=gate[l][:], in0=gate[l][:], in1=bbc[:, 2 * D:3 * D], op=ALU.add))

    k.barrier()

    def rope_tables(tmps, t, invf, ikey, nf, cos_out, sin_out, okey):
        ang, kf, r, m, ki = [x[:, :nf] for x in tmps]
        k.do("dve", [ikey, "posf"], ["rt_ang"], lambda: dve.tensor_scalar(out=ang, in0=invf[:, :nf], scalar1=posf[:, t:t + 1], scalar2=None, op0=ALU.mult))
        k.do("dve", ["rt_ang"], ["rt_ki"], lambda: dve.tensor_scalar(out=ki, in0=ang, scalar1=float(1.0 / TWO_PI), scalar2=None, op0=ALU.mult))
        k.do("dve", ["rt_ki"], ["rt_kf"], lambda: dve.tensor_copy(out=kf, in_=ki))
        k.do("dve", ["rt_kf", "rt_ang"], ["rt_r"], lambda: dve.scalar_tensor_tensor(out=r, in0=kf, scalar=-CW1, in1=ang, op0=ALU.mult, op1=ALU.add))
        k.do("dve", ["rt_kf", "rt_r"], ["rt_r"], lambda: dve.scalar_tensor_tensor(out=r, in0=kf, scalar=-CW2, in1=r, op0=ALU.mult, op1=ALU.add))

        def fold(dst, dkey, src, skey, add):
            if add != 0.0:
                k.do("dve", [skey], [dkey], lambda: dve.tensor_scalar(out=dst, in0=src, scalar1=float(add), scalar2=None, op0=ALU.add))
                src, skey = dst, dkey
            k.do("dve", [skey], ["rt_m"], lambda: dve.tensor_scalar(out=m, in0=src, scalar1=float(np.pi), scalar2=float(-TWO_PI), op0=ALU.is_gt, op1=ALU.mult))
            k.do("dve", [skey, "rt_m"], [dkey], lambda: dve.tensor_tensor(out=dst, in0=src, in1=m, op=ALU.add))
            k.do("dve", [dkey], ["rt_m"], lambda: dve.tensor_scalar(out=m, in0=dst, scalar1=float(-np.pi), scalar2=float(TWO_PI), op0=ALU.is_lt, op1=ALU.mult))
            k.do("dve", [dkey, "rt_m"], [dkey], lambda: dve.tensor_tensor(out=dst, in0=dst, in1=m, op=ALU.add))
            k.do("dve", [dkey], [dkey], lambda: dve.tensor_scalar(out=dst, in0=dst, scalar1=float(-np.pi), scalar2=float(np.pi), op0=ALU.max, op1=ALU.min))

        fold(r, "rt_r", r, "rt_r", 0.0)
        fold(ang, "rt_ang", r, "rt_r", float(np.pi / 2))
        k.do("act", ["rt_r"], [(okey, "sin")], lambda: act.activation(out=sin_out, in_=r, func=AF.Sin))
        k.do("act", ["rt_ang"], [(okey, "cos")], lambda: act.activation(out=cos_out, in_=ang, func=AF.Sin))

    def prologue_a(l, xt, xkey, junk, ssq, rstd, hb):
        k.do("act", [xkey], ["junk", "ssq"], lambda: act.activation(out=junk[:], in_=xt[:], func=AF.Square, accum_out=ssq[:]))
        k.do("act", ["ssq"], ["rstd"], lambda: act.activation(out=rstd[:], in_=ssq[:], func=AF.Sqrt, scale=float(1.0 / D), bias=float(EPS)))
        k.do("dve", ["rstd"], ["rstd"], lambda: dve.reciprocal(out=rstd[:], in_=rstd[:]))
        k.do("dve", [xkey, "rstd", f"g1_{l}"], [xkey], lambda: dve.scalar_tensor_tensor(out=xt[:], in0=xt[:], scalar=rstd[:, 0:1], in1=g1[l][:], op0=ALU.mult, op1=ALU.mult))
        k.do("dve", [xkey, f"shift{l}"], ["hb"], lambda: dve.tensor_tensor(out=hb[:], in0=xt[:], in1=shift[l][:], op=ALU.add))

    def prologue_b(hb, pT, hT, tl, hk="hT"):
        k.group("pe", ["hb", "ident"], ["pT"], [(lambda kc=kc: pe.transpose(pT[:, kc * 128:(kc + 1) * 128], hb[:, kc * 128:(kc + 1) * 128], ident[:])) for kc in range(16)])
        for hf in range(2):
            k.do("act", ["pT"], [(hk, tl)], lambda hf=hf: act.copy(out=hT[:, hf * 8:(hf + 1) * 8, tl * 128:(tl + 1) * 128],
                                                                in_=pT[:, hf * 1024:(hf + 1) * 1024].rearrange("p (a b) -> p a b", a=8)))

    def prologue_tile(l, xt, xkey, junk, ssq, rstd, hb, pT, hT, tl, hk="hT"):
        prologue_a(l, xt, xkey, junk, ssq, rstd, hb)
        prologue_b(hb, pT, hT, tl, hk)

    if upto >= 1:
        with ExitStack() as pa:
            xts = [sb(f"xt{i}", [128, D], F32, pa) for i in range(2)]
            junk = sb("junk", [128, D], BF16, pa)
            hb = sb("hb", [128, D], BF16, pa)
            ssq = sb("ssq", [128, 1], F32, pa)
            rstd = sb("rstd", [128, 1], F32, pa)
            hTs = [sb(f"hT{i}", [128, 16, GTOK], BF16, pa) for i in range(2)]
            wblk = [sb(f"wblk{i}", [128, 16, 512], BF16, pa) for i in range(2)]
            stg = [sb(f"stg{i}", [128, GT, 512], BF16, pa) for i in range(2)]
            cosbs = [sb(f"cosb{i}", [128, GT, 128], F32, pa) for i in range(2)]
            sinbs = [sb(f"sinb{i}", [128, GT, 128], F32, pa) for i in range(2)]
            rtmp = [sb(f"rt{i}", [128, 128], F32, pa) for i in range(4)] + [sb("rti", [128, 128], I32, pa)]
            tt = [sb(f"tt{i}", [128, 128], F32, pa) for i in range(4)]
            gnb = sb("gnb", [128, 2048], F32, pa)
            pT = ps("pT", [128, D], BF16, pa)
            mmps = [ps(f"mm{i}", [128, 512], F32, pa) for i in range(4)]

            k.dma("sp", [], ["gnb"], gnb[:], rgn_d.partition_broadcast(128), "cb0")
            xi = wi = si = mi = 0

            def pro_a(g, tl):
                nonlocal xi
                t = g * GT + tl
                j = xi % 2
                xi += 1
                k.dma("sp", ["x"], [("xt", j)], xts[j][:], x_d[t * 128:(t + 1) * 128, :], f"xt{j}")
                prologue_a(0, xts[j], ("xt", j), junk, ssq, rstd, hb)
                rope_tables(rtmp, t, invf0, "invf0", 128, cosbs[g % 2][:, tl, :], sinbs[g % 2][:, tl, :], ("tab%d" % (g % 2), tl))

            def pro_b(g, tl):
                prologue_b(hb, pT, hTs[g % 2], tl, "hT%d" % (g % 2))

            def prologue_group(g):
                for tl in range(GT):
                    pro_a(g, tl)
                    pro_b(g, tl)

            def load_w(idx):
                cb_ = idx % 12
                j_ = idx % 2
                k.dma("sp", ["rwin_b"], [("wblk", j_)], wblk[j_][:], rwin_b[:, cb_ * 512:(cb_ + 1) * 512].rearrange("(kc p) c -> p kc c", p=128), f"wblk{j_}")

            prologue_group(0)
            load_w(0)
            for g in range(NG):
                hT = hTs[g % 2]
                hk = "hT%d" % (g % 2)
                cosb, sinb = cosbs[g % 2], sinbs[g % 2]
                tabk = "tab%d" % (g % 2)
                for cb in range(12):
                    j = wi % 2
                    wi += 1
                    wb = wblk[j]
                    if wi < NG * 12:
                        load_w(wi)
                    if g + 1 < NG and GT == 4:
                        if 5 <= cb <= 8:
                            pro_b(g + 1, cb - 5)
                        if 4 <= cb <= 7:
                            pro_a(g + 1, cb - 4)
                    elif g + 1 < NG and cb == 6:
                        prologue_group(g + 1)
                    sj = si % 2
                    si += 1
                    st = stg[sj]
                    for tl in range(GT):
                        mj = mi % 4
                        mi += 1
                        mp = mmps[mj]
                        k.group("pe", [("wblk", j), (hk, tl)], [("mm", mj)],
                                [(lambda kc=kc, mp=mp, tl=tl, wb=wb: pe.matmul(mp[:], lhsT=hT[:, kc, tl * 128:(tl + 1) * 128], rhs=wb[:, kc, :],
                                                                          start=(kc == 0), stop=(kc == 15))) for kc in range(16)])
                        skey = ("stg", sj, tl)
                        if cb < 4:
                            dtab, dkey = (dq, "dq") if cb < 2 else (dk, "dk")
                            for hh in range(2):
                                h = (cb % 2) * 2 + hh
                                x1 = mp[:, hh * 256:hh * 256 + 128]
                                x2 = mp[:, hh * 256 + 128:hh * 256 + 256]
                                dsc = dtab[:, h:h + 1]
                                cs, sn = cosb[:, tl, :], sinb[:, tl, :]
                                RK = [("mm", mj), dkey, ((tabk, tl), "cos"), ((tabk, tl), "sin")]
                                k.do("dve", RK, ["tt0"], lambda: dve.scalar_tensor_tensor(out=tt[0][:], in0=x1, scalar=dsc, in1=cs, op0=ALU.mult, op1=ALU.mult))
                                k.do("dve", RK, ["tt1"], lambda: dve.scalar_tensor_tensor(out=tt[1][:], in0=x2, scalar=dsc, in1=sn, op0=ALU.mult, op1=ALU.mult))
                                k.do("dve", RK, ["tt2"], lambda: dve.scalar_tensor_tensor(out=tt[2][:], in0=x2, scalar=dsc, in1=cs, op0=ALU.mult, op1=ALU.mult))
                                k.do("dve", RK, ["tt3"], lambda: dve.scalar_tensor_tensor(out=tt[3][:], in0=x1, scalar=dsc, in1=sn, op0=ALU.mult, op1=ALU.mult))
                                k.do("pool", ["tt0", "tt1"], [skey + (hh, 0)], lambda: pool.tensor_tensor(out=st[:, tl, hh * 256:hh * 256 + 128], in0=tt[0][:], in1=tt[1][:], op=ALU.subtract))
                                k.do("pool", ["tt2", "tt3"], [skey + (hh, 1)], lambda: pool.tensor_tensor(out=st[:, tl, hh * 256 + 128:hh * 256 + 256], in0=tt[2][:], in1=tt[3][:], op=ALU.add))
                        elif cb < 8:
                            sk4 = [skey + (hh, q) for hh in range(2) for q in range(2)]
                            k.do("act", [("mm", mj)], sk4, lambda: act.copy(out=st[:, tl, :], in_=mp[:]))
                        else:
                            sk4 = [skey + (hh, q) for hh in range(2) for q in range(2)]
                            k.do("act", [("mm", mj)], sk4, lambda: act.activation(out=st[:, tl, :], in_=mp[:], func=AF.Silu))
                            k.do("pool", sk4 + ["gnb"], sk4, lambda: pool.tensor_tensor(out=st[:, tl, :], in0=st[:, tl, :], in1=gnb[:, (cb - 8) * 512:(cb - 7) * 512], op=ALU.mult))
                    if cb < 2:
                        dst, dname = qs_d[g * GTOK:(g + 1) * GTOK, cb * 512:(cb + 1) * 512], "qs"
                    elif cb < 4:
                        dst, dname = ks_d[g * GTOK:(g + 1) * GTOK, (cb - 2) * 512:(cb - 1) * 512], "ks"
                    elif cb < 8:
                        dst, dname = vs_d[g * GTOK:(g + 1) * GTOK, (cb - 4) * 512:(cb - 3) * 512], "vs"
                    else:
                        dst, dname = gs_d[g * GTOK:(g + 1) * GTOK, (cb - 8) * 512:(cb - 7) * 512], "gs"
                    rkeys = [("stg", sj, tl, hh, q) for tl in range(GT) for hh in range(2) for q in range(2)]
                    tok = k.dma("sp", rkeys, [], dst.rearrange("(t p) c -> p t c", p=128), st[:], f"stg{sj}")
                    k.stores.append(tok)

            k.barrier()

    if upto >= 2:
        with ExitStack() as pb:
            qin = [sb(f"qin{i}", [128, 1024], BF16, pb) for i in range(2)]
            kin = [sb(f"kin{i}", [128, 1024], BF16, pb) for i in range(2)]
            vin = [sb(f"vin{i}", [128, 2048], BF16, pb) for i in range(2)]
            gin = [sb(f"gin{i}", [128, 2048], BF16, pb) for i in range(2)]
            qTs = [sb(f"qTs{i}", [128, 8, 128], BF16, pb) for i in range(2)]
            kTs = [sb(f"kTs{i}", [128, 8, 128], BF16, pb) for i in range(2)]
            sTm = [sb(f"sTm{i}", [128, 4, 128], BF16, pb) for i in range(2)]
            cmask = sb("cmask", [128, 128], BF16, pb)
            g128 = sb("g128", [128, 4], F32, pb)
            state = sb("state", [128, 8, 512], F32, pb)
            sbf = [sb(f"sbf{i}", [128, 8, 512], BF16, pb) for i in range(2)]
            sgt = [sb(f"sgt{i}", [128, 512], F32, pb) for i in range(2)]
            st6 = sb("st6", [128, 6], F32, pb)
            mv = sb("mv", [128, 2], F32, pb)
            rs = sb("rs", [128, 1], F32, pb)
            yn = [sb(f"yn{i}", [128, 512], F32, pb) for i in range(2)]
            zst = [sb(f"zst{i}", [128, 2048], BF16, pb) for i in range(2)]
            pq = ps("pq", [128, 1024], BF16, pb)
            pk = ps("pk", [128, 1024], BF16, pb)
            psT = ps("psT", [128, 512], F32, pb)
            po = [ps(f"po{i}", [128, 512], F32, pb) for i in range(2)]
            pU = [ps(f"pU{i}", [128, 512], F32, pb) for i in range(3)]

            k.dma_batch("sp", [([], ["cmask"], cmask[:], cmask_d), ([], ["g128"], g128[:], g128_d)], "cb0")
            k.do("dve", [], [("state", i) for i in range(8)], lambda: dve.memset(state[:], 0.0))
            k.do("pool", [], [("sbf", 0, i) for i in range(8)], lambda: pool.memset(sbf[0][:], 0.0))

            def loads(c):
                j = c % 2
                r0 = c * 128
                k.dma("sp", [], [("qin", j)], qin[j][:], qs_d[r0:r0 + 128, :], f"qin{j}")
                k.dma("sp", [], [("kin", j)], kin[j][:], ks_d[r0:r0 + 128, :], f"kin{j}")
                k.dma("sp", [], [("vin", j)], vin[j][:], vs_d[r0:r0 + 128, :], f"vin{j}")
                k.dma("sp", [], [("gin", j)], gin[j][:], gs_d[r0:r0 + 128, :], f"gin{j}")

            def stage1(c):
                j = c % 2
                k.group("pe", [("qin", j), "ident"], ["pq"], [(lambda i=i: pe.transpose(pq[:, i * 128:(i + 1) * 128], qin[j][:, i * 128:(i + 1) * 128], ident[:])) for i in range(8)])
                k.group("pe", [("kin", j), "ident"], ["pk"], [(lambda i=i: pe.transpose(pk[:, i * 128:(i + 1) * 128], kin[j][:, i * 128:(i + 1) * 128], ident[:])) for i in range(8)])
                k.do("act", ["pq"], [("qTs", j)], lambda: act.copy(out=qTs[j][:], in_=pq[:].rearrange("p (a b) -> p a b", a=8)))
                k.do("act", ["pk"], [("kTs", j)], lambda: act.copy(out=kTs[j][:], in_=pk[:].rearrange("p (a b) -> p a b", a=8)))
                fns = []
                for h in range(4):
                    for dc in range(2):
                        fns.append(lambda h=h, dc=dc: pe.matmul(psT[:, h * 128:(h + 1) * 128], lhsT=kTs[j][:, h * 2 + dc, :], rhs=qTs[j][:, h * 2 + dc, :],
                                                               start=(dc == 0), stop=(dc == 1)))
                k.group("pe", [("qTs", j), ("kTs", j)], ["psT"], fns)
                k.do("dve", ["psT", "cmask"], [("sTm", j)], lambda: dve.tensor_tensor(out=sTm[j][:], in0=psT[:].rearrange("p (a b) -> p a b", a=4),
                                                                                  in1=cmask[:].unsqueeze(1).broadcast_to([128, 4, 128]), op=ALU.mult))

            cnt = {"u": 0, "o": 0}

            def stage2(c):
                j = c % 2
                ver = c % 2
                for h in range(4):
                    vsl = vin[j][:, h * 512:(h + 1) * 512]
                    ujs = []
                    for dc in range(2):
                        uj = cnt["u"] % 3
                        cnt["u"] += 1
                        ujs.append(uj)
                        k.do("pe", [("kin", j), ("vin", j)], [("pU", uj)],
                             lambda uj=uj, dc=dc: pe.matmul(pU[uj][:], lhsT=kin[j][:, h * 256 + dc * 128:h * 256 + dc * 128 + 128], rhs=vsl, start=True, stop=True))
                    oj = cnt["o"] % 2
                    cnt["o"] += 1
                    k.group("pe", [("sTm", j), ("vin", j), ("qTs", j), ("sbf", ver, h * 2), ("sbf", ver, h * 2 + 1)], [("po", oj)], [
                        lambda: pe.matmul(po[oj][:], lhsT=sTm[j][:, h, :], rhs=vsl, start=True, stop=False),
                        lambda: pe.matmul(po[oj][:], lhsT=qTs[j][:, h * 2, :], rhs=sbf[ver][:, h * 2, :], start=False, stop=False),
                        lambda: pe.matmul(po[oj][:], lhsT=qTs[j][:, h * 2 + 1, :], rhs=sbf[ver][:, h * 2 + 1, :], start=False, stop=True)])
                    k.do("dve", [("po", oj)], ["st6"], lambda: dve.bn_stats(out=st6[:], in_=po[oj][:]))
                    k.do("dve", ["st6"], ["mv"], lambda: dve.bn_aggr(out=mv[:], in_=st6[:]))
                    k.do("act", ["mv"], ["rs"], lambda: act.activation(out=rs[:], in_=mv[:, 1:2], func=AF.Sqrt, bias=float(EPS)))
                    k.do("dve", ["rs"], ["rs"], lambda: dve.reciprocal(out=rs[:], in_=rs[:]))
                    yj = oj
                    k.do("dve", [("po", oj), "mv", "rs"], [("yn", yj)], lambda: dve.tensor_scalar(out=yn[yj][:], in0=po[oj][:], scalar1=mv[:, 0:1], scalar2=rs[:, 0:1],
                                                                                              op0=ALU.subtract, op1=ALU.mult))
                    k.do("pool", [("yn", yj), ("gin", j)], [("zst", j, h)], lambda: pool.tensor_tensor(out=zst[j][:, h * 512:(h + 1) * 512], in0=yn[yj][:],
                                                                                                 in1=gin[j][:, h * 512:(h + 1) * 512], op=ALU.mult))
                    for dc in range(2):
                        i = h * 2 + dc
                        uj = ujs[dc]
                        k.do("act", [("state", i), "g128"], [("sgt", dc)], lambda: act.activation(out=sgt[dc][:], in_=state[:, i, :], func=AF.Copy, scale=g128[:, h:h + 1]))
                        k.do("dve", [("pU", uj), ("sgt", dc), "g128"], [("state", i)],
                             lambda: dve.scalar_tensor_tensor(out=state[:, i, :], in0=pU[uj][:], scalar=g128[:, h:h + 1], in1=sgt[dc][:], op0=ALU.mult, op1=ALU.add))
                        k.do("act", [("state", i)], [("sbf", 1 - ver, i)], lambda: act.copy(out=sbf[1 - ver][:, i, :], in_=state[:, i, :]))
                tok = k.dma("sp", [("zst", j, h) for h in range(4)], [], zs_d[c * 128:(c + 1) * 128, :], zst[j][:], f"zst{j}")
                k.stores.append(tok)

            loads(0)
            if NT > 1:
                loads(1)
            stage1(0)
            for c in range(NT):
                if c + 1 < NT:
                    stage1(c + 1)
                stage2(c)
                if c + 2 < NT:
                    loads(c + 2)
            k.barrier()

    ar0_toks = []
    rs1_toks = []
    if upto >= 3:
        with ExitStack() as pc:
            wout = sb("wout", [128, 16, D], BF16, pc)
            zin = [sb(f"zin{i}", [128, 2048], BF16, pc) for i in range(2)]
            zT = [sb(f"zT{i}", [128, 16, 128], BF16, pc) for i in range(2)]
            ost = [sb(f"ost{i}", [128, D], F32, pc) for i in range(2)]
            xc = [sb(f"xc{i}", [128, D], F32, pc) for i in range(2)]
            pT = ps("pT3", [128, D], BF16, pc)
            pout = [ps(f"pout{i}", [128, 512], F32, pc) for i in range(4)]
            for q4 in range(4):
                k.dma("sp", ["rwout_b"], [("wout", q4)], wout[:, q4 * 4:(q4 + 1) * 4, :],
                      rwout_b[q4 * 512:(q4 + 1) * 512, :].rearrange("(kc p) c -> p kc c", p=128), "cb1", fence=(q4 == 0))
            k.lw[("wout", 0)] = k.lw[("wout", 1)] = k.lw[("wout", 2)] = k.lw[("wout", 3)]
            ctoks = []

            def c0_loads(t):
                j_ = t % 2
                k.dma("sp", [], [("zin", j_)], zin[j_][:], zs_d[t * 128:(t + 1) * 128, :], f"zin{j_}")
                k.dma("sp", ["x"], [("xc", j_)], xc[j_][:], x_d[t * 128:(t + 1) * 128, :], f"xc{j_}")

            c0_loads(0)
            for t in range(NT):
                j = t % 2
                if t + 1 < NT:
                    c0_loads(t + 1)
                k.group("pe", [("zin", j), "ident"], ["pT3"], [(lambda kc=kc: pe.transpose(pT[:, kc * 128:(kc + 1) * 128], zin[j][:, kc * 128:(kc + 1) * 128], ident[:])) for kc in range(16)])
                for hf in range(2):
                    k.do("act", ["pT3"], [("zT", j, hf)], lambda: act.copy(out=zT[j][:, hf * 8:(hf + 1) * 8, :], in_=pT[:, hf * 1024:(hf + 1) * 1024].rearrange("p (a b) -> p a b", a=8)))
                for cb in range(4):
                    k.group("pe", [("zT", j, 0), ("zT", j, 1)] + [("wout", q4) for q4 in range(4)], [("pout", cb)],
                            [(lambda kc=kc: pe.matmul(pout[cb][:], lhsT=zT[j][:, kc, :], rhs=wout[:, kc, cb * 512:(cb + 1) * 512], start=(kc == 0), stop=(kc == 15))) for kc in range(16)])
                    csl = slice(cb * 512, (cb + 1) * 512)
                    k.do("dve", [("pout", cb), "gate0"], [("ost", j, cb)], lambda: dve.tensor_tensor(out=ost[j][:, csl], in0=pout[cb][:], in1=gate[0][:, csl], op=ALU.mult))
                    k.do("dve", [("ost", j, cb), ("xc", j)], [("ost", j, cb)], lambda: dve.scalar_tensor_tensor(out=ost[j][:, csl], in0=xc[j][:, csl], scalar=0.5, in1=ost[j][:, csl], op0=ALU.mult, op1=ALU.add))
                tok = k.dma("sp", [("ost", j, cb) for cb in range(4)], [], p0_d[t * 128:(t + 1) * 128, :], ost[j][:], f"ost{j}")
                k.stores.append(tok)
                ctoks.append(tok)
                if t % 4 == 3 and upto >= 4:
                    r0 = (t - 3) * 128
                    ar0_toks.append(collective(p0_d[r0:r0 + 512, :], s0_d[r0:r0 + 512, :], "ar0", "AllReduce", ctoks))
                    ctoks = []
            k.barrier(exclude=("cc_ar0",))

    QSC = float(192.0 ** -0.5)
    NQG = S // 512
    if upto >= 5:
        with ExitStack() as pa:
            xts = [sb(f"xt{i}", [128, D], F32, pa) for i in range(2)]
            junk = sb("junk", [128, D], BF16, pa)
            hb = sb("hb", [128, D], BF16, pa)
            ssq = sb("ssq", [128, 1], F32, pa)
            rstd = sb("rstd", [128, 1], F32, pa)
            hT = sb("hT", [128, 16, 512], BF16, pa)
            wblk = [sb(f"wblk{i}", [128, 16, 512], BF16, pa) for i in range(2)]
            wkr = sb("wkr", [128, 16, 64], BF16, pa)
            uqn = sb("uqn", [128, 4, 1024], BF16, pa)
            uqr = sb("uqr", [128, 4, 512], BF16, pa)
            ukvk = sb("ukvk", [128, 4, 1024], BF16, pa)
            ukvv = sb("ukvv", [128, 4, 1024], BF16, pa)
            qng = sb("qng", [128, 512], F32, pa)
            kvng = sb("kvng", [128, 512], F32, pa)
            cn = [sb(f"cn{i}", [128, 512], BF16, pa) for i in range(2)]
            cnT = [sb("cqnT", [128, 4, 512], BF16, pa), sb("ckvnT", [128, 4, 512], BF16, pa)]
            cos1 = sb("cos1", [128, 4, 32], F32, pa)
            sin1 = sb("sin1", [128, 4, 32], F32, pa)
            cosq = sb("cosq", [128, 4, 32], F32, pa)
            sinq = sb("sinq", [128, 4, 32], F32, pa)
            rtmp = [sb(f"rt{i}", [128, 32], F32, pa) for i in range(4)] + [sb("rti", [128, 32], I32, pa)]
            tk = [sb(f"tk{i}", [128, 32], F32, pa) for i in range(4)]
            tq = [sb(f"tq{i}", [128, 8, 32], F32, pa) for i in range(4)]
            kr = sb("kr", [128, 64], BF16, pa)
            qr = sb("qr", [128, 8, 64], BF16, pa)
            ktrst = [sb(f"ktrst{i}", [64, 512], BF16, pa) for i in range(2)]
            qtrst = [sb(f"qtrst{i}", [64, 8, 512], BF16, pa) for i in range(1)]
            fst = [sb(f"fst{i}", [128, 512], BF16, pa) for i in range(3)]
            vst = [sb(f"vst{i}", [128, 4, 1024], BF16, pa) for i in range(1)]
            pT = ps("pT5", [128, D], BF16, pa)
            mmps = [ps(f"mm5_{i}", [128, 512], F32, pa) for i in range(4)]
            pt2 = ps("pt2", [128, 1024], BF16, pa)

            k.dma_batch("sp", [(["mwin_b"], ["wkr"], wkr[:], mwin_b[:, 1024:1088].rearrange("(kc p) c -> p kc c", p=128)),
                               (["muqn_b"], ["uqn"], uqn[:], muqn_b.rearrange("(kc p) c -> p kc c", p=128)),
                               (["muqr_b"], ["uqr"], uqr[:], muqr_b.rearrange("(kc p) c -> p kc c", p=128)),
                               (["mukvk_b"], ["ukvk"], ukvk[:], mukvk_b.rearrange("(kc p) c -> p kc c", p=128)),
                               (["mukvv_b"], ["ukvv"], ukvv[:], mukvv_b.rearrange("(kc p) c -> p kc c", p=128)),
                               ([], ["qng"], qng[:], mqn_d.partition_broadcast(128)),
                               ([], ["kvng"], kvng[:], mkvn_d.partition_broadcast(128))], "cb0")
            xi = wi = mi = fi = 0
            wcols = [0, 512, 1088, 1600]

            def load_w1(idx):
                c0 = wcols[idx % 4]
                j_ = idx % 2
                k.dma("sp", ["mwin_b"], [("wblk", j_)], wblk[j_][:], mwin_b[:, c0:c0 + 512].rearrange("(kc p) c -> p kc c", p=128), f"wblk{j_}")

            load_w1(0)
            for g in range(NQG):
                tsl = slice(g * 512, (g + 1) * 512)
                for tl in range(4):
                    t = g * 4 + tl
                    j = xi % 2
                    xi += 1
                    k.wait("sp", ar0_toks[g])
                    k.dma("sp", [], [("xt", j)], xts[j][:], s0_d[t * 128:(t + 1) * 128, :], f"xt{j}")
                    prologue_tile(1, xts[j], ("xt", j), junk, ssq, rstd, hb, pT, hT, tl)
                    rope_tables(rtmp, t, invf1, "invf1", 32, cos1[:, tl, :], sin1[:, tl, :], ("tab1", tl))
                    k.do("dve", [(("tab1", tl), "cos")], [("cosq", tl)], lambda: dve.tensor_scalar(out=cosq[:, tl, :], in0=cos1[:, tl, :], scalar1=QSC, scalar2=None, op0=ALU.mult))
                    k.do("dve", [(("tab1", tl), "sin")], [("sinq", tl)], lambda: dve.tensor_scalar(out=sinq[:, tl, :], in0=sin1[:, tl, :], scalar1=QSC, scalar2=None, op0=ALU.mult))

                for blk in range(2):
                    j = wi % 2
                    wi += 1
                    wb = wblk[j]
                    if wi < NQG * 4:
                        load_w1(wi)
                    gn_t, gn_k = (qng, "qng") if blk == 0 else (kvng, "kvng")
                    def c_mm(tl):
                        nonlocal mi
                        mj = mi % 4
                        mi += 1
                        mp = mmps[mj]
                        k.group("pe", [("wblk", j), ("hT", tl)], [("mm", mj)],
                                [(lambda kc=kc: pe.matmul(mp[:], lhsT=hT[:, kc, tl * 128:(tl + 1) * 128], rhs=wb[:, kc, :], start=(kc == 0), stop=(kc == 15))) for kc in range(16)])
                        return mj

                    def c_post(tl, mj):
                        mp = mmps[mj]
                        k.do("act", [("mm", mj)], ["junk", "ssq"], lambda: act.activation(out=junk[:, 0:512], in_=mp[:], func=AF.Square, accum_out=ssq[:]))
                        k.do("act", ["ssq"], ["rstd"], lambda: act.activation(out=rstd[:], in_=ssq[:], func=AF.Sqrt, scale=float(1.0 / 512), bias=float(EPS)))
                        k.do("dve", ["rstd"], ["rstd"], lambda: dve.reciprocal(out=rstd[:], in_=rstd[:]))
                        cj = tl % 2
                        k.do("dve", [("mm", mj), "rstd", gn_k], [("cn", cj)], lambda: dve.scalar_tensor_tensor(out=cn[cj][:], in0=mp[:], scalar=rstd[:, 0:1], in1=gn_t[:], op0=ALU.mult, op1=ALU.mult))

                    def c_tr(tl):
                        cj = tl % 2
                        k.group("pe", [("cn", cj), "ident"], ["pt2"], [(lambda rc=rc: pe.transpose(pt2[:, rc * 128:(rc + 1) * 128], cn[cj][:, rc * 128:(rc + 1) * 128], ident[:])) for rc in range(4)])
                        k.do("act", ["pt2"], [("cnT", blk, tl)], lambda: act.copy(out=cnT[blk][:, :, tl * 128:(tl + 1) * 128], in_=pt2[:, 0:512].rearrange("p (a b) -> p a b", a=4)))

                    mjs = {}
                    mjs[0] = c_mm(0)
                    for tl in range(4):
                        if tl + 1 < 4:
                            mjs[tl + 1] = c_mm(tl + 1)
                        c_post(tl, mjs[tl])
                        if tl >= 1:
                            c_tr(tl - 1)
                    c_tr(3)

                kj = g % 2
                for tl in range(4):
                    mj = mi % 4
                    mi += 1
                    mp = mmps[mj]
                    k.group("pe", ["wkr", ("hT", tl)], [("mm", mj)],
                            [(lambda kc=kc: pe.matmul(mp[:, 0:64], lhsT=hT[:, kc, tl * 128:(tl + 1) * 128], rhs=wkr[:, kc, :], start=(kc == 0), stop=(kc == 15))) for kc in range(16)])
                    x1, x2 = mp[:, 0:32], mp[:, 32:64]
                    RK = [("mm", mj), (("tab1", tl), "cos"), (("tab1", tl), "sin")]
                    k.do("dve", RK, ["tk0"], lambda: dve.tensor_tensor(out=tk[0][:], in0=x1, in1=cos1[:, tl, :], op=ALU.mult))
                    k.do("dve", RK, ["tk1"], lambda: dve.tensor_tensor(out=tk[1][:], in0=x2, in1=sin1[:, tl, :], op=ALU.mult))
                    k.do("dve", RK, ["tk2"], lambda: dve.tensor_tensor(out=tk[2][:], in0=x2, in1=cos1[:, tl, :], op=ALU.mult))
                    k.do("dve", RK, ["tk3"], lambda: dve.tensor_tensor(out=tk[3][:], in0=x1, in1=sin1[:, tl, :], op=ALU.mult))
                    k.do("pool", ["tk0", "tk1"], [("kr", 0)], lambda: pool.tensor_tensor(out=kr[:, 0:32], in0=tk[0][:], in1=tk[1][:], op=ALU.subtract))
                    k.do("pool", ["tk2", "tk3"], [("kr", 1)], lambda: pool.tensor_tensor(out=kr[:, 32:64], in0=tk[2][:], in1=tk[3][:], op=ALU.add))
                    k.do("pe", [("kr", 0), ("kr", 1), "ident"], ["pt2"], lambda: pe.transpose(pt2[0:64, 0:128], kr[:], ident[:]))
                    k.do("act", ["pt2"], [("ktrst", kj, tl)], lambda: act.copy(out=ktrst[kj][:, tl * 128:(tl + 1) * 128], in_=pt2[0:64, 0:128]))
                tok = k.dma("sp", [("ktrst", kj, tl) for tl in range(4)], [], ktr_d[:, tsl], ktrst[kj][:], f"ktrst{kj}")
                k.stores.append(tok)

                for blk in range(2):
                    j = wi % 2
                    wi += 1
                    wb = wblk[j]
                    if wi < NQG * 4:
                        load_w1(wi)
                    for cc in range(4):
                        mj = mi % 4
                        mi += 1
                        mp = mmps[mj]
                        k.group("pe", [("wblk", j)] + [("hT", tl) for tl in range(4)], [("mm", mj)],
                                [(lambda kc=kc: pe.matmul(mp[:], lhsT=wb[:, kc, cc * 128:(cc + 1) * 128], rhs=hT[:, kc, :], start=(kc == 0), stop=(kc == 15))) for kc in range(16)])
                        fj = fi % 3
                        fi += 1
                        k.do("act", [("mm", mj)], [("fst", fj)], lambda: act.activation(out=fst[fj][:], in_=mp[:], func=AF.Silu))
                        row = (blk * 4 + cc) * 128
                        tok = k.dma("sp", [("fst", fj)], [], gt_d[row:row + 128, tsl], fst[fj][:], f"fst{fj}")
                        k.stores.append(tok)

                for which in range(2):
                    wt, wk_, dst_d = [(uqn, "uqn", qtn_d), (ukvk, "ukvk", ktn_d)][which]
                    for hd in range(8):
                        mj = mi % 4
                        mi += 1
                        mp = mmps[mj]
                        k.group("pe", [wk_] + [("cnT", which, tl) for tl in range(4)], [("mm", mj)],
                                [(lambda rc=rc: pe.matmul(mp[:], lhsT=wt[:, rc, hd * 128:(hd + 1) * 128], rhs=cnT[which][:, rc, :], start=(rc == 0), stop=(rc == 3))) for rc in range(4)])
                        fj = fi % 3
                        fi += 1
                        if which == 0:
                            k.do("act", [("mm", mj)], [("fst", fj)], lambda: act.activation(out=fst[fj][:], in_=mp[:], func=AF.Copy, scale=QSC))
                        else:
                            k.do("dve", [("mm", mj)], [("fst", fj)], lambda: dve.tensor_copy(out=fst[fj][:], in_=mp[:]))
                        tok = k.dma("sp", [("fst", fj)], [], dst_d[hd, :, tsl], fst[fj][:], f"fst{fj}")
                        k.stores.append(tok)

                qj = 0
                for tl in range(4):
                    mj = mi % 4
                    mi += 1
                    mp = mmps[mj]
                    k.group("pe", ["uqr", ("cnT", 0, tl)], [("mm", mj)],
                            [(lambda rc=rc: pe.matmul(mp[:], lhsT=cnT[0][:, rc, tl * 128:(tl + 1) * 128], rhs=uqr[:, rc, :], start=(rc == 0), stop=(rc == 3))) for rc in range(4)])
                    mv4 = mp[:].rearrange("p (h t f) -> p h t f", h=8, t=2)
                    x1, x2 = mv4[:, :, 0, :], mv4[:, :, 1, :]
                    cb_ = cosq[:, tl, :].unsqueeze(1).broadcast_to([128, 8, 32])
                    sb_ = sinq[:, tl, :].unsqueeze(1).broadcast_to([128, 8, 32])
                    RK = [("mm", mj), ("cosq", tl), ("sinq", tl)]
                    k.do("dve", RK, ["tq0"], lambda: dve.tensor_tensor(out=tq[0][:], in0=x1, in1=cb_, op=ALU.mult))
                    k.do("dve", RK, ["tq1"], lambda: dve.tensor_tensor(out=tq[1][:], in0=x2, in1=sb_, op=ALU.mult))
                    k.do("dve", RK, ["tq2"], lambda: dve.tensor_tensor(out=tq[2][:], in0=x2, in1=cb_, op=ALU.mult))
                    k.do("dve", RK, ["tq3"], lambda: dve.tensor_tensor(out=tq[3][:], in0=x1, in1=sb_, op=ALU.mult))
                    k.do("pool", ["tq0", "tq1"], [("qr", 0)], lambda: pool.tensor_tensor(out=qr[:, :, 0:32], in0=tq[0][:], in1=tq[1][:], op=ALU.subtract))
                    k.do("pool", ["tq2", "tq3"], [("qr", 1)], lambda: pool.tensor_tensor(out=qr[:, :, 32:64], in0=tq[2][:], in1=tq[3][:], op=ALU.add))
                    k.group("pe", [("qr", 0), ("qr", 1), "ident"], ["pt2"], [(lambda hd=hd: pe.transpose(pt2[0:64, hd * 128:(hd + 1) * 128], qr[:, hd, :], ident[:])) for hd in range(8)])
                    k.do("act", ["pt2"], [("qtrst", qj, tl)], lambda: act.copy(out=qtrst[qj][:, :, tl * 128:(tl + 1) * 128], in_=pt2[0:64, :].rearrange("p (a b) -> p a b", a=8)))
                tok = k.dma("sp", [("qtrst", qj, tl) for tl in range(4)], [], qtr_d[:, :, tsl].rearrange("h d t -> d h t"), qtrst[qj][:], f"qtrst{qj}")
                k.stores.append(tok)

                vj = 0
                for tl in range(4):
                    for vb in range(2):
                        mj = mi % 4
                        mi += 1
                        mp = mmps[mj]
                        k.group("pe", ["ukvv", ("cnT", 1, tl)], [("mm", mj)],
                                [(lambda rc=rc: pe.matmul(mp[:], lhsT=cnT[1][:, rc, tl * 128:(tl + 1) * 128], rhs=ukvv[:, rc, vb * 512:(vb + 1) * 512], start=(rc == 0), stop=(rc == 3))) for rc in range(4)])
                        if vb == 0:
                            k.do("act", [("mm", mj)], [("vst", vj, tl, vb)], lambda: act.copy(out=vst[vj][:, tl, 0:512], in_=mp[:]))
                        else:
                            k.do("dve", [("mm", mj)], [("vst", vj, tl, vb)], lambda: dve.tensor_copy(out=vst[vj][:, tl, 512:1024], in_=mp[:]))
                for hd in range(8):
                    tok = k.dma("sp", [("vst", vj, tl, vb) for tl in range(4) for vb in range(2)], [],
                                v1_d[hd, :, g * 4:(g + 1) * 4, :], vst[vj][:, :, hd * 128:(hd + 1) * 128], f"vst{vj}", fence=(hd == 0))
                k.stores.append(tok)
            k.barrier()

    if upto >= 6:
        with ExitStack() as pb:
            ktn = [sb(f"ktn{i}", [128, S], BF16, pb) for i in range(2)]
            vh = [sb(f"vh{i}", [128, NT, 128], BF16, pb) for i in range(2)]
            ktr = sb("ktr", [128, S], BF16, pb)
            qn = [sb(f"qn{i}", [128, 512], BF16, pb) for i in range(2)]
            qrr = [sb(f"qrr{i}", [128, 512], BF16, pb) for i in range(2)]
            gtt = [sb(f"gtt{i}", [128, 512], BF16, pb) for i in range(2)]
            pTt = [sb(f"pTt{i}", [128, 512], BF16, pb) for i in range(3)]
            amask = sb("amask", [128, 4, 512], BF16, pb)
            ones = sb("ones", [128, 128], BF16, pb)
            rden = [sb(f"rden{i}", [128, 512], F32, pb) for i in range(2)]
            on = [sb(f"on{i}", [128, 512], F32, pb) for i in range(2)]
            zst = [sb(f"zst1_{i}", [128, 512], BF16, pb) for i in range(2)]
            pss = [ps(f"pss{i}", [128, 512], F32, pb) for i in range(3)]
            pso = [ps(f"pso{i}", [128, 512], F32, pb) for i in range(2)]
            psd = [ps(f"psd{i}", [128, 512], F32, pb) for i in range(2)]

            k.do("dve", [], ["ktr_pad"], lambda: dve.memset(ktr[64:128, :], 0.0))
            k.do("pool", [], [("qrr_pad", 0)], lambda: pool.memset(qrr[0][64:128, :], 0.0))
            k.do("pool", [], [("qrr_pad", 1)], lambda: pool.memset(qrr[1][64:128, :], 0.0))
            k.dma_batch("sp", [([], ["amask"], amask[:], amask_d), ([], ["ktr"], ktr[0:64, :], ktr_d)], "cb0")
            k.do("dve", [], ["ones"], lambda: dve.memset(ones[:], 1.0))

            def load_head(hd):
                hj = hd % 2
                k.dma("sp", [], [("ktn", hj)], ktn[hj][:], ktn_d[hd], f"ktn{hj}")
                k.dma("sp", [], [("vh", hj)], vh[hj][:], v1_d[hd], f"vh{hj}")

            load_head(0)
            si = pi = qi = 0
            for hd in range(8):
                hj = hd % 2
                if hd + 1 < 8:
                    load_head(hd + 1)
                for G in range(NQG):
                    qj = qi % 2
                    qi += 1
                    tsl = slice(G * 512, (G + 1) * 512)
                    k.dma("sp", [], [("qn", qj)], qn[qj][:], qtn_d[hd, :, tsl], f"qn{qj}")
                    k.dma("sp", [], [("qrr", qj)], qrr[qj][0:64, :], qtr_d[hd, :, tsl], f"qrr{qj}")
                    k.dma("sp", [], [("gtt", qj)], gtt[qj][:], gt_d[hd * 128:(hd + 1) * 128, tsl], f"gtt{qj}")
                    nkb = 4 * G + 4

                    def s_part(kb):
                        sj = kb % 3
                        ksl = slice(kb * 128, (kb + 1) * 128)
                        k.group("pe", [("ktn", hj), "ktr", "ktr_pad", ("qn", qj), ("qrr", qj), ("qrr_pad", qj)], [("pss", sj)], [
                            lambda: pe.matmul(pss[sj][:], lhsT=ktn[hj][:, ksl], rhs=qn[qj][:], start=True, stop=False),
                            lambda: pe.matmul(pss[sj][:], lhsT=ktr[:, ksl], rhs=qrr[qj][:], start=False, stop=True)])

                    def p_part(kb):
                        sj = kb % 3
                        pj = kb % 3
                        k.do("act", [("pss", sj)], [("pTt", pj)], lambda: act.activation(out=pTt[pj][:], in_=pss[sj][:], func=AF.Exp))
                        if kb >= 4 * G:
                            r = kb - 4 * G
                            k.do("pool", [("pTt", pj), "amask"], [("pTt", pj)], lambda: pool.tensor_tensor(out=pTt[pj][:], in0=pTt[pj][:], in1=amask[:, r, :], op=ALU.mult))
                        k.group("pe", [("vh", hj), ("pTt", pj), "ones"], [("pso", qj), ("psd", qj)], [
                            lambda: pe.matmul(pso[qj][:], lhsT=vh[hj][:, kb, :], rhs=pTt[pj][:], start=(kb == 0), stop=(kb == nkb - 1)),
                            lambda: pe.matmul(psd[qj][:], lhsT=ones[:], rhs=pTt[pj][:], start=(kb == 0), stop=(kb == nkb - 1))])

                    s_part(0)
                    if nkb > 1:
                        s_part(1)
                    for kb in range(nkb):
                        if kb + 2 < nkb:
                            k.wait("pe", k.lw.get(("pTt", kb % 3)))
                            s_part(kb + 2)
                        p_part(kb)
                    k.do("dve", [("psd", qj)], [("rden", qj)], lambda: dve.reciprocal(out=rden[qj][:], in_=psd[qj][:]))
                    k.do("dve", [("pso", qj), ("rden", qj)], [("on", qj)], lambda: dve.tensor_tensor(out=on[qj][:], in0=pso[qj][:], in1=rden[qj][:], op=ALU.mult))
                    k.do("pool", [("on", qj), ("gtt", qj)], [("zst1", qj)], lambda: pool.tensor_tensor(out=zst[qj][:], in0=on[qj][:], in1=gtt[qj][:], op=ALU.mult))
                    tok = k.dma("sp", [("zst1", qj)], [], zt_d[hd * 128:(hd + 1) * 128, tsl], zst[qj][:], f"zst1_{qj}")
                    k.stores.append(tok)
            k.barrier()

    if upto >= 7:
        with ExitStack() as pc:
            wout = sb("wout1", [128, 8, D], BF16, pc)
            zTg = [sb(f"zTg{i}", [128, 8, 512], BF16, pc) for i in range(2)]
            ost = [sb(f"ost{i}", [128, D], F32, pc) for i in range(2)]
            xc = [sb(f"xc{i}", [128, D], F32, pc) for i in range(2)]
            pout = [ps(f"pout1_{i}", [128, 512], F32, pc) for i in range(4)]
            for q4 in range(2):
                k.dma("sp", ["mwout_b"], [("wout1", q4)], wout[:, q4 * 4:(q4 + 1) * 4, :],
                      mwout_b[q4 * 512:(q4 + 1) * 512, :].rearrange("(kc p) c -> p kc c", p=128), "cb1", fence=(q4 == 0))
            k.lw[("wout1", 0)] = k.lw[("wout1", 1)]
            oi = 0
            for g in range(NQG):
                ctoks = []
                gj = g % 2
                k.dma("sp", [], [("zTg", gj)], zTg[gj][:], zt_d[:, g * 512:(g + 1) * 512].rearrange("(kc p) t -> p kc t", p=128), f"zTg{gj}")
                for tl in range(4):
                    t = g * 4 + tl
                    j = oi % 2
                    oi += 1
                    k.dma("sp", [], [("xc", j)], xc[j][:], s0_d[t * 128:(t + 1) * 128, :], f"xc{j}")
                    for cb in range(4):
                        k.group("pe", [("zTg", gj), ("wout1", 0), ("wout1", 1)], [("pout", cb)],
                                [(lambda kc=kc: pe.matmul(pout[cb][:], lhsT=zTg[gj][:, kc, tl * 128:(tl + 1) * 128], rhs=wout[:, kc, cb * 512:(cb + 1) * 512], start=(kc == 0), stop=(kc == 7))) for kc in range(8)])
                        csl = slice(cb * 512, (cb + 1) * 512)
                        k.do("dve", [("pout", cb), "gate1"], [("ost", j, cb)], lambda: dve.tensor_tensor(out=ost[j][:, csl], in0=pout[cb][:], in1=gate[1][:, csl], op=ALU.mult))
                        k.do("dve", [("ost", j, cb), ("xc", j)], [("ost", j, cb)], lambda: dve.scalar_tensor_tensor(out=ost[j][:, csl], in0=xc[j][:, csl], scalar=0.5, in1=ost[j][:, csl], op0=ALU.mult, op1=ALU.add))
                    tok = k.dma("sp", [("ost", j, cb) for cb in range(4)], [], p1_d[t * 128:(t + 1) * 128, :], ost[j][:], f"ost{j}")
                    k.stores.append(tok)
                    ctoks.append(tok)
                rs1_toks.append(collective(p1_d[g * 512:(g + 1) * 512, :], s1_d[g * 256:(g + 1) * 256, :], "rs1", "ReduceScatter", ctoks))
            k.barrier(exclude=("cc_rs1",))

    if upto >= 8:
        with ExitStack() as pf:
            xf = [sb(f"xf{i}", [128, D], F32, pf) for i in range(2)]
            junk = sb("junkf", [128, D], BF16, pf)
            ssq = sb("ssqf", [128, 1], F32, pf)
            rstd = sb("rstdf", [128, 1], F32, pf)
            fg = sb("fg", [128, D], F32, pf)
            k.dma("sp", [], ["fg"], fg[:], fng_d.partition_broadcast(128), "cb0")
            for t in range(NT // 2):
                j = t % 2
                k.wait("sp", rs1_toks[t // 2])
                k.dma("sp", [], [("xf", j)], xf[j][:], s1_d[t * 128:(t + 1) * 128, :], f"xf{j}")
                k.do("act", [("xf", j)], ["junkf", "ssqf"], lambda: act.activation(out=junk[:], in_=xf[j][:], func=AF.Square, accum_out=ssq[:]))
                k.do("act", ["ssqf"], ["rstdf"], lambda: act.activation(out=rstd[:], in_=ssq[:], func=AF.Sqrt, scale=float(1.0 / D), bias=float(EPS)))
                k.do("dve", ["rstdf"], ["rstdf"], lambda: dve.reciprocal(out=rstd[:], in_=rstd[:]))
                k.do("dve", [("xf", j), "rstdf", "fg"], [("xf", j)], lambda: dve.scalar_tensor_tensor(out=xf[j][:], in0=xf[j][:], scalar=rstd[:, 0:1], in1=fg[:], op0=ALU.mult, op1=ALU.mult))
                tok = k.dma("sp", [("xf", j)], [], out_d[t * 128:(t + 1) * 128, :], xf[j][:], f"xfo{j}")
                k.stores.append(tok)
            k.barrier()

    if dbg:
        for nm, src in ([("p0", p0_d), ("s0", s0_d)] if upto >= 4 else []) + ([("p1", p1_d), ("s1", s1_d)] if upto >= 7 else []):
            dd = nc.dram_tensor(nm, list(src.shape), F32, kind="ExternalOutput").ap()
            k.stores.append(k.dma("sp", [], [nm + "_dbg"], dd, src, "dbg_" + nm))

    k.wait("sp", *k.stores)
    for nm in ["rwin_b", "rwout_b", "mwin_b", "muqn_b", "muqr_b", "mukvk_b", "mukvv_b", "mwout_b"]:
        k.wait("pool", k.lw[nm])
    for e in ["pe", "act", "dve", "pool"]:
        if e in k.last_tok:
            k.wait("sp", k.last_tok[e])
    es.close()
    return nc


def host_inputs(S, x, c, positions, ada_w, ada_b, norm_g, ret_w_in, ret_gn_g, ret_w_out,
                mla_w_in, mla_q_norm_g, mla_w_uq, mla_kv_norm_g, mla_w_ukv, mla_w_out, final_norm_g):
    f32 = np.float32
    bf = ml_dtypes.bfloat16
    NT = S // 128
    ident = np.eye(128, dtype=f32).astype(bf)
    invf0 = (10000.0 ** (-np.arange(0, 256, 2, dtype=f32) / f32(256))).astype(f32)
    invf1 = (10000.0 ** (-np.arange(0, 64, 2, dtype=f32) / f32(64))).astype(f32)
    n = np.arange(128, dtype=np.float64)
    cm = (n[None, :] >= n[:, None]).astype(f32).astype(bf)
    am = np.zeros((128, 4, 512), f32)
    qq = np.arange(512)
    for r in range(4):
        am[:, r, :] = (128 * r + np.arange(128)[:, None] <= qq[None, :])
    am = am.astype(bf)
    maps = []
    for core in range(NCORES):
        b, hg = core // 2, core % 2
        hs = np.arange(hg * 4, hg * 4 + 4)
        lg = np.log1p(-np.exp2(-5.0 - hs.astype(np.float64)))
        dqv = np.exp(lg[None, :] * (n[:, None] + 1.0)).astype(f32)
        dkv = (np.exp(-lg[None, :] * (n[:, None] + 1.0)) / 16.0).astype(f32)
        w = ret_w_in[0]
        qc = np.concatenate([np.arange(h * 256, (h + 1) * 256) for h in hs])
        vc = np.concatenate([np.arange(h * 512, (h + 1) * 512) for h in hs])
        rw = np.concatenate([w[:, qc], w[:, 2048 + qc], w[:, 4096 + vc], w[:, 8192 + vc]], axis=1)
        mh = np.arange(hg * 8, hg * 8 + 8)
        mw = mla_w_in[0]
        gcol = 1088 + np.concatenate([np.arange(h * 128, (h + 1) * 128) for h in mh])
        mwin = np.concatenate([mw[:, :1088], mw[:, gcol]], axis=1)
        uq = mla_w_uq[0].reshape(512, 16, 192)[:, mh, :]
        ukv = mla_w_ukv[0].reshape(512, 16, 256)[:, mh, :]
        m = {
            "x": np.ascontiguousarray(x[b]),
            "c_l": np.ascontiguousarray(c[b].reshape(16, 128).T),
            "pos_l": np.ascontiguousarray(positions[b].reshape(NT, 128).T.astype(np.int32)),
            "ada_w": np.ascontiguousarray(ada_w[:, :, hg * 3072:(hg + 1) * 3072]), "ada_b": ada_b, "norm_g": norm_g,
            "ret_w_in": np.ascontiguousarray(rw),
            "ret_gn_g": np.ascontiguousarray(ret_gn_g[0][vc]),
            "ret_w_out": np.ascontiguousarray(ret_w_out[0][vc, :]),
            "mla_w_in": np.ascontiguousarray(mwin),
            "mla_q_norm_g": mla_q_norm_g[0],
            "mla_w_uq_nope": np.ascontiguousarray(uq[:, :, :128].reshape(512, 1024)),
            "mla_w_uq_rope": np.ascontiguousarray(uq[:, :, 128:].reshape(512, 512)),
            "mla_kv_norm_g": mla_kv_norm_g[0],
            "mla_w_ukv_k": np.ascontiguousarray(ukv[:, :, :128].reshape(512, 1024)),
            "mla_w_ukv_v": np.ascontiguousarray(ukv[:, :, 128:].reshape(512, 1024)),
            "mla_w_out": np.ascontiguousarray(mla_w_out[0][np.concatenate([np.arange(h * 128, (h + 1) * 128) for h in mh]), :]),
            "final_norm_g": final_norm_g,
            "ident": ident, "invf0": invf0, "invf1": invf1, "dq": dqv, "dk": dkv, "g128": np.ascontiguousarray(np.broadcast_to(np.exp(lg * 128.0).astype(f32)[None, :], (128, 4))),
            "cmask": cm, "amask": am,
        }
        maps.append(m)
    return maps


_CACHE = {}


def kernel(**inputs):
    inputs = {kk: np.asarray(v) for kk, v in inputs.items()}
    B, S, _ = inputs["x"].shape
    if S not in _CACHE:
        _CACHE[S] = build_program(S)
    nc = _CACHE[S]
    maps = host_inputs(S, **inputs)
    res = run_bass_kernel_spmd(nc, maps, core_ids=list(range(NCORES)))
    out = np.empty((B, S, D), np.float32)
    for core in range(NCORES):
        b, hg = core // 2, core % 2
        out[b].reshape(S // 512, 2, 256, D)[:, hg] = res.results[core]["out"].reshape(S // 512, 256, D)
    return out
```
